# Optimizing a Trainium2 kernel written in Bass

```python
import math
import jax, jax.numpy as jnp
from jax import lax
import numpy as np

D_MODEL = 2048
BATCH = 8
SEQ = 2048
DEPTH = 2
DEC_BATCH = 128
DEC_SEQ = 1
PAST_LEN = 2048
PAGE_SIZE = 128

N_EVEN = (DEPTH + 1) // 2
N_ODD = DEPTH // 2
ATTN_WIDTH = D_MODEL // 2
HEAD_DIM = 128
N_HEADS = ATTN_WIDTH // HEAD_DIM
ATTN_SCALE = HEAD_DIM ** -0.5
MOBA_BLOCK = 256
MOBA_TOPK = 3
ROPE_THETA = 10000.0
Q_CHUNK = 32
S5_WIDTH = D_MODEL - ATTN_WIDTH
S5_GROUP = 16
S5_GROUPS = S5_WIDTH // S5_GROUP
S5_STATE = 64
S5_DT_MIN = 0.001
S5_DT_MAX = 0.1
EVEN_IN = 4 * ATTN_WIDTH + 2 * S5_WIDTH
SSD_WIDTH = 2 * D_MODEL
SSD_HEAD_DIM = 64
SSD_HEADS = SSD_WIDTH // SSD_HEAD_DIM
SSD_GROUPS = 8
SSD_STATE = 128
SSD_CONV = 4
SSD_CHUNK = 128
SSD_CONV_DIM = SSD_WIDTH + 2 * SSD_GROUPS * SSD_STATE
ODD_IN = SSD_WIDTH + SSD_CONV_DIM + SSD_HEADS
EPS = 1e-6

kernel_name = "moba_s5_mamba2_hybrid_step"


def rms_norm(x, w):
    xf = x.astype(jnp.float32)
    xf = xf * lax.rsqrt(jnp.mean(xf * xf, axis=-1, keepdims=True) + EPS)
    return (xf * w.astype(jnp.float32)).astype(x.dtype)


def rope(x, pos):
    half = HEAD_DIM // 2
    inv = ROPE_THETA ** (-jnp.arange(half, dtype=jnp.float32) / half)
    ang = pos.astype(jnp.float32)[:, None] * inv[None, :]
    cos = jnp.cos(ang)[:, None, :]
    sin = jnp.sin(ang)[:, None, :]
    xf = x.astype(jnp.float32)
    x1, x2 = xf[..., :half], xf[..., half:]
    return jnp.concatenate([x1 * cos - x2 * sin, x2 * cos + x1 * sin], axis=-1).astype(x.dtype)


def even_in_proj(h, pos, w_in, q_norm_w, k_norm_w):
    n, t, _ = h.shape
    proj = h @ w_in
    q, k, v, gate_a, u, gate_b = jnp.split(
        proj, [ATTN_WIDTH, 2 * ATTN_WIDTH, 3 * ATTN_WIDTH, 4 * ATTN_WIDTH, 4 * ATTN_WIDTH + S5_WIDTH], axis=-1)
    shp = (n, t, N_HEADS, HEAD_DIM)
    q = rope(rms_norm(q.reshape(shp), q_norm_w), pos)
    k = rope(rms_norm(k.reshape(shp), k_norm_w), pos)
    return q, k, v.reshape(shp), gate_a, u, gate_b


def moba_n_blocks(length):
    return max(-(-length // MOBA_BLOCK), MOBA_TOPK)


def to_blocks(k, n_blocks):
    length = k.shape[-3]
    pad = [(0, 0)] * k.ndim
    pad[-3] = (0, n_blocks * MOBA_BLOCK - length)
    k = jnp.pad(k, pad)
    return k.reshape(k.shape[:-3] + (n_blocks, MOBA_BLOCK) + k.shape[-2:])


def moba_query(q, q_pos, kb, vb, k_mean):
    f32 = jnp.float32
    nb = kb.shape[0]
    own = q_pos // MOBA_BLOCK
    s_blk = jnp.einsum('thd,nhd->thn', q.astype(f32), k_mean)
    fully_past = jnp.arange(nb, dtype=jnp.int32)[None, None, :] < own[:, None, None]
    s_blk = jnp.where(fully_past, s_blk, -jnp.inf)
    _, top_i = lax.top_k(s_blk, MOBA_TOPK)
    sel_ok = top_i < own[:, None, None]
    idx = jnp.concatenate([top_i, jnp.broadcast_to(own[:, None, None], top_i.shape[:2] + (1,))], axis=-1)
    ok = jnp.concatenate([sel_ok, jnp.ones(top_i.shape[:2] + (1,), dtype=bool)], axis=-1)
    heads = jnp.arange(N_HEADS)[None, :, None]
    kg = kb.transpose(2, 0, 1, 3)[heads, idx]
    vg = vb.transpose(2, 0, 1, 3)[heads, idx]
    key_pos = idx[..., None] * MOBA_BLOCK + jnp.arange(MOBA_BLOCK, dtype=jnp.int32)
    mask = ok[..., None] & (key_pos <= q_pos[:, None, None, None])
    s = jnp.einsum('thd,thkjd->thkj', q, kg, preferred_element_type=f32) * ATTN_SCALE
    p = jax.nn.softmax(jnp.where(mask, s, -jnp.inf), axis=(-2, -1))
    out = jnp.einsum('thkj,thkjd->thd', p, vg.astype(f32))
    return out.astype(q.dtype)


def moba_prompt(q, k, v):
    n, t = q.shape[:2]
    nb = moba_n_blocks(t)
    kb = to_blocks(k, nb)
    vb = to_blocks(v, nb)
    k_mean = jnp.mean(kb.astype(jnp.float32), axis=2)
    n_chunks = t // Q_CHUNK
    qc = q.reshape(n * n_chunks, Q_CHUNK, N_HEADS, HEAD_DIM)
    cid = jnp.arange(n * n_chunks, dtype=jnp.int32)

    def body(args):
        qi, ci = args
        b = ci // n_chunks
        pos = (ci % n_chunks) * Q_CHUNK + jnp.arange(Q_CHUNK, dtype=jnp.int32)
        return moba_query(qi, pos, kb[b], vb[b], k_mean[b])

    out = lax.map(body, (qc, cid))
    return out.reshape(n, t, ATTN_WIDTH)


def moba_sample(q, k, v, cache_k, cache_v, page_table):
    n, t = q.shape[:2]
    n_past = page_table.shape[1] * cache_k.shape[1]
    nb = moba_n_blocks(n_past + t)
    pos = n_past + jnp.arange(t, dtype=jnp.int32)

    def body(args):
        qi, ki, vi, pt = args
        kp = cache_k[pt].reshape(n_past, N_HEADS, HEAD_DIM)
        vp = cache_v[pt].reshape(n_past, N_HEADS, HEAD_DIM)
        kb = to_blocks(jnp.concatenate([kp, ki.astype(kp.dtype)], axis=0), nb)
        vb = to_blocks(jnp.concatenate([vp, vi.astype(vp.dtype)], axis=0), nb)
        k_mean = jnp.mean(kb.astype(jnp.float32), axis=1)
        return moba_query(qi, pos, kb, vb, k_mean)

    out = lax.map(body, (q, k, v, page_table))
    return out.reshape(n, t, ATTN_WIDTH)


def complex_affine(e1, e2):
    a1r, a1i, b1r, b1i = e1
    a2r, a2i, b2r, b2i = e2
    return (a2r * a1r - a2i * a1i, a2r * a1i + a2i * a1r,
            a2r * b1r - a2i * b1i + b2r, a2r * b1i + a2i * b1r + b2i)


def s5_branch(u, x0, lam_re, lam_im, log_dt, b_ri, c_ri, d_skip, glu_w, glu_b):
    f32 = jnp.float32
    n, t, _ = u.shape
    uf = u.astype(f32)
    ug = uf.reshape(n, t, S5_GROUPS, S5_GROUP)
    dt = jnp.exp(log_dt.astype(f32))[:, None]
    lr, li = lam_re.astype(f32), lam_im.astype(f32)
    mag = jnp.exp(lr * dt)
    ar, ai = mag * jnp.cos(li * dt), mag * jnp.sin(li * dt)
    d2 = lr * lr + li * li
    cr = ((ar - 1.0) * lr + ai * li) / d2
    ci = (ai * lr - (ar - 1.0) * li) / d2
    br, bi = b_ri[..., 0].astype(f32), b_ri[..., 1].astype(f32)
    bbr = cr[..., None] * br - ci[..., None] * bi
    bbi = cr[..., None] * bi + ci[..., None] * br
    bur = jnp.einsum('ntgc,gpc->ntgp', ug, bbr)
    bui = jnp.einsum('ntgc,gpc->ntgp', ug, bbi)
    a_r = jnp.broadcast_to(ar, bur.shape)
    a_i = jnp.broadcast_to(ai, bur.shape)
    acr, aci, bcr, bci = lax.associative_scan(complex_affine, (a_r, a_i, bur, bui), axis=1)
    x0r = x0[..., 0].astype(f32)[:, None]
    x0i = x0[..., 1].astype(f32)[:, None]
    xr = acr * x0r - aci * x0i + bcr
    xi = acr * x0i + aci * x0r + bci
    c_re, c_im = c_ri[..., 0].astype(f32), c_ri[..., 1].astype(f32)
    y = jnp.einsum('ntgp,gcp->ntgc', xr, c_re) - jnp.einsum('ntgp,gcp->ntgc', xi, c_im)
    y = y.reshape(n, t, S5_WIDTH) + d_skip.astype(f32) * uf
    z = jax.nn.gelu(y)
    out = z * jax.nn.sigmoid(z @ glu_w.astype(f32) + glu_b.astype(f32))
    new_state = jnp.stack([xr[:, -1], xi[:, -1]], axis=-1)
    return out.astype(u.dtype), new_state.astype(x0.dtype)


def segsum(a):
    t = a.shape[-1]
    x = jnp.broadcast_to(a[..., None], a.shape + (t,))
    x = jnp.where(jnp.tril(jnp.ones((t, t), dtype=bool), -1), x, 0.0)
    cs = jnp.cumsum(x, axis=-2)
    return jnp.where(jnp.tril(jnp.ones((t, t), dtype=bool)), cs, -jnp.inf)


def ssd_chunked(xdt, a_dt, bm, cm, h0):
    n, t, nh, hp = xdt.shape
    q = min(SSD_CHUNK, t)
    pad = (-t) % q
    if pad:
        xdt = jnp.pad(xdt, ((0, 0), (0, pad), (0, 0), (0, 0)))
        a_dt = jnp.pad(a_dt, ((0, 0), (0, pad), (0, 0)))
        bm = jnp.pad(bm, ((0, 0), (0, pad), (0, 0), (0, 0)))
        cm = jnp.pad(cm, ((0, 0), (0, pad), (0, 0), (0, 0)))
    nc = (t + pad) // q
    r = nh // SSD_GROUPS
    x = xdt.reshape(n, nc, q, SSD_GROUPS, r, hp)
    a = a_dt.reshape(n, nc, q, SSD_GROUPS, r).transpose(0, 3, 4, 1, 2)
    b_ = bm.reshape(n, nc, q, SSD_GROUPS, SSD_STATE)
    c_ = cm.reshape(n, nc, q, SSD_GROUPS, SSD_STATE)
    a_cs = jnp.cumsum(a, axis=-1)
    l_mat = jnp.exp(segsum(a))
    y_diag = jnp.einsum('bclgn,bcsgn,bgrcls,bcsgrp->bclgrp', c_, b_, l_mat, x)
    decay_states = jnp.exp(a_cs[..., -1:] - a_cs)
    states = jnp.einsum('bclgn,bgrcl,bclgrp->bcgrpn', b_, decay_states, x)
    states = jnp.concatenate([h0.reshape(n, 1, SSD_GROUPS, r, hp, SSD_STATE), states], axis=1)
    a_chunk = jnp.pad(a_cs[..., -1], ((0, 0), (0, 0), (0, 0), (1, 0)))
    decay_chunk = jnp.exp(segsum(a_chunk))
    new_states = jnp.einsum('bgrzc,bcgrpn->bzgrpn', decay_chunk, states)
    states, final = new_states[:, :-1], new_states[:, -1]
    y_off = jnp.einsum('bclgn,bcgrpn,bgrcl->bclgrp', c_, states, jnp.exp(a_cs))
    y = (y_diag + y_off).reshape(n, nc * q, nh, hp)[:, :t]
    return y, final.reshape(n, nh, hp, SSD_STATE)


def odd_mix(h, conv_buf, h0, w_in, conv_w, conv_b, dt_bias, a_log, d_skip, norm_w, w_out):
    f32 = jnp.float32
    n, t, _ = h.shape
    proj = h @ w_in
    z, xbc, dt_raw = jnp.split(proj, [SSD_WIDTH, SSD_WIDTH + SSD_CONV_DIM], axis=-1)
    xfull = jnp.concatenate([conv_buf.astype(xbc.dtype), xbc], axis=1)
    conv = lax.conv_general_dilated(
        xfull, conv_w[:, None, :].astype(xfull.dtype), window_strides=(1,), padding='VALID',
        dimension_numbers=('NWC', 'WIO', 'NWC'), feature_group_count=SSD_CONV_DIM) + conv_b
    new_buf = xfull[:, -(SSD_CONV - 1):]
    xbc = jax.nn.silu(conv.astype(f32))
    xs, bm, cm = jnp.split(xbc, [SSD_WIDTH, SSD_WIDTH + SSD_GROUPS * SSD_STATE], axis=-1)
    xs = xs.reshape(n, t, SSD_HEADS, SSD_HEAD_DIM)
    bm = bm.reshape(n, t, SSD_GROUPS, SSD_STATE)
    cm = cm.reshape(n, t, SSD_GROUPS, SSD_STATE)
    dt = jax.nn.softplus(dt_raw.astype(f32) + dt_bias.astype(f32))
    a = -jnp.exp(a_log.astype(f32))
    y, h_new = ssd_chunked(xs * dt[..., None], dt * a, bm, cm, h0.astype(f32))
    y = y + d_skip.astype(f32)[:, None] * xs
    y = y.reshape(n, t, SSD_WIDTH) * jax.nn.silu(z.astype(f32))
    yg = y.reshape(n, t, SSD_GROUPS, SSD_WIDTH // SSD_GROUPS)
    yg = yg * lax.rsqrt(jnp.mean(yg * yg, axis=-1, keepdims=True) + EPS)
    y = yg.reshape(n, t, SSD_WIDTH) * norm_w.astype(f32)
    out = y.astype(h.dtype) @ w_out
    return out, new_buf, h_new.astype(h0.dtype)


def setup_inputs(seed: int = 0) -> dict:
    key = jax.random.key(seed)
    ks = jax.random.split(key, 32)
    f32 = jnp.float32
    n_pages = PAST_LEN // PAGE_SIZE
    n_used = DEC_BATCH * n_pages
    n_pool = n_used + n_used // 4

    def nrm(k, shape, scale):
        return jax.random.normal(k, shape, f32) * scale

    x_prompt = nrm(ks[0], (BATCH, SEQ, D_MODEL), 1.0)
    x_sample = nrm(ks[1], (DEC_BATCH, DEC_SEQ, D_MODEL), 1.0)
    cache_k = nrm(ks[2], (N_EVEN, n_pool, PAGE_SIZE, N_HEADS, HEAD_DIM), 1.0)
    cache_v = nrm(ks[3], (N_EVEN, n_pool, PAGE_SIZE, N_HEADS, HEAD_DIM), 1.0)
    page_table = jax.random.permutation(ks[4], n_pool)[:n_used].reshape(DEC_BATCH, n_pages).astype(jnp.int32)
    state_s5 = nrm(ks[5], (N_EVEN, DEC_BATCH, S5_GROUPS, S5_STATE, 2), 0.1)
    state_conv = nrm(ks[6], (N_ODD, DEC_BATCH, SSD_CONV - 1, SSD_CONV_DIM), 1.0)
    state_ssd = nrm(ks[7], (N_ODD, DEC_BATCH, SSD_HEADS, SSD_HEAD_DIM, SSD_STATE), 0.1)
    norm_w = 1.0 + nrm(ks[8], (DEPTH, D_MODEL), 0.05)
    w_in_even = nrm(ks[9], (N_EVEN, D_MODEL, EVEN_IN), D_MODEL ** -0.5)
    q_norm_w = 1.0 + nrm(ks[10], (N_EVEN, HEAD_DIM), 0.05)
    k_norm_w = 1.0 + nrm(ks[11], (N_EVEN, HEAD_DIM), 0.05)
    s5_lambda_re = -0.5 + nrm(ks[12], (N_EVEN, S5_GROUPS, S5_STATE), 0.01)
    s5_lambda_im = jnp.pi * jnp.arange(S5_STATE, dtype=f32) + nrm(ks[13], (N_EVEN, S5_GROUPS, S5_STATE), 0.01)
    s5_log_dt = jax.random.uniform(ks[14], (N_EVEN, S5_GROUPS), f32, math.log(S5_DT_MIN), math.log(S5_DT_MAX))
    s5_b = nrm(ks[15], (N_EVEN, S5_GROUPS, S5_STATE, S5_GROUP, 2), S5_GROUP ** -0.5)
    s5_c = nrm(ks[16], (N_EVEN, S5_GROUPS, S5_GROUP, S5_STATE, 2), 0.5)
    s5_d = nrm(ks[17], (N_EVEN, S5_WIDTH), 1.0)
    s5_glu_w = nrm(ks[18], (N_EVEN, S5_WIDTH, S5_WIDTH), S5_WIDTH ** -0.5)
    s5_glu_b = nrm(ks[19], (N_EVEN, S5_WIDTH), 0.02)
    w_out_even = nrm(ks[20], (N_EVEN, ATTN_WIDTH + S5_WIDTH, D_MODEL), (ATTN_WIDTH + S5_WIDTH) ** -0.5)
    w_in_odd = nrm(ks[21], (N_ODD, D_MODEL, ODD_IN), D_MODEL ** -0.5)
    conv_w = nrm(ks[22], (N_ODD, SSD_CONV, SSD_CONV_DIM), SSD_CONV ** -0.5)
    conv_b = nrm(ks[23], (N_ODD, SSD_CONV_DIM), 0.02)
    dt0 = jnp.exp(jax.random.uniform(ks[24], (N_ODD, SSD_HEADS), f32, math.log(0.001), math.log(0.1)))
    ssd_dt_bias = dt0 + jnp.log(-jnp.expm1(-dt0))
    ssd_a_log = jnp.log(jax.random.uniform(ks[25], (N_ODD, SSD_HEADS), f32, 1.0, 16.0))
    ssd_d = 1.0 + nrm(ks[26], (N_ODD, SSD_HEADS), 0.1)
    ssd_norm_w = 1.0 + nrm(ks[27], (N_ODD, SSD_WIDTH), 0.05)
    w_out_odd = nrm(ks[28], (N_ODD, SSD_WIDTH, D_MODEL), SSD_WIDTH ** -0.5)
    return {"x_prompt": x_prompt, "x_sample": x_sample, "cache_k": cache_k, "cache_v": cache_v,
            "page_table": page_table, "state_s5": state_s5, "state_conv": state_conv, "state_ssd": state_ssd,
            "norm_w": norm_w, "w_in_even": w_in_even, "q_norm_w": q_norm_w, "k_norm_w": k_norm_w,
            "s5_lambda_re": s5_lambda_re, "s5_lambda_im": s5_lambda_im, "s5_log_dt": s5_log_dt,
            "s5_b": s5_b, "s5_c": s5_c, "s5_d": s5_d, "s5_glu_w": s5_glu_w, "s5_glu_b": s5_glu_b,
            "w_out_even": w_out_even, "w_in_odd": w_in_odd, "conv_w": conv_w, "conv_b": conv_b,
            "ssd_dt_bias": ssd_dt_bias, "ssd_a_log": ssd_a_log, "ssd_d": ssd_d, "ssd_norm_w": ssd_norm_w,
            "w_out_odd": w_out_odd}


def reference(x_prompt, x_sample, cache_k, cache_v, page_table, state_s5, state_conv, state_ssd,
              norm_w, w_in_even, q_norm_w, k_norm_w, s5_lambda_re, s5_lambda_im, s5_log_dt,
              s5_b, s5_c, s5_d, s5_glu_w, s5_glu_b, w_out_even, w_in_odd, conv_w, conv_b,
              ssd_dt_bias, ssd_a_log, ssd_d, ssd_norm_w, w_out_odd):
    n_past = page_table.shape[1] * cache_k.shape[2]
    pos_p = jnp.arange(x_prompt.shape[1], dtype=jnp.int32)
    pos_s = n_past + jnp.arange(x_sample.shape[1], dtype=jnp.int32)
    yp, ys = x_prompt, x_sample
    k_p, v_p, k_s, v_s, s5_p, s5_s = [], [], [], [], [], []
    cv_p, cv_s, ssd_p, ssd_s = [], [], [], []
    for i in range(DEPTH):
        hp = rms_norm(yp, norm_w[i])
        hs = rms_norm(ys, norm_w[i])
        if i % 2 == 0:
            e = i // 2
            qp, kp, vp, gap, up, gbp = even_in_proj(hp, pos_p, w_in_even[e], q_norm_w[e], k_norm_w[e])
            qs, kss, vss, gas, us, gbs = even_in_proj(hs, pos_s, w_in_even[e], q_norm_w[e], k_norm_w[e])
            att_p = moba_prompt(qp, kp, vp)
            att_s = moba_sample(qs, kss, vss, cache_k[e], cache_v[e], page_table)
            so_p, st_p = s5_branch(up, jnp.zeros((x_prompt.shape[0], S5_GROUPS, S5_STATE, 2), state_s5.dtype),
                                   s5_lambda_re[e], s5_lambda_im[e], s5_log_dt[e], s5_b[e], s5_c[e],
                                   s5_d[e], s5_glu_w[e], s5_glu_b[e])
            so_s, st_s = s5_branch(us, state_s5[e], s5_lambda_re[e], s5_lambda_im[e], s5_log_dt[e],
                                   s5_b[e], s5_c[e], s5_d[e], s5_glu_w[e], s5_glu_b[e])
            yp = yp + jnp.concatenate([att_p * jax.nn.silu(gap), so_p * jax.nn.silu(gbp)], axis=-1) @ w_out_even[e]
            ys = ys + jnp.concatenate([att_s * jax.nn.silu(gas), so_s * jax.nn.silu(gbs)], axis=-1) @ w_out_even[e]
            k_p.append(kp); v_p.append(vp); k_s.append(kss); v_s.append(vss)
            s5_p.append(st_p); s5_s.append(st_s)
        else:
            o = i // 2
            out_p, buf_p, hn_p = odd_mix(
                hp, jnp.zeros((x_prompt.shape[0], SSD_CONV - 1, SSD_CONV_DIM), state_conv.dtype),
                jnp.zeros((x_prompt.shape[0], SSD_HEADS, SSD_HEAD_DIM, SSD_STATE), state_ssd.dtype),
                w_in_odd[o], conv_w[o], conv_b[o], ssd_dt_bias[o], ssd_a_log[o], ssd_d[o], ssd_norm_w[o], w_out_odd[o])
            out_s, buf_s, hn_s = odd_mix(
                hs, state_conv[o], state_ssd[o], w_in_odd[o], conv_w[o], conv_b[o], ssd_dt_bias[o],
                ssd_a_log[o], ssd_d[o], ssd_norm_w[o], w_out_odd[o])
            yp = yp + out_p
            ys = ys + out_s
            cv_p.append(buf_p); cv_s.append(buf_s); ssd_p.append(hn_p); ssd_s.append(hn_s)
    return (yp, ys, jnp.stack(k_p), jnp.stack(v_p), jnp.stack(k_s), jnp.stack(v_s),
            jnp.stack(s5_p), jnp.stack(s5_s), jnp.stack(cv_p), jnp.stack(cv_s),
            jnp.stack(ssd_p), jnp.stack(ssd_s))
```

```python
import numpy as np
from contextlib import ExitStack
import concourse.bass as bass
import concourse.mybir as mybir
from concourse.bass_utils import run_bass_kernel_spmd

F32 = mybir.dt.float32
BF16 = mybir.dt.bfloat16
I32 = mybir.dt.int32
AF = mybir.ActivationFunctionType
ALU = mybir.AluOpType
AX = mybir.AxisListType

NCORES = 8
T = 2048
NT = T // 128
NS = 16
TS = T + NS
D = 2048
KC = D // 128
EPS = 1e-6
NEG = -30000.0
ENGS = ("pe", "act", "dve", "pool", "sp")
EPOCH = 700
POOLC = "dve"


class Tok:
    __slots__ = ("w", "r")
    registry = []

    def __init__(self):
        self.w = None
        self.r = {}
        Tok.registry.append(self)


class DSem:
    __slots__ = ("h", "cnt", "name")

    def __init__(self, name):
        self.name = name
        self.h = None
        self.cnt = 0


class Op:
    __slots__ = ("eng", "fn", "waits", "sig", "signo", "dsem", "idx")


class Prog:
    def __init__(self, nc):
        self.nc = nc
        self.ops = {e: [] for e in ENGS}
        self.seen = {e: {} for e in ENGS}
        self.dsems = []
        Tok.registry = []
        self.t_phase = Tok()

    def barrier(self, eng, fn):
        toks = list(Tok.registry)
        return self.add(eng, fn, (), toks)

    def dsem(self, name):
        d = DSem(name)
        self.dsems.append(d)
        return d

    def add(self, eng, fn, reads=(), writes=(), dsem=None):
        op = Op()
        op.eng = eng
        op.fn = fn
        op.sig = False
        op.signo = 0
        op.dsem = dsem
        op.idx = len(self.ops[eng])
        deps = []
        if self.t_phase.w is not None:
            deps.append(self.t_phase.w)
        for t in reads:
            if t.w is not None:
                deps.append(t.w)
        for t in writes:
            if t.w is not None:
                deps.append(t.w)
            deps.extend(t.r.values())
        seen = self.seen[eng]
        waits = {}
        for d in deps:
            if d[0] == "e":
                o2 = d[1]
                if o2.eng == "pe" and eng == "pe":
                    continue
                key = ("e", o2.eng)
                val = o2.idx
            else:
                key = ("d", d[1])
                val = d[2]
            if seen.get(key, -1) >= val:
                continue
            if key not in waits or waits[key][0] < val:
                waits[key] = (val, d)
        op.waits = []
        for key, (val, d) in waits.items():
            seen[key] = val
            if d[0] == "e":
                d[1].sig = True
            op.waits.append(d)
        if dsem is not None:
            dsem.cnt += 16
            me = ("d", dsem, dsem.cnt)
            mkey = ("d", dsem)
        else:
            me = ("e", op)
            mkey = ("e", eng)
        for t in reads:
            t.r[mkey] = me
        for t in writes:
            t.w = me
            t.r = {}
        self.ops[eng].append(op)
        return op

    def emit(self, es):
        nc = self.nc
        sems = {}
        for e in ENGS:
            n = 0
            for op in self.ops[e]:
                if op.sig:
                    n += 1
                    op.signo = n
            print("[prog] %s: ops=%d signals=%d waits=%d" % (e, len(self.ops[e]), n, sum(len(o.waits) for o in self.ops[e])))
            nep = max(1, (n + EPOCH - 1) // EPOCH)
            sems[e] = [es.enter_context(nc.semaphore("s_%s_%d" % (e, i))) for i in range(nep)]
        print("[prog] dsems=%d engine_sems=%d maxdcnt=%d" % (len(self.dsems), sum(len(v) for v in sems.values()), max(d.cnt for d in self.dsems)))
        for d in self.dsems:
            d.h = es.enter_context(nc.semaphore("d_" + d.name))
        prog = self

        def run(ename, eng):
            for op in prog.ops[ename]:
                for d in op.waits:
                    if d[0] == "e":
                        s = d[1].signo
                        ep = (s - 1) // EPOCH
                        eng.wait_ge(sems[d[1].eng][ep], s - ep * EPOCH)
                    else:
                        eng.wait_ge(d[1].h, d[2])
                ins = op.fn(eng)
                if op.dsem is not None:
                    ins.then_inc(op.dsem.h, 16)
                elif op.sig:
                    ep = (op.signo - 1) // EPOCH
                    ins.then_inc(sems[ename][ep], 1)
            if ename == "sp":
                for d in prog.dsems:
                    if d.cnt:
                        eng.wait_ge(d.h, d.cnt)

        with nc.Block() as block:
            @block.tensor
            def _(eng):
                run("pe", eng)

            @block.scalar
            def _(eng):
                run("act", eng)

            @block.vector
            def _(eng):
                run("dve", eng)

            @block.gpsimd
            def _(eng):
                run("pool", eng)

            @block.sync
            def _(eng):
                run("sp", eng)


class K:
    def __init__(self, nc):
        self.nc = nc
        self.P = Prog(nc)
        self.ptr = self.SB_BASE
        self.nalloc = 0
        self.fdummy = self.sb("fdummy", [128, 8], F32)
        self.t_fd = Tok()

    SB_BASE = 16512
    SB_TOP = 229344

    def sb(self, name, shape, dt):
        esz = {F32: 4, BF16: 2, I32: 4}[dt]
        n = 1
        for d in shape[1:]:
            n *= d
        nbytes = (n * esz + 31) // 32 * 32
        off = self.ptr
        assert off + nbytes <= self.SB_TOP, "SBUF overflow at %s: %d" % (name, off + nbytes)
        self.ptr = off + nbytes
        self.nalloc += 1
        return self.nc.alloc_sbuf_tensor_at("%s_%d" % (name, self.nalloc), list(shape), dt, offset=off)

    def barrier(self, eng="dve"):
        d = self.fdummy
        return self.P.barrier(eng, lambda e: e.memset(d[0:1, 0:1], 0.0))

    def fence(self, old, new, eng="dve"):
        d = self.fdummy
        return self.P.add(eng, lambda e: e.memset(d[0:1, 0:1], 0.0), (), tuple(old) + tuple(new) + (self.t_fd,))

    def ps(self, name, shape, dt=F32):
        return self.nc.alloc_psum_tensor(name, list(shape), dt)

    def mm(self, out, lhsT, rhs, start, stop, r, w, tp=None):
        if tp is not None:
            return self.P.add("pe", lambda e: e.matmul(out, lhsT, rhs, start=start, stop=stop, tile_position=tp), r, w)
        return self.P.add("pe", lambda e: e.matmul(out, lhsT, rhs, start=start, stop=stop), r, w)

    def tr(self, out, in_, ident, r, w):
        return self.P.add("pe", lambda e: e.transpose(out, in_, ident), r, w)

    def act(self, out, in_, func, r, w, bias=None, scale=None, accum=None, eng="act"):
        kw = {}
        if bias is not None:
            kw["bias"] = bias
        if scale is not None:
            kw["scale"] = scale
        if accum is not None:
            kw["accum_out"] = accum
        return self.P.add(eng, lambda e: e.activation(out=out, in_=in_, func=func, **kw), r, w)

    def tt(self, out, in0, in1, op, r, w, eng="dve"):
        return self.P.add(eng, lambda e: e.tensor_tensor(out=out, in0=in0, in1=in1, op=op), r, w)

    def ts(self, out, in0, s1, s2, op0, op1, r, w, eng="dve", accum=None):
        kw = {}
        if accum is not None:
            kw["accum_out"] = accum
        if op1 is None:
            return self.P.add(eng, lambda e: e.tensor_scalar(out=out, in0=in0, scalar1=s1, scalar2=None, op0=op0, **kw), r, w)
        return self.P.add(eng, lambda e: e.tensor_scalar(out=out, in0=in0, scalar1=s1, scalar2=s2, op0=op0, op1=op1, **kw), r, w)

    def stt(self, out, in0, scalar, in1, op0, op1, r, w, eng="dve"):
        return self.P.add(eng, lambda e: e.scalar_tensor_tensor(out=out, in0=in0, scalar=scalar, in1=in1, op0=op0, op1=op1), r, w)

    def cp(self, out, in_, r, w, eng="dve"):
        if eng == "act":
            return self.P.add(eng, lambda e: e.copy(out=out, in_=in_), r, w)
        return self.P.add(eng, lambda e: e.tensor_copy(out=out, in_=in_), r, w)

    def red(self, out, in_, op, r, w, eng="dve", axis=AX.X):
        return self.P.add(eng, lambda e: e.tensor_reduce(out=out, in_=in_, axis=axis, op=op), r, w)

    def memset(self, ap, val, w, eng="dve"):
        return self.P.add(eng, lambda e: e.memset(ap, val), (), w)

    def dma(self, out, in_, dsem, r, w, eng="sp", **kw):
        return self.P.add(eng, lambda e: e.dma_start(out=out, in_=in_, **kw), r, w, dsem=dsem)


C_ID = 0
C_CM = C_ID + 128
C_ES = C_CM + 4 * 512
C_ONE = C_ES + 16 * 128
C_BT = C_ONE + 128
C_MM = C_BT + 2048
C_OWN = C_MM + 8 * 16
C_EV = C_OWN + 8 * 16
C_NV = C_EV + 17
C_U = C_NV + 128
C_PI = C_U + 128
C_DG = C_PI + 1
C_END = C_DG + 128


def host_consts():
    c = np.zeros((128, C_END), np.float32)
    c[:, C_ID:C_ID + 128] = np.eye(128, dtype=np.float32)
    key = np.arange(128)[:, None]
    q = np.arange(512)[None, :]
    for j in range(4):
        c[:, C_CM + j * 512:C_CM + (j + 1) * 512] = np.where(128 * j + key <= q, 0.0, NEG)
    for r in range(16):
        c[r, C_ES + r * 128:C_ES + (r + 1) * 128] = 1.0
    c[:, C_ONE:C_ONE + 128] = 1.0
    own = np.arange(T) // 256
    for j in range(2):
        for n in range(8):
            c[j * 8 + n, C_BT:C_BT + T] = np.where(n <= own, 0.0, NEG)
    for m in range(8):
        for j in range(2):
            for n in range(8):
                c[:, C_MM + m * 16 + j * 8 + n] = 0.0 if n < m else -1e30
                c[:, C_OWN + m * 16 + j * 8 + n] = 1.0 if n == m else 0.0
    c[:, C_EV:C_EV + 17] = np.arange(17, dtype=np.float32)[None, :]
    c[:, C_NV:C_NV + 128] = np.arange(128, dtype=np.float32)[None, :]
    c[:, C_U:C_U + 128] = (np.arange(128)[:, None] <= np.arange(128)[None, :]).astype(np.float32)
    c[:, C_PI] = np.arange(128, dtype=np.float32)
    for hh in range(8):
        for j in range(16):
            c[hh, C_DG + j * 8 + hh] = 1.0
    return c


def host_rope():
    half = 64
    inv = (10000.0 ** (-np.arange(half, dtype=np.float32) / half)).astype(np.float32)
    pos = np.arange(T + 1, dtype=np.float32)
    ang = pos[:, None] * inv[None, :]
    cos = np.cos(ang).astype(np.float32)
    sin = np.sin(ang).astype(np.float32)
    cc = np.concatenate([cos, cos], axis=1)
    ss = np.concatenate([-sin, sin], axis=1)
    return np.ascontiguousarray(np.stack([cc, ss], axis=1))


def build(dbg=None):
    nc = bass.Bass("TRN2", target_bir_lowering=False)
    k = K(nc)
    P = k.P
    dbg = dbg or {}

    def din(name, shape, dt=F32):
        return nc.dram_tensor(name, list(shape), dt, kind="ExternalInput").ap()

    def dout(name, shape, dt=F32):
        return nc.dram_tensor(name, list(shape), dt, kind="ExternalOutput").ap()

    xp = din("xp", [T, D])
    xs = din("xs", [NS, D])
    norm_w = din("norm_w", [2, D])
    w_in_even = din("w_in_even", [D, 6144])
    q_norm_w = din("q_norm_w", [1, 128])
    k_norm_w = din("k_norm_w", [1, 128])
    cst = din("cst", [128, C_END])
    rope = din("rope", [T + 1, 2, 128])

    k_p = dout("k_p", [T, 1024])
    v_p = dout("v_p", [T, 1024])
    k_s = dout("k_s", [NS, 1024])
    v_s = dout("v_s", [NS, 1024])

    ident = k.sb("ident", [128, 128], BF16)
    ones_bf = k.sb("ones_bf", [128, 128], BF16)
    t_const = Tok()
    ds_c = P.dsem("const")
    k.dma(ident[:], cst[:, C_ID:C_ID + 128], ds_c, (), (t_const,), eng="pool")
    k.dma(ones_bf[:], cst[:, C_ONE:C_ONE + 128], ds_c, (), (t_const,), eng="pool")

    NSLOT = 4
    wslot = [k.sb("wslot%d" % i, [128, KC * 256], BF16) for i in range(NSLOT)]
    catS = k.sb("catS", [128, 16, NS], BF16)
    ynSd = k.sb("ynSd", [128, 32, NS], BF16)
    t_ynSd = Tok()
    hT_base = k.ptr
    hT = k.sb("hT", [128, KC, TS], BF16)
    t_hT = Tok()
    region = k.ptr
    nw_bc = k.sb("nw_bc", [128, D], F32)
    t_nw = Tok()
    k.dma(nw_bc[:], norm_w[0:1, :].partition_broadcast(128), ds_c, (), (t_nw,))
    xt = [k.sb("xt%d" % i, [128, D], F32) for i in range(2)]
    t_xt = [Tok(), Tok()]
    ds_x = [P.dsem("x0"), P.dsem("x1")]
    hb = [k.sb("hb%d" % i, [128, D], BF16) for i in range(2)]
    t_hb = [Tok(), Tok()]
    junk = k.sb("junk", [128, D], BF16)
    t_junk = Tok()
    ss = k.sb("ss", [128, NT + 1], F32)
    rstd = k.sb("rstd", [128, NT + 1], F32)
    t_ss = [Tok() for _ in range(NT + 1)]
    pst = [k.ps("pst%d" % i, [128, 8, 128], BF16) for i in range(2)]
    t_pst = [Tok(), Tok()]

    def l0a(norm_row, srcp, srcs, rtoks=()):
        if norm_row:
            k.dma(nw_bc[:], norm_w[norm_row:norm_row + 1, :].partition_broadcast(128), ds_c, (), (t_nw,))
        for i in range(NT + 1):
            s = i % 2
            np_ = 128 if i < NT else NS
            src = srcp[i * 128:(i + 1) * 128, :] if i < NT else srcs
            k.dma(xt[s][:np_, :], src, ds_x[s], tuple(rtoks), (t_xt[s],))
            k.act(junk[:np_, :], xt[s][:np_, :], AF.Square, (t_xt[s],), (t_junk, t_ss[i]), accum=ss[:np_, i:i + 1])
            k.ts(rstd[:np_, i:i + 1], ss[:np_, i:i + 1], 1.0 / D, EPS, ALU.mult, ALU.add, (t_ss[i],), (t_ss[i],))
            k.act(rstd[:np_, i:i + 1], rstd[:np_, i:i + 1], AF.Ln, (t_ss[i],), (t_ss[i],))
            k.act(rstd[:np_, i:i + 1], rstd[:np_, i:i + 1], AF.Exp, (t_ss[i],), (t_ss[i],), scale=-0.5)
            k.stt(hb[s][:np_, :], xt[s][:np_, :], rstd[:np_, i:i + 1], nw_bc[:np_, :], ALU.mult, ALU.mult,
                  (t_xt[s], t_ss[i], t_nw), (t_hb[s],))
            for half in range(2):
                for j in range(8):
                    kc = half * 8 + j
                    k.tr(pst[half][:, j, :np_], hb[s][:np_, kc * 128:(kc + 1) * 128], ident[:np_, :np_],
                         (t_hb[s], t_const), (t_pst[half],))
                c0 = i * 128
                k.cp(hT[:, half * 8:(half + 1) * 8, c0:c0 + np_], pst[half][:, :, :np_], (t_pst[half],), (t_hT,),
                     eng="act" if half == 0 else "dve")

    l0a(0, xp, xs[:, :])


    t_wslot = [Tok() for _ in range(NSLOT)]
    ds_w = [P.dsem("w%d" % i) for i in range(NSLOT)]
    wctr = [0]

    def wload(src, kc, width):
        sl = wctr[0] % NSLOT
        wctr[0] += 1
        view = wslot[sl][:, 0:kc * width].rearrange("p (a b) -> p a b", a=kc)
        k.dma(view, src.rearrange("(a p) c -> p a c", p=128), ds_w[sl], (), (t_wslot[sl],), eng="pool")
        return view, t_wslot[sl]

    bank = [k.ps("bank%d" % i, [128, 512], F32) for i in range(6)]
    t_bank = [Tok() for _ in range(6)]

    catT = nc.dram_tensor("catT", [16, 128, T], BF16).ap()
    t_catT = Tok()
    t_catS = Tok()

    l0a_toks = [t_nw, t_junk] + t_xt + t_hb
    k.ptr = region
    qs_f = k.sb("qs_f", [NS, 8, 128], F32)
    ks_f = k.sb("ks_f", [NS, 8, 128], F32)
    vs_f = k.sb("vs_f", [NS, 8, 128], F32)
    t_qkvs = Tok()
    sgaS = k.sb("sgaS", [128, 8, NS], BF16)
    t_sgaS = Tok()
    ds_ks = P.dsem("ks_out")
    sattn_base = k.ptr
    qkw4 = k.sb("qkw4", [128, 4, 128], F32)
    cm = k.sb("cm", [128, 4, 512], BF16)
    esel = k.sb("esel", [16, 16, 128], BF16)
    qkw = k.sb("qkw", [128, 2, 128], F32)
    biasT0 = k.sb("biasT0", [16, T], BF16)
    mmask = k.sb("mmask", [128, 8, 16], F32)
    ownm = k.sb("ownm", [128, 8, 16], F32)
    t_acst = Tok()
    t_qkw = Tok()
    ropet = [k.sb("ropet%d" % i, [128, 2, 128], F32) for i in range(2)]
    t_ropet = [Tok(), Tok()]
    ds_rope = [P.dsem("rope0"), P.dsem("rope1")]
    sq = k.sb("sq", [128, 512], F32)
    t_sq = Tok()
    ss4 = k.sb("ss4", [128, 4], F32)
    t_ss4 = Tok()
    tn = k.sb("tn", [128, 4, 128], F32)
    t_tn = Tok()
    ut = k.sb("ut", [128, 4, 128], F32)
    t_ut = Tok()
    vt = k.sb("vt", [128, 4, 128], F32)
    t_vt = Tok()
    qkb = [k.sb("qkb%d" % i, [128, 4, 128], BF16) for i in range(2)]
    t_qkb = [Tok(), Tok()]
    kf = [k.sb("kf%d" % i, [128, 256], F32) for i in range(2)]
    t_kf = [Tok(), Tok()]
    ds_kf = [P.dsem("kf0"), P.dsem("kf1")]
    vf = [k.sb("vf%d" % i, [128, 256], F32) for i in range(2)]
    t_vf = [Tok(), Tok()]
    ds_vf = [P.dsem("vf0"), P.dsem("vf1")]
    qkT = k.sb("qkT", [128, 4, T], BF16)
    t_qkT = Tok()
    Vg = k.sb("Vg", [128, NT, 256], BF16)
    t_Vg = Tok()
    sga = k.sb("sga", [128, 2, T], BF16)
    t_sga = Tok()

    l0b_toks = [t_sq, t_ss4, t_tn, t_ut, t_vt, t_qkT, t_Vg, t_sga, t_qkvs, t_sgaS] + t_ropet + t_qkb + t_kf + t_vf
    t_qkw4 = Tok()
    l0b_toks.append(t_qkw4)

    def l0b_proj(g, wq, wk, wv, tq, tk, tv):
        for i in range(NT + 1):
            s = i % 2
            np_ = 128 if i < NT else NS
            c0 = i * 128
            bq = bank[s]
            bv = bank[2]
            if i < NT:
                k.dma(ropet[s][:, :, :], rope[c0:c0 + 128, :, :], ds_rope[s], (), (t_ropet[s],))
            else:
                k.dma(ropet[s][:NS, :, :], rope[T:T + 1, :, :].partition_broadcast(NS), ds_rope[s], (), (t_ropet[s],))
            for kc in range(KC):
                k.mm(bq[:np_, 0:256], hT[:, kc, c0:c0 + np_], wq[:, kc, :], kc == 0, kc == KC - 1, (t_hT, tq), (t_bank[s],))
            for kc in range(KC):
                k.mm(bq[:np_, 256:512], hT[:, kc, c0:c0 + np_], wk[:, kc, :], kc == 0, kc == KC - 1, (t_hT, tk), (t_bank[s],))
            for kc in range(KC):
                k.mm(bv[:np_, s * 256:(s + 1) * 256], hT[:, kc, c0:c0 + np_], wv[:, kc, :], kc == 0, kc == KC - 1, (t_hT, tv), (t_bank[2],))
            k.act(sq[:np_, :], bq[:np_, :], AF.Square, (t_bank[s],), (t_sq,))
            k.red(ss4[:np_, :], sq[:np_, :].rearrange("p (a b) -> p a b", a=4), ALU.add, (t_sq,), (t_ss4,))
            k.act(ss4[:np_, :], ss4[:np_, :], AF.Ln, (t_ss4,), (t_ss4,), bias=128.0 * EPS)
            k.act(ss4[:np_, :], ss4[:np_, :], AF.Exp, (t_ss4,), (t_ss4,), scale=-0.5)
            k.tt(tn[:np_], bq[:np_, :].rearrange("p (a b) -> p a b", a=4),
                 ss4[:np_, :].unsqueeze(2).broadcast_to([np_, 4, 128]), ALU.mult, (t_bank[s], t_ss4), (t_tn,))
            k.tt(tn[:np_], tn[:np_], qkw4[:np_], ALU.mult, (t_tn, t_qkw4), (t_tn,))
            cc = ropet[s][:np_, 0:1, :].broadcast_to([np_, 4, 128])
            k.tt(ut[:np_], tn[:np_], cc, ALU.mult, (t_tn, t_ropet[s]), (t_ut,))
            k.tt(vt[:np_, :, 0:64], tn[:np_, :, 64:128], ropet[s][:np_, 1:2, 0:64].broadcast_to([np_, 4, 64]), ALU.mult,
                 (t_tn, t_ropet[s]), (t_vt,))
            k.tt(vt[:np_, :, 64:128], tn[:np_, :, 0:64], ropet[s][:np_, 1:2, 64:128].broadcast_to([np_, 4, 64]), ALU.mult,
                 (t_tn, t_ropet[s]), (t_vt,))
            if i < NT:
                k.tt(qkb[s][:, 0:2, :], ut[:, 0:2, :], vt[:, 0:2, :], ALU.add, (t_ut, t_vt), (t_qkb[s],))
                kfv = kf[s][:, :].rearrange("p (a b) -> p a b", a=2)
                k.tt(kfv, ut[:, 2:4, :], vt[:, 2:4, :], ALU.add, (t_ut, t_vt), (t_kf[s],))
                k.cp(qkb[s][:, 2:4, :], kfv, (t_kf[s],), (t_qkb[s],), eng="act")
                k.dma(k_p[c0:c0 + 128, g * 256:(g + 1) * 256], kf[s][:, :], ds_kf[s], (t_kf[s],), ())
                k.cp(vf[s][:, :], bv[:, s * 256:(s + 1) * 256], (t_bank[2],), (t_vf[s],), eng="act")
                k.dma(v_p[c0:c0 + 128, g * 256:(g + 1) * 256], vf[s][:, :], ds_vf[s], (t_vf[s],), ())
                k.cp(Vg[:, i, :], vf[s][:, :], (t_vf[s],), (t_Vg,), eng="act")
                for j in range(4):
                    k.tr(pst[s][:, j, :], qkb[s][:, j, :], ident[:, :], (t_qkb[s], t_const), (t_pst[s],))
                k.cp(qkT[:, :, c0:c0 + 128], pst[s][:, 0:4, :], (t_pst[s],), (t_qkT,), eng="act")
            else:
                k.tt(qs_f[:, 2 * g:2 * g + 2, :], ut[:NS, 0:2, :], vt[:NS, 0:2, :], ALU.add, (t_ut, t_vt), (t_qkvs,))
                k.tt(ks_f[:, 2 * g:2 * g + 2, :], ut[:NS, 2:4, :], vt[:NS, 2:4, :], ALU.add, (t_ut, t_vt), (t_qkvs,))
                k.cp(vs_f[:, 2 * g:2 * g + 2, :], bv[:NS, s * 256:(s + 1) * 256].rearrange("p (a b) -> p a b", a=2),
                     (t_bank[2],), (t_qkvs,), eng="act")

    def l0b_gate(g, wga, tga):
        for j in range(2):
            for b in range(5):
                c0 = b * 512
                n = 512 if b < 4 else NS
                bk = bank[3 + (b % 2)]
                tb = t_bank[3 + (b % 2)]
                for kc in range(KC):
                    k.mm(bk[:, 0:n], wga[:, kc, j * 128:(j + 1) * 128], hT[:, kc, c0:c0 + n], kc == 0, kc == KC - 1, (t_hT, tga), (tb,))
                if b < 4:
                    k.act(sga[:, j, c0:c0 + n], bk[:, 0:n], AF.Silu, (tb,), (t_sga,))
                else:
                    k.act(sgaS[:, 2 * g + j, :], bk[:, 0:n], AF.Silu, (tb,), (t_sgaS,))

    biasT = k.sb("biasT", [16, T], BF16)
    t_biasT = Tok()
    km = k.sb("km", [128, 2, 8], F32)
    kmf = k.sb("kmf", [128, 2, 8], F32)
    kmh = k.sb("kmh", [128, 2, 8], BF16)
    kml = k.sb("kml", [128, 2, 8], BF16)
    t_km = Tok()
    sbk = k.sb("sbk", [128, 16], F32)
    mx8 = k.sb("mx8", [128, 2, 8], F32)
    selt = k.sb("selt", [128, 16], F32)
    gbias = k.sb("gbias", [128, 16], BF16)
    t_gate = Tok()
    pT = [k.sb("pT%d" % i, [128, 512], BF16) for i in range(2)]
    t_pT = [Tok(), Tok()]
    rden = k.sb("rden", [128, 512], F32)
    attf = k.sb("attf", [128, 512], F32)
    t_att = Tok()
    catst = [k.sb("catst%d" % i, [128, 512], BF16) for i in range(2)]
    t_catst = [Tok(), Tok()]
    ds_cat = [P.dsem("cat0"), P.dsem("cat1")]
    l0b_toks += [t_biasT, t_km, t_gate, t_att] + t_pT + t_catst
    actr = [0]

    def l0b_gates(g):
        k.red(km[:], qkT[:, 2:4, :].rearrange("p j (n c) -> p j n c", n=8), ALU.add, (t_qkT,), (t_km,))
        k.cp(kmh[:], km[:], (t_km,), (t_km,))
        k.cp(kmf[:], kmh[:], (t_km,), (t_km,))
        k.tt(kmf[:], km[:], kmf[:], ALU.subtract, (t_km,), (t_km,))
        k.cp(kml[:], kmf[:], (t_km,), (t_km,))
        k.cp(biasT[:, :], biasT0[:, :], (t_const, t_acst,), (t_biasT,))
        for i in range(8, NT):
            m = i // 2
            c0 = i * 128
            sbp = bank[5]
            for j in range(2):
                k.mm(sbp[:, j * 8:(j + 1) * 8], qkT[:, j, c0:c0 + 128], kmh[:, j, :], True, False, (t_qkT, t_km), (t_bank[5],))
                k.mm(sbp[:, j * 8:(j + 1) * 8], qkT[:, j, c0:c0 + 128], kml[:, j, :], False, True, (t_qkT, t_km), (t_bank[5],))
            k.tt(sbk[:, :], sbp[:, 0:16], mmask[:, m, :], ALU.add, (t_bank[5], t_const, t_acst), (t_gate,))
            for j in range(2):
                P.add("dve", (lambda jj: (lambda e: e.max(out=mx8[:, jj, :], in_=sbk[:, jj * 8:(jj + 1) * 8])))(j), (t_gate,), (t_gate,))
            k.tt(selt[:, :].rearrange("p (a b) -> p a b", a=2), sbk[:, :].rearrange("p (a b) -> p a b", a=2),
                 mx8[:, :, 2:3].broadcast_to([128, 2, 8]), ALU.is_ge, (t_gate,), (t_gate,))
            k.tt(selt[:, :], selt[:, :], ownm[:, m, :], ALU.add, (t_gate, t_const, t_acst), (t_gate,))
            k.ts(gbias[:, :], selt[:, :], -1.0, -NEG, ALU.add, ALU.mult, (t_gate,), (t_gate,))
            s = i % 2
            k.tr(pst[s][:16, 0, :], gbias[:, :], ident[:, :], (t_gate, t_const, t_acst), (t_pst[s],))
            k.cp(biasT[:, c0:c0 + 128], pst[s][:16, 0, :], (t_pst[s],), (t_biasT,), eng="act")

    def l0b_attn(g):
        for j in range(2):
            for c in range(4):
                q0 = c * 512
                ai = actr[0] % 2
                actr[0] += 1
                A = bank[3 + ai]
                tA = t_bank[3 + ai]
                Dn = bank[2 if ai == 0 else 5]
                tD = t_bank[2 if ai == 0 else 5]
                nk = 4 * c + 4
                for kt in range(nk):
                    s = kt % 2
                    st = bank[s]
                    diag = kt >= 4 * c
                    k.mm(st[:, :], qkT[:, 2 + j, kt * 128:(kt + 1) * 128], qkT[:, j, q0:q0 + 512], True, False, (t_qkT,), (t_bank[s],))
                    k.mm(st[:, :], esel[:, j * 8 + kt // 2, :], biasT[:, q0:q0 + 512], False, not diag, (t_const, t_acst, t_biasT), (t_bank[s],))
                    if diag:
                        k.mm(st[:, :], ident[:, :], cm[:, kt - 4 * c, :], False, True, (t_const, t_acst,), (t_bank[s],))
                    k.act(pT[s][:, :], st[:, :], AF.Exp, (t_bank[s],), (t_pT[s],))
                    k.mm(A[:, :], Vg[:, kt, j * 128:(j + 1) * 128], pT[s][:, :], kt == 0, kt == nk - 1, (t_Vg, t_pT[s]), (tA,))
                    k.mm(Dn[:, :], ones_bf[:, :], pT[s][:, :], kt == 0, kt == nk - 1, (t_const, t_acst, t_pT[s]), (tD,))
                P.add("dve", (lambda d: (lambda e: e.reciprocal(out=rden[:, :], in_=d[:, :])))(Dn), (tD,), (t_att,))
                k.tt(attf[:, :], A[:, :], rden[:, :], ALU.mult, (tA, t_att), (t_att,))
                cs = (2 * g + j + c) % 2
                k.tt(catst[cs][:, :], attf[:, :], sga[:, j, q0:q0 + 512], ALU.mult, (t_att, t_sga), (t_catst[cs],))
                k.dma(catT[2 * g + j, :, q0:q0 + 512], catst[cs][:, :], ds_cat[cs], (t_catst[cs],), (t_catT,))

    def wcols(c0, w):
        return w_in_even[:, c0:c0 + w]

    l0b_toks += [t_acst, t_qkw]
    k.fence(l0a_toks, l0b_toks)
    k.dma(biasT0[:], cst[0:16, C_BT:C_BT + T], ds_c, (), (t_acst,), eng="pool")
    k.dma(mmask[:], cst[:, C_MM:C_MM + 128].rearrange("p (a b) -> p a b", a=8), ds_c, (), (t_acst,))
    k.dma(ownm[:], cst[:, C_OWN:C_OWN + 128].rearrange("p (a b) -> p a b", a=8), ds_c, (), (t_acst,))
    k.dma(cm[:], cst[:, C_CM:C_CM + 2048].rearrange("p (j q) -> p j q", j=4), ds_c, (), (t_acst,), eng="pool")
    k.dma(esel[:], cst[0:16, C_ES:C_ES + 2048].rearrange("p (j q) -> p j q", j=16), ds_c, (), (t_acst,), eng="pool")
    k.dma(qkw[:, 0, :], q_norm_w[0:1, :].partition_broadcast(128), ds_c, (), (t_qkw,))
    k.dma(qkw[:, 1, :], k_norm_w[0:1, :].partition_broadcast(128), ds_c, (), (t_qkw,))
    k.ts(qkw[:, 1, :], qkw[:, 1, :], float(np.sqrt(128.0)), None, ALU.mult, None, (t_qkw,), (t_qkw,))
    for j in range(4):
        k.cp(qkw4[:, j, :], qkw[:, j // 2, :], (t_qkw,), (t_qkw4,))
    for g in range(int(dbg.get("ngroups", 4))):
        wq, tq = wload(wcols(256 * g, 256), KC, 256)
        wk, tk = wload(wcols(1024 + 256 * g, 256), KC, 256)
        wv, tv = wload(wcols(2048 + 256 * g, 256), KC, 256)
        wga, tga = wload(wcols(3072 + 256 * g, 256), KC, 256)
        l0b_proj(g, wq, wk, wv, tq, tk, tv)
        l0b_gate(g, wga, tga)
        l0b_gates(g)
        l0b_attn(g)
        if "qkT" in dbg and g == 0:
            o = dout("dbg_qkT", [128, 4 * T], BF16)
            dsd = P.dsem("dbg1")
            k.dma(o, qkT[:].rearrange("p a b -> p (a b)"), dsd, (t_qkT,), ())
            o2 = dout("dbg_sga", [128, 2 * T], BF16)
            k.dma(o2, sga[:].rearrange("p a b -> p (a b)"), dsd, (t_sga,), ())

    k.dma(k_s[:, :], ks_f[:].rearrange("p a b -> p (a b)"), ds_ks, (t_qkvs,), ())
    k.dma(v_s[:, :], vs_f[:].rearrange("p a b -> p (a b)"), ds_ks, (t_qkvs,), ())

    if dbg.get("sattn", 1):
        cache_k = din("cache_k", [2560 * 128, 1024])
        cache_v = din("cache_v", [2560 * 128, 1024])
        page_table = din("page_table", [1, NS * 16], I32)
    qs_d = nc.dram_tensor("qs_d", [NS, 1024], F32).ap()
    att_d = nc.dram_tensor("att_d", [NS, 1024], F32).ap()
    den_d = nc.dram_tensor("den_d", [NS, 8], F32).ap()

    def sample_attn():
        k.barrier()
        k.ptr = sattn_base
        f32t = lambda name, shape: k.sb(name, shape, F32)
        bft = lambda name, shape: k.sb(name, shape, BF16)
        ds_sa = P.dsem("sa_ld")
        t_qsd = Tok()
        k.dma(qs_d[:, :], qs_f[:].rearrange("p a b -> p (a b)"), ds_sa, (t_qkvs,), (t_qsd,))
        pti = k.sb("pti", [128, NS * 16], I32)
        idx = k.sb("idx", [128, NS * 16], I32)
        pcol = f32t("pcol", [128, 1])
        ident_f = f32t("identf2", [128, 128])
        dgm = f32t("dgm", [8, 128])
        onesf = f32t("onesf2", [128, 128])
        t_sc0 = Tok()
        k.dma(pti[:, :], page_table[0:1, :].partition_broadcast(128), ds_sa, (), (t_sc0,))
        k.dma(pcol[:, :], cst[:, C_PI:C_PI + 1], ds_sa, (), (t_sc0,), allow_slow_non_contiguous=True)
        k.dma(ident_f[:, :], cst[:, C_ID:C_ID + 128], ds_sa, (), (t_sc0,))
        k.dma(onesf[:, :], cst[:, C_ONE:C_ONE + 128], ds_sa, (), (t_sc0,))
        k.dma(dgm[:, :], cst[0:8, C_DG:C_DG + 128], ds_sa, (), (t_sc0,))
        k.ts(idx[:, :], pti[:, :], 128.0, pcol[:, 0:1], ALU.mult, ALU.add, (t_sc0,), (t_sc0,))
        sprod = f32t("sprod", [NS, 8, 128])
        sself = f32t("sself", [NS, 8])
        pself = f32t("pself", [NS, 8])
        t_self = Tok()
        k.tt(sprod[:], qs_f[:], ks_f[:], ALU.mult, (t_qkvs,), (t_self,))
        k.red(sself[:, :], sprod[:], ALU.add, (t_self,), (t_self,))
        k.act(pself[:, :], sself[:, :], AF.Exp, (t_self,), (t_self,))
        NKR, NVR = 4, 16
        kring = [f32t("kring%d" % i, [128, 1024]) for i in range(NKR)]
        t_kr = [Tok() for _ in range(NKR)]
        ds_kr = [P.dsem("kr%d" % i) for i in range(NKR)]
        vring = [bft("vring%d" % i, [128, 1024]) for i in range(NVR)]
        t_vr = [Tok() for _ in range(NVR)]
        ds_vr = [P.dsem("vr%d" % i) for i in range(4)]
        qbc = [f32t("qbc%d" % i, [128, 1024]) for i in range(2)]
        t_qbc = [Tok(), Tok()]
        ds_qbc = [P.dsem("qbc0"), P.dsem("qbc1")]
        prod = f32t("prod", [128, 8, 128])
        t_prod = Tok()
        S_all = f32t("S_all", [128, 16, 8])
        Sb = f32t("Sb", [128, 16, 8])
        Pb = bft("Pb", [128, 16, 8])
        t_S = Tok()
        sblk = f32t("sblk", [8, 16])
        sb8 = f32t("sb8", [8, 8])
        mx8s = f32t("mx8s", [8, 8])
        gb8 = f32t("gb8", [8, 8])
        bexp = f32t("bexp", [8, 16, 8])
        t_g = Tok()
        att_row = [f32t("att_row%d" % i, [1, 1024]) for i in range(2)]
        den_row = [f32t("den_row%d" % i, [1, 8]) for i in range(2)]
        t_rows = [Tok(), Tok()]
        ds_rows = [P.dsem("rows0"), P.dsem("rows1")]
        t_attd = Tok()
        den128 = f32t("den128", [1, 128])
        t_row = Tok()
        ones_c = bft("ones_c", [128, 1])
        k.memset(ones_c[:, :], 1.0, (t_sc0,))
        u32 = mybir.dt.uint32
        for s_ in range(NS):
            qb = qbc[s_ % 2]
            k.dma(qb[:, :], qs_d[s_:s_ + 1, :].partition_broadcast(128), ds_qbc[s_ % 2], (t_qsd,), (t_qbc[s_ % 2],))
            for j in range(16):
                col = s_ * 16 + j
                r_ = (s_ * 16 + j) % NKR
                P.add("pool", (lambda dst, cc: (lambda e: e.indirect_dma_start(
                    out=dst[:, :], out_offset=None, in_=cache_k[:, :],
                    in_offset=bass.IndirectOffsetOnAxis(ap=idx[:, cc:cc + 1].bitcast(u32), axis=0))))(kring[r_], col),
                    (t_sc0,), (t_kr[r_],), dsem=ds_kr[r_])
                k.tt(prod[:], kring[r_][:, :].rearrange("p (h d) -> p h d", h=8), qb[:, :].rearrange("p (h d) -> p h d", h=8), ALU.mult,
                     (t_kr[r_], t_qbc[s_ % 2]), (t_prod,))
                k.red(S_all[:, j, :], prod[:], ALU.add, (t_prod,), (t_S,))
            for j in range(16):
                col = s_ * 16 + j
                P.add("pool", (lambda dst, cc: (lambda e: e.indirect_dma_start(
                    out=dst[:, :], out_offset=None, in_=cache_v[:, :],
                    in_offset=bass.IndirectOffsetOnAxis(ap=idx[:, cc:cc + 1].bitcast(u32), axis=0))))(vring[j], col),
                    (t_sc0,), (t_vr[j],), dsem=ds_vr[j % 4])
            bk, tbk = bank[0], t_bank[0]
            for j in range(16):
                k.mm(bk[0:8, j:j + 1], S_all[:, j, :], onesf[:, 0:1], True, True, (t_S, t_sc0), (tbk,))
            k.cp(sblk[:, :], bk[0:8, 0:16], (tbk,), (t_g,))
            k.tt(sb8[:, :], sblk[:, 0:16:2], sblk[:, 1:16:2], ALU.add, (t_g,), (t_g,))
            P.add("dve", lambda e: e.max(out=mx8s[:, :], in_=sb8[:, :]), (t_g,), (t_g,))
            k.tt(gb8[:, :], sb8[:, :], mx8s[:, 2:3].broadcast_to([8, 8]), ALU.is_ge, (t_g,), (t_g,))
            k.ts(gb8[:, :], gb8[:, :], -1.0, -NEG, ALU.add, ALU.mult, (t_g,), (t_g,))
            k.tt(bexp[:].rearrange("p (n t) h -> p n t h", t=2), gb8[:, :].unsqueeze(2).unsqueeze(3).broadcast_to([8, 8, 2, 8]),
                 dgm[:, :].rearrange("p (n t h) -> p n t h", n=8, t=2), ALU.mult, (t_g, t_sc0), (t_g,))
            bk2, tbk2 = bank[1], t_bank[1]
            k.mm(bk2[:, 0:128], onesf[0:8, :], bexp[:].rearrange("p j h -> p (j h)"), True, True, (t_sc0, t_g), (tbk2,))
            k.tt(Sb[:].rearrange("p j h -> p (j h)"), S_all[:].rearrange("p j h -> p (j h)"), bk2[:, 0:128], ALU.add, (t_S, tbk2), (t_S,))
            k.act(Pb[:].rearrange("p j h -> p (j h)"), Sb[:].rearrange("p j h -> p (j h)"), AF.Exp, (t_S,), (t_S,))
            ba, tba = bank[2 + s_ % 2], t_bank[2 + s_ % 2]
            ba2, tba2 = bank[4 + s_ % 2], t_bank[4 + s_ % 2]
            for j in range(16):
                for h in range(8):
                    bb, tbb = (ba, tba) if h < 4 else (ba2, tba2)
                    k.mm(bb[0:1, (h % 4) * 128:(h % 4 + 1) * 128], Pb[:, j, h:h + 1], vring[j][:, h * 128:(h + 1) * 128], j == 0, j == 15,
                         (t_S, t_vr[j]), (tbb,))
            rs = s_ % 2
            k.cp(att_row[rs][0:1, 0:512], ba[0:1, :], (tba,), (t_rows[rs],), eng="act")
            k.cp(att_row[rs][0:1, 512:1024], ba2[0:1, :], (tba2,), (t_rows[rs],), eng="act")
            k.mm(bk[0:1, 128:256], ones_c[:, 0:1], Pb[:].rearrange("p j h -> p (j h)"), True, True, (t_sc0, t_S), (tbk,))
            k.cp(den128[0:1, :], bk[0:1, 128:256], (tbk,), (t_row,))
            k.red(den_row[rs][0:1, :], den128[0:1, :].rearrange("p (j h) -> p h j", h=8), ALU.add, (t_row,), (t_rows[rs],))
            k.dma(att_d[s_:s_ + 1, :], att_row[rs][0:1, :], ds_rows[rs], (t_rows[rs],), (t_attd,))
            k.dma(den_d[s_:s_ + 1, :], den_row[rs][0:1, :], ds_rows[rs], (t_rows[rs],), (t_attd,))
        att_t = f32t("att_t", [NS, 8, 128])
        den_t = f32t("den_t", [NS, 8])
        t_fin = Tok()
        k.dma(att_t[:].rearrange("p a b -> p (a b)"), att_d[:, :], ds_sa, (t_attd,), (t_fin,))
        k.dma(den_t[:, :], den_d[:, :], ds_sa, (t_attd,), (t_fin,))
        k.tt(sprod[:], vs_f[:], pself[:, :].unsqueeze(2).broadcast_to([NS, 8, 128]), ALU.mult, (t_qkvs, t_self), (t_self,))
        k.tt(att_t[:], att_t[:], sprod[:], ALU.add, (t_fin, t_self), (t_fin,))
        k.tt(den_t[:, :], den_t[:, :], pself[:, :], ALU.add, (t_fin, t_self), (t_fin,))
        P.add("dve", lambda e: e.reciprocal(out=den_t[:, :], in_=den_t[:, :]), (t_fin,), (t_fin,))
        k.tt(att_t[:], att_t[:], den_t[:, :].unsqueeze(2).broadcast_to([NS, 8, 128]), ALU.mult, (t_fin,), (t_fin,))
        bk, tbk = bank[0], t_bank[0]
        for h in range(8):
            k.tr(bk[:, h * 16:(h + 1) * 16], att_t[:, h, :], ident_f[0:16, 0:16], (t_fin, t_sc0), (tbk,))
        k.tt(catS[:, 0:8, :], bk[:, 0:128].rearrange("p (h s) -> p h s", h=8), sgaS[:, :, :], ALU.mult, (tbk, t_sgaS), (t_catS,))

    if dbg.get("sattn", 1):
        sample_attn()


    lam_re = din("lam_re", [64, 64])
    lam_im = din("lam_im", [64, 64])
    log_dt = din("log_dt", [1, 64])
    s5_b = din("s5_b", [64, 64, 32])
    s5_c = din("s5_c", [64, 16, 128])
    s5_d = din("s5_d", [1, 1024])
    glu_w = din("glu_w", [1024, 1024])
    glu_b = din("glu_b", [1, 1024])
    st_s5 = din("st_s5", [NS, 8192])
    s5_p = dout("s5_p", [64, 64, 2])
    s5_s = dout("s5_s", [NS, 8192])
    uT_d = nc.dram_tensor("uT_d", [8, 128, TS], BF16).ap()
    sgbT_d = nc.dram_tensor("sgbT_d", [8, 128, TS], BF16).ap()
    zT_d = nc.dram_tensor("zT_d", [8, 128, T], BF16).ap()
    t_uTd, t_sgbTd, t_zTd = Tok(), Tok(), Tok()
    k.barrier()
    k.ptr = sattn_base
    ust = [k.sb("ust%d" % i, [128, TS], BF16) for i in range(2)]
    sgst = [k.sb("sgst%d" % i, [128, TS], BF16) for i in range(2)]
    t_ust, t_sgst = [Tok(), Tok()], [Tok(), Tok()]
    ds_ust, ds_sgst = [P.dsem("ust0"), P.dsem("ust1")], [P.dsem("sgst0"), P.dsem("sgst1")]
    if dbg.get("s5", 1):
        for f in range(8):
            s_ = f % 2
            wu, tu = wload(w_in_even[:, 4096 + 128 * f:4096 + 128 * (f + 1)], KC, 128)
            wgb, tgb = wload(w_in_even[:, 5120 + 128 * f:5120 + 128 * (f + 1)], KC, 128)
            for which, (ww, tw) in enumerate(((wu, tu), (wgb, tgb))):
                for b in range(5):
                    c0 = b * 512
                    n = 512 if b < 4 else NS
                    bi = (which * 5 + b) % 2
                    for kc in range(KC):
                        k.mm(bank[bi][:, 0:n], ww[:, kc, :], hT[:, kc, c0:c0 + n], kc == 0, kc == KC - 1, (t_hT, tw), (t_bank[bi],))
                    if which == 0:
                        k.act(ust[s_][:, c0:c0 + n], bank[bi][:, 0:n], AF.Copy, (t_bank[bi],), (t_ust[s_],))
                    else:
                        k.act(sgst[s_][:, c0:c0 + n], bank[bi][:, 0:n], AF.Silu, (t_bank[bi],), (t_sgst[s_],))
            k.dma(uT_d[f], ust[s_][:, :], ds_ust[s_], (t_ust[s_],), (t_uTd,))
            k.dma(sgbT_d[f], sgst[s_][:, :], ds_sgst[s_], (t_sgst[s_],), (t_sgbTd,))

    def l0c():
        k.barrier()
        k.ptr = hT_base
        f32t = lambda name, shape: k.sb(name, shape, F32)
        TWO_PI = float(2.0 * np.pi)
        ident_f = f32t("ident_f", [128, 128])
        evec = f32t("evec", [128, 17])
        nvec = f32t("nvec", [128, 128])
        t_c2 = Tok()
        ds_s5 = P.dsem("s5ld")
        k.dma(ident_f[:], cst[:, C_ID:C_ID + 128], ds_s5, (), (t_c2,))
        k.dma(evec[:], cst[:, C_EV:C_EV + 17], ds_s5, (), (t_c2,))
        k.dma(nvec[:], cst[:, C_NV:C_NV + 128], ds_s5, (), (t_c2,))
        dvec, gbvec = f32t("dvec", [128, 8]), f32t("gbvec", [128, 8])
        yf = [f32t("yf%d" % i, [128, 512]) for i in range(2)]
        zst = [k.sb("zst%d" % i, [128, 512], BF16) for i in range(2)]
        t_yf, t_zst = [Tok(), Tok()], [Tok(), Tok()]
        ds_zst = [P.dsem("zst0"), P.dsem("zst1")]
        zS = k.sb("zS", [128, 8, NS], BF16)
        t_zS = Tok()
        glu_base = k.ptr
        lr, li, lgdt = f32t("lr", [128, 32]), f32t("li", [128, 32]), f32t("lgdt", [128, 32])
        t_l = Tok()
        with nc.allow_non_contiguous_dma(reason="tiny parameter relayout"):
            for g2 in range(2):
                hs = slice(g2 * 64, (g2 + 1) * 64)
                k.dma(lr[hs, :], lam_re.rearrange("(q g) p -> g p q", g=2)[g2], ds_s5, (), (t_l,), allow_slow_non_contiguous=True)
                k.dma(li[hs, :], lam_im.rearrange("(q g) p -> g p q", g=2)[g2], ds_s5, (), (t_l,), allow_slow_non_contiguous=True)
                k.dma(lgdt[hs, :], log_dt.rearrange("o (q g) -> g o q", g=2)[g2].partition_broadcast(64), ds_s5, (), (t_l,), allow_slow_non_contiguous=True)
            k.dma(dvec[:, :], s5_d.rearrange("o (f p) -> p (o f)", p=128), ds_s5, (), (t_l,), allow_slow_non_contiguous=True)
            k.dma(gbvec[:, :], glu_b.rearrange("o (f p) -> p (o f)", p=128), ds_s5, (), (t_l,), allow_slow_non_contiguous=True)
        NE = 17 * 32
        tA, tB, tC, tD = (f32t("tmp%d" % i, [128, 544]) for i in range(4))
        tI = k.sb("tmpI", [128, 544], I32)
        t_tmp = Tok()

        def sincos(ang, n, out_s, out_c, r, w):
            for shift, out in ((0.0, out_s), (float(np.pi / 2), out_c)):
                if out is None:
                    continue
                k.ts(tA[:, 0:n], ang, shift, 1.0 / TWO_PI, ALU.add, ALU.mult, r, (t_tmp,))
                k.cp(tI[:, 0:n], tA[:, 0:n], (t_tmp,), (t_tmp,))
                k.cp(tA[:, 0:n], tI[:, 0:n], (t_tmp,), (t_tmp,))
                k.stt(tA[:, 0:n], tA[:, 0:n], -TWO_PI, ang, ALU.mult, ALU.add, r + (t_tmp,), (t_tmp,))
                k.act(out, tA[:, 0:n], AF.Sin, (t_tmp,), w, bias=shift) if shift == 0.0 else None
                if shift != 0.0:
                    k.ts(tA[:, 0:n], tA[:, 0:n], shift, None, ALU.add, None, (t_tmp,), (t_tmp,))
                    k.ts(tB[:, 0:n], tA[:, 0:n], float(np.pi), -TWO_PI, ALU.is_gt, ALU.mult, (t_tmp,), (t_tmp,))
                    k.tt(tA[:, 0:n], tA[:, 0:n], tB[:, 0:n], ALU.add, (t_tmp,), (t_tmp,))
                    k.act(out, tA[:, 0:n], AF.Sin, (t_tmp,), w)

        dtv, rho, th = f32t("dtv", [128, 32]), f32t("rho", [128, 32]), f32t("th", [128, 32])
        k.act(dtv[:, :], lgdt[:, :], AF.Exp, (t_l,), (t_l,))
        k.tt(rho[:, :], lr[:, :], dtv[:, :], ALU.mult, (t_l,), (t_l,))
        k.tt(th[:, :], li[:, :], dtv[:, :], ALU.mult, (t_l,), (t_l,))
        RH, TH, MAG = f32t("RH", [128, 17, 32]), f32t("TH", [128, 17, 32]), f32t("MAG", [128, 17, 32])
        APR, API = f32t("APR", [128, 17, 32]), f32t("API", [128, 17, 32])
        SN, CS = f32t("SN", [128, 17, 32]), f32t("CS", [128, 17, 32])
        t_tab = Tok()
        ev_b = evec[:, :].unsqueeze(2).broadcast_to([128, 17, 32])
        k.tt(RH[:], rho[:, :].unsqueeze(1).broadcast_to([128, 17, 32]), ev_b, ALU.mult, (t_l, t_c2), (t_tab,))
        k.tt(TH[:], th[:, :].unsqueeze(1).broadcast_to([128, 17, 32]), ev_b, ALU.mult, (t_l, t_c2), (t_tab,))
        flat = lambda t: t[:].rearrange("p a b -> p (a b)")
        k.act(flat(MAG), flat(RH), AF.Exp, (t_tab,), (t_tab,))
        sincos(flat(TH), NE, flat(SN), flat(CS), (t_tab,), (t_tab,))
        k.tt(APR[:], MAG[:], CS[:], ALU.mult, (t_tab,), (t_tab,))
        k.tt(API[:], MAG[:], SN[:], ALU.mult, (t_tab,), (t_tab,))
        ph16 = f32t("ph16", [128, 32])
        k.ts(tA[:, 0:32], TH[:, 16, :], 1.0 / TWO_PI, None, ALU.mult, None, (t_tab,), (t_tmp,))
        k.cp(tI[:, 0:32], tA[:, 0:32], (t_tmp,), (t_tmp,))
        k.cp(tA[:, 0:32], tI[:, 0:32], (t_tmp,), (t_tmp,))
        k.stt(ph16[:, :], tA[:, 0:32], -TWO_PI, TH[:, 16, :], ALU.mult, ALU.add, (t_tmp, t_tab), (t_tab,))
        cr, ci = f32t("cr", [128, 32]), f32t("ci", [128, 32])
        e1, e2, e3 = tA[:, 0:32], tB[:, 0:32], tC[:, 0:32]
        k.tt(e1, lr[:, :], lr[:, :], ALU.mult, (t_l,), (t_tmp,))
        k.tt(e2, li[:, :], li[:, :], ALU.mult, (t_l,), (t_tmp,))
        k.tt(e1, e1, e2, ALU.add, (t_tmp,), (t_tmp,))
        P.add("dve", lambda e: e.reciprocal(out=tD[:, 0:32], in_=tA[:, 0:32]), (t_tmp,), (t_tmp,))
        k.ts(e3, APR[:, 1, :], -1.0, None, ALU.add, None, (t_tab,), (t_tmp,))
        k.tt(e1, e3, lr[:, :], ALU.mult, (t_tmp, t_l), (t_tmp,))
        k.tt(e2, API[:, 1, :], li[:, :], ALU.mult, (t_tab, t_l), (t_tmp,))
        k.tt(e1, e1, e2, ALU.add, (t_tmp,), (t_tmp,))
        k.tt(cr[:, :], e1, tD[:, 0:32], ALU.mult, (t_tmp,), (t_tab,))
        k.tt(e1, API[:, 1, :], lr[:, :], ALU.mult, (t_tab, t_l), (t_tmp,))
        k.tt(e2, e3, li[:, :], ALU.mult, (t_tmp, t_l), (t_tmp,))
        k.tt(e1, e1, e2, ALU.subtract, (t_tmp,), (t_tmp,))
        k.tt(ci[:, :], e1, tD[:, 0:32], ALU.mult, (t_tmp,), (t_tab,))
        if dbg.get("s5stop") == 1:
            return
        Braw = f32t("Braw", [128, 32, 32])
        Bbr, Bbi = f32t("Bbr", [128, 32, 16]), f32t("Bbi", [128, 32, 16])
        t_B = Tok()
        for g2 in range(2):
            hs = slice(g2 * 64, (g2 + 1) * 64)
            k.dma(Braw[hs, :, :], s5_b.rearrange("(q g) p x -> g p q x", g=2)[g2], ds_s5, (), (t_B,))
        Bre = Braw[:, :, 0:32:2]
        Bim = Braw[:, :, 1:32:2]
        crb = cr[:, :].unsqueeze(2).broadcast_to([128, 32, 16])
        cib = ci[:, :].unsqueeze(2).broadcast_to([128, 32, 16])
        v3 = lambda t: t[:, 0:512].rearrange("p (a b) -> p a b", a=32)
        k.tt(v3(tA), Bre, crb, ALU.mult, (t_B, t_tab), (t_tmp,))
        k.tt(v3(tB), Bim, cib, ALU.mult, (t_B, t_tab), (t_tmp,))
        k.tt(Bbr[:], v3(tA), v3(tB), ALU.subtract, (t_tmp,), (t_B,))
        k.tt(v3(tA), Bim, crb, ALU.mult, (t_B, t_tab), (t_tmp,))
        k.tt(v3(tB), Bre, cib, ALU.mult, (t_B, t_tab), (t_tmp,))
        k.tt(Bbi[:], v3(tA), v3(tB), ALU.add, (t_tmp,), (t_B,))
        if dbg.get("s5stop") == 2:
            return
        big_base = k.ptr
        big = f32t("big", [16, 8192])
        t_big = Tok()
        k.ptr = big_base
        W4 = [128, 17, 4, 16]
        t1, t2 = f32t("t1", W4), f32t("t2", W4)
        Wr, Wi = f32t("Wr", [128, 16, 4, 2, 16]), f32t("Wi", [128, 16, 4, 2, 16])
        Kw = k.sb("Kw", [128, 16, 128], BF16)
        Win = k.sb("Win", [128, 16, 2, 128], BF16)
        assert k.ptr >= big_base + 32768
        CreT, CimT = f32t("CreT", [128, 32, 16]), f32t("CimT", [128, 32, 16])
        X0r, X0i = f32t("X0r", [128, 32, 16]), f32t("X0i", [128, 32, 16])
        CrePad, NCimPad = f32t("CrePad", [128, 32, 2, 16]), f32t("NCimPad", [128, 32, 2, 16])
        t_C = Tok()
        bview = big[:, :].rearrange("c (q x) -> c q x", q=32)

        def to_pair_layout(dst_r, dst_i, wtok):
            for ri, dst in ((0, dst_r), (1, dst_i)):
                pb = bank[ri]
                pbv = pb[:, :].rearrange("p (q c) -> p q c", q=32)
                for q in range(32):
                    k.tr(pbv[:, q, :], bview[:, q, ri:256:2], ident_f[0:16, 0:16], (t_big, t_c2), (t_bank[ri],))
                k.cp(dst[:], pbv, (t_bank[ri],), (wtok,), eng="act")

        k.dma(big[:, :].rearrange("c (q g x) -> c q g x", q=32, g=2), s5_c.rearrange("(q g) c x -> c q g x", g=2), ds_s5, (), (t_big,))
        to_pair_layout(CreT, CimT, t_C)
        k.dma(big[:, :], st_s5[:, :], ds_s5, (), (t_big,))
        to_pair_layout(X0r, X0i, t_C)
        k.memset(CrePad[:].rearrange("p a b c -> p (a b c)"), 0.0, (t_C,))
        k.memset(NCimPad[:].rearrange("p a b c -> p (a b c)"), 0.0, (t_C,))
        for g2 in range(2):
            hs = slice(g2 * 64, (g2 + 1) * 64)
            k.cp(CrePad[hs, :, g2, :], CreT[hs, :, :], (t_C,), (t_C,))
            k.ts(NCimPad[hs, :, g2, :], CimT[hs, :, :], -1.0, None, ALU.mult, None, (t_C,), (t_C,))

        if dbg.get("s5stop") == 3:
            return
        Vr, Vi = k.sb("Vr", [128, 17, 4, 2, 16], BF16), k.sb("Vi", [128, 17, 4, 2, 16], BF16)
        t_w = Tok()
        t_t12 = Tok()
        k.fence([t_big], [t_w, t_t12])
        for tz in (Wr, Wi):
            k.memset(tz[:].rearrange("p a b c d -> p (a b c d)"), 0.0, (t_w,))
        for tz in (Vr, Vi):
            k.memset(tz[:].rearrange("p a b c d -> p (a b c d)"), 0.0, (t_w,))
        k.memset(Kw[:].rearrange("p a b -> p (a b)"), 0.0, (t_w,))
        uTb = [k.sb("uTb%d" % i, [128, TS], BF16) for i in range(2)]
        t_uTb = [Tok(), Tok()]
        ds_uTb = [P.dsem("uTb0"), P.dsem("uTb1")]
        Sre, Sim = f32t("Sre", [128, 4, 128]), f32t("Sim", [128, 4, 128])
        cosn, sinn = f32t("cosn", [128, 4, 128]), f32t("sinn", [128, 4, 128])
        m1, m2 = f32t("m1", [128, 4, 128]), f32t("m2", [128, 4, 128])
        Zr, Zi = f32t("Zr", [128, 4, 128]), f32t("Zi", [128, 4, 128])
        R0 = f32t("R0", [128, 4, 128])
        Xpr, Xpi = k.sb("Xpr", [128, 4, 128], BF16), k.sb("Xpi", [128, 4, 128], BF16)
        t_scan = Tok()
        k.memset(Xpr[:].rearrange("p a b -> p (a b)"), 0.0, (t_scan,))
        k.memset(Xpi[:].rearrange("p a b -> p (a b)"), 0.0, (t_scan,))
        s5st_p = f32t("s5st_p", [128, 32, 2])
        s5st_s = f32t("s5st_s", [128, 32, 2, 16])
        t_st = Tok()
        xsn_r, xsn_i = f32t("xsn_r", [128, 4, 16]), f32t("xsn_i", [128, 4, 16])
        xsb_r, xsb_i = k.sb("xsb_r", [128, 4, 16], BF16), k.sb("xsb_i", [128, 4, 16], BF16)
        t_xs = Tok()
        yfs = f32t("yfs", [128, 16])
        xsps = f32t("xsps", [128, 4, 2, 16])
        t_xsps = Tok()
        f2 = lambda t: t[:].rearrange("p a b -> p (a b)")
        bc_e = lambda tab, ne, qs: tab[:, 0:ne, qs].unsqueeze(3).broadcast_to([128, ne, 4, 16])
        bc_x = lambda src, ne, qs: src[:, qs, :].unsqueeze(1).broadcast_to([128, ne, 4, 16])

        for f in range(8):
            qs = slice(4 * f, 4 * f + 4)
            s_ = f % 2
            k.dma(uTb[s_][:, :], uT_d[f], ds_uTb[s_], (t_uTd,), (t_uTb[s_],))
            for dst, (ta, xa, tb, xb, op) in ((Wr, (APR, Bbr, API, Bbi, ALU.subtract)), (Wi, (APR, Bbi, API, Bbr, ALU.add))):
                k.tt(t1[:, 0:16], bc_e(ta, 16, qs), bc_x(xa, 16, qs), ALU.mult, (t_tab, t_B), (t_t12,))
                k.tt(t2[:, 0:16], bc_e(tb, 16, qs), bc_x(xb, 16, qs), ALU.mult, (t_tab, t_B), (t_t12,))
                for g2 in range(2):
                    hs = slice(g2 * 64, (g2 + 1) * 64)
                    k.tt(dst[hs, :, :, g2, :], t1[hs, 0:16], t2[hs, 0:16], op, (t_t12,), (t_w,))
            kb = bank[2]
            kbv = kb[:, :].rearrange("p (a b) -> p a b", a=4)
            for t0 in range(0, 16, 4):
                for tl in range(4):
                    tau = t0 + tl
                    for qi in range(4):
                        ps_ = slice(32 * qi, 32 * qi + 32)
                        k.mm(kbv[ps_, tl, ps_], Wr[:, tau, qi, :, :].rearrange("p a b -> p (a b)"),
                             CrePad[:, 4 * f + qi, :, :].rearrange("p a b -> p (a b)"), True, False, (t_w, t_C), (t_bank[2],), tp=(0, 32 * qi))
                        k.mm(kbv[ps_, tl, ps_], Wi[:, tau, qi, :, :].rearrange("p a b -> p (a b)"),
                             NCimPad[:, 4 * f + qi, :, :].rearrange("p a b -> p (a b)"), False, True, (t_w, t_C), (t_bank[2],), tp=(0, 32 * qi))
                for qi in range(4):
                    ps_ = slice(32 * qi, 32 * qi + 32)
                    k.cp(Kw[ps_, t0:t0 + 4, ps_], kbv[ps_, :, ps_], (t_bank[2],), (t_w,), eng="act")
            cnt = 0
            for i in range(16):
                for ri, src in ((0, Wr), (1, Wi)):
                    wb = bank[3 + (cnt // 4) % 2]
                    twb = t_bank[3 + (cnt // 4) % 2]
                    k.tr(wb[:, (cnt % 4) * 128:(cnt % 4 + 1) * 128], src[:, 15 - i, :, :, :].rearrange("p a b c -> p (a b c)"),
                         ident_f[:, :], (t_w, t_c2), (twb,))
                    cnt += 1
                    if cnt % 4 == 0:
                        i0 = (cnt - 4) // 2
                        k.cp(Win[:, i0:i0 + 2, :, :].rearrange("p a b c -> p (a b c)"), wb[:, :], (twb,), (t_w,), eng="act")
            k.tt(t1[:], bc_e(APR, 17, qs), bc_x(CreT, 17, qs), ALU.mult, (t_tab, t_C), (t_t12,))
            k.tt(t2[:], bc_e(API, 17, qs), bc_x(CimT, 17, qs), ALU.mult, (t_tab, t_C), (t_t12,))
            for g2 in range(2):
                hs = slice(g2 * 64, (g2 + 1) * 64)
                k.tt(Vr[hs, :, :, g2, :], t1[hs], t2[hs], ALU.subtract, (t_t12,), (t_w,))
            k.tt(t1[:], bc_e(API, 17, qs), bc_x(CreT, 17, qs), ALU.mult, (t_tab, t_C), (t_t12,))
            k.tt(t2[:], bc_e(APR, 17, qs), bc_x(CimT, 17, qs), ALU.mult, (t_tab, t_C), (t_t12,))
            for g2 in range(2):
                hs = slice(g2 * 64, (g2 + 1) * 64)
                k.stt(Vi[hs, :, :, g2, :], t1[hs], -1.0, t2[hs], ALU.mult, ALU.subtract, (t_t12,), (t_w,))
            if dbg.get("s5stop") == 4:
                return
            u = uTb[s_]
            for qi in range(4):
                ps_ = slice(32 * qi, 32 * qi + 32)
                for ri in range(2):
                    for i in range(16):
                        k.mm(bank[qi][:, ri * 128:(ri + 1) * 128], Win[ps_, i, ri, :], u[ps_, i:T:16], i == 0, i == 15,
                             (t_w, t_uTb[s_]), (t_bank[qi],), tp=(32 * qi, 0))
                k.cp(Sre[:, qi, :], bank[qi][:, 0:128], (t_bank[qi],), (t_scan,), eng="act")
                k.cp(Sim[:, qi, :], bank[qi][:, 128:256], (t_bank[qi],), (t_scan,), eng="act")
            if dbg.get("s5stop") == 41:
                return
            k.tt(m1[:], ph16[:, qs].unsqueeze(2).broadcast_to([128, 4, 128]), nvec[:, :].unsqueeze(1).broadcast_to([128, 4, 128]),
                 ALU.mult, (t_tab, t_c2), (t_scan,))
            k.ts(f2(m1), f2(m1), float(128 * 2 * np.pi), None, ALU.add, None, (t_scan,), (t_scan,))
            sincos(f2(m1), 512, f2(sinn), f2(cosn), (t_scan,), (t_scan,))
            if dbg.get("s5stop") == 42:
                return
            k.cp(R0[:], MAG[:, 16, qs].unsqueeze(2).broadcast_to([128, 4, 128]), (t_tab,), (t_scan,))
            k.memset(R0[:, :, 0:1], 0.0, (t_scan,))
            k.tt(m1[:], Sre[:], cosn[:], ALU.mult, (t_scan,), (t_scan,))
            k.tt(m2[:], Sim[:], sinn[:], ALU.mult, (t_scan,), (t_scan,))
            k.tt(m1[:], m1[:], m2[:], ALU.add, (t_scan,), (t_scan,))
            k.tt(m2[:], Sim[:], cosn[:], ALU.mult, (t_scan,), (t_scan,))
            k.tt(Sim[:], Sre[:], sinn[:], ALU.mult, (t_scan,), (t_scan,))
            k.tt(m2[:], m2[:], Sim[:], ALU.subtract, (t_scan,), (t_scan,))
            if dbg.get("s5stop") == 43:
                return
            P.add("dve", lambda e: e.tensor_tensor_scan(out=f2(Zr), data0=f2(R0), data1=f2(m1), initial=0.0, op0=ALU.mult, op1=ALU.add),
                  (t_scan,), (t_scan,))
            P.add("dve", lambda e: e.tensor_tensor_scan(out=f2(Zi), data0=f2(R0), data1=f2(m2), initial=0.0, op0=ALU.mult, op1=ALU.add),
                  (t_scan,), (t_scan,))
            if dbg.get("s5stop") == 44:
                return
            k.tt(m1[:], Zr[:], cosn[:], ALU.mult, (t_scan,), (t_scan,))
            k.tt(m2[:], Zi[:], sinn[:], ALU.mult, (t_scan,), (t_scan,))
            k.tt(Sre[:], m1[:], m2[:], ALU.subtract, (t_scan,), (t_scan,))
            k.tt(m1[:], Zr[:], sinn[:], ALU.mult, (t_scan,), (t_scan,))
            k.tt(m2[:], Zi[:], cosn[:], ALU.mult, (t_scan,), (t_scan,))
            k.tt(Sim[:], m1[:], m2[:], ALU.add, (t_scan,), (t_scan,))
            if dbg.get("s5stop") == 45:
                return
            k.cp(s5st_p[:, qs, 0:1], Sre[:, :, 127:128], (t_scan,), (t_st,))
            k.cp(s5st_p[:, qs, 1:2], Sim[:, :, 127:128], (t_scan,), (t_st,))
            k.cp(Xpr[:, :, 1:128], Sre[:, :, 0:127], (t_scan,), (t_scan,))
            k.cp(Xpi[:, :, 1:128], Sim[:, :, 0:127], (t_scan,), (t_scan,))
            if dbg.get("s5stop") == 5:
                return
            for b in range(4):
                yb = bank[4 + b % 2]
                tyb = t_bank[4 + b % 2]
                ybv = yb[:, :].rearrange("p (n j) -> p n j", j=16)
                ubv = u[:, b * 512:(b + 1) * 512].rearrange("p (n j) -> p n j", j=16)
                for tau in range(16):
                    k.mm(ybv[:, :, tau:16], Kw[:, tau, :], ubv[:, :, 0:16 - tau], tau == 0, False, (t_w, t_uTb[s_]), (tyb,))
                for qi in range(4):
                    ps_ = slice(32 * qi, 32 * qi + 32)
                    for j in range(16):
                        last = (qi == 3 and j == 15)
                        k.mm(yb[ps_, j:512:16], Vr[:, j + 1, qi, :, :].rearrange("p a b -> p (a b)"), Xpr[:, qi, b * 32:(b + 1) * 32],
                             False, False, (t_w, t_scan), (tyb,), tp=(0, 32 * qi))
                        k.mm(yb[ps_, j:512:16], Vi[:, j + 1, qi, :, :].rearrange("p a b -> p (a b)"), Xpi[:, qi, b * 32:(b + 1) * 32],
                             False, last, (t_w, t_scan), (tyb,), tp=(0, 32 * qi))
                ys_ = (4 * f + b) % 2
                k.stt(yf[ys_][:, :], u[:, b * 512:(b + 1) * 512], dvec[:, f:f + 1], yb[:, :], ALU.mult, ALU.add,
                      (t_uTb[s_], t_l, tyb), (t_yf[ys_],))
                k.act(zst[ys_][:, :], yf[ys_][:, :], AF.Gelu_apprx_tanh, (t_yf[ys_],), (t_zst[ys_],))
                k.dma(zT_d[f, :, b * 512:(b + 1) * 512], zst[ys_][:, :], ds_zst[ys_], (t_zst[ys_],), (t_zTd,))
            if dbg.get("s5stop") == 6:
                return
            xbv = xsps[:]
            for qi in range(4):
                ps_ = slice(32 * qi, 32 * qi + 32)
                for ri in range(2):
                    k.mm(bank[qi][:, 256 + ri * 16:256 + (ri + 1) * 16], Win[ps_, 15, ri, :], u[ps_, T:TS], True, True,
                         (t_w, t_uTb[s_]), (t_bank[qi],), tp=(32 * qi, 0))
                k.cp(xsps[:, qi, :, :], bank[qi][:, 256:288].rearrange("p (r s) -> p r s", r=2), (t_bank[qi],), (t_xsps,), eng="act")
            a1r = APR[:, 1, qs].unsqueeze(2).broadcast_to([128, 4, 16])
            a1i = API[:, 1, qs].unsqueeze(2).broadcast_to([128, 4, 16])
            k.tt(xsn_r[:], X0r[:, qs, :], a1r, ALU.mult, (t_C, t_tab), (t_xs,))
            k.tt(xsn_i[:], X0i[:, qs, :], a1i, ALU.mult, (t_C, t_tab), (t_xs,))
            k.tt(xsn_r[:], xsn_r[:], xsn_i[:], ALU.subtract, (t_xs,), (t_xs,))
            k.tt(s5st_s[:, qs, 0, :], xsn_r[:], xbv[:, :, 0, :], ALU.add, (t_xs, t_xsps), (t_st,))
            k.tt(xsn_r[:], X0r[:, qs, :], a1i, ALU.mult, (t_C, t_tab), (t_xs,))
            k.tt(xsn_i[:], X0i[:, qs, :], a1r, ALU.mult, (t_C, t_tab), (t_xs,))
            k.tt(xsn_r[:], xsn_r[:], xsn_i[:], ALU.add, (t_xs,), (t_xs,))
            k.tt(s5st_s[:, qs, 1, :], xsn_r[:], xbv[:, :, 1, :], ALU.add, (t_xs, t_xsps), (t_st,))
            k.cp(xsb_r[:], s5st_s[:, qs, 0, :], (t_st,), (t_xs,))
            k.cp(xsb_i[:], s5st_s[:, qs, 1, :], (t_st,), (t_xs,))
            yb = bank[3]
            for qi in range(4):
                ps_ = slice(32 * qi, 32 * qi + 32)
                k.mm(yb[ps_, 0:16], Vr[:, 0, qi, :, :].rearrange("p a b -> p (a b)"), xsb_r[:, qi, :], True, False, (t_w, t_xs), (t_bank[3],), tp=(0, 32 * qi))
                k.mm(yb[ps_, 0:16], Vi[:, 0, qi, :, :].rearrange("p a b -> p (a b)"), xsb_i[:, qi, :], False, True, (t_w, t_xs), (t_bank[3],), tp=(0, 32 * qi))
            k.stt(yfs[:, :], u[:, T:TS], dvec[:, f:f + 1], yb[:, 0:16], ALU.mult, ALU.add, (t_uTb[s_], t_l, t_bank[3]), (t_xs,))
            k.act(zS[:, f, :], yfs[:, :], AF.Gelu_apprx_tanh, (t_xs,), (t_zS,))

        ds_so = P.dsem("s5out")
        with nc.allow_non_contiguous_dma(reason="small state relayout"):
            for g2 in range(2):
                hs = slice(g2 * 64, (g2 + 1) * 64)
                k.dma(s5_p.rearrange("(q g) p r -> g p q r", g=2)[g2], s5st_p[hs, :, :], ds_so, (t_st,), (), allow_slow_non_contiguous=True)
        k.fence([t_w, t_t12], [t_big])
        for ri in range(2):
            pb = bank[ri]
            pbv = pb[0:16, :].rearrange("s (q x) -> s q x", q=4)
            for q0 in range(0, 32, 4):
                for ql in range(4):
                    k.tr(pbv[:, ql, :], s5st_s[:, q0 + ql, ri, :], ident_f[:, :], (t_st, t_c2), (t_bank[ri],))
                k.cp(bview[:, q0:q0 + 4, ri:256:2], pbv, (t_bank[ri],), (t_big,), eng="act")
        k.dma(s5_s[:, :], big[:, :], ds_so, (t_big,), ())

        k.barrier()
        k.ptr = glu_base
        zT = k.sb("zT", [128, 8, T], BF16)
        t_zT = Tok()
        ds_zl = P.dsem("zload")
        for f in range(8):
            k.dma(zT[:, f, :], zT_d[f], ds_zl, (t_zTd,), (t_zT,))
        sgl = [k.sb("sgl%d" % i, [128, TS], BF16) for i in range(2)]
        t_sgl = [Tok(), Tok()]
        ds_sgl = [P.dsem("sgl0"), P.dsem("sgl1")]
        sg = [f32t("sg%d" % i, [128, 512]) for i in range(2)]
        t_sg = [Tok(), Tok()]
        g_ct = [0]
        for fo in range(8):
            s_ = fo % 2
            wg, twg = wload(glu_w[:, fo * 128:(fo + 1) * 128], 8, 128)
            k.dma(sgl[s_][:, :], sgbT_d[fo], ds_sgl[s_], (t_sgbTd,), (t_sgl[s_],))
            for b in range(5):
                n = 512 if b < 4 else NS
                c0 = b * 512
                bi = 4 + b % 2
                for kc in range(8):
                    rhs = zT[:, kc, c0:c0 + n] if b < 4 else zS[:, kc, :]
                    k.mm(bank[bi][:, 0:n], wg[:, kc, :], rhs, kc == 0, kc == 7, (twg, t_zT, t_zS), (t_bank[bi],))
                gi = g_ct[0] % 2
                g_ct[0] += 1
                k.act(sg[gi][:, 0:n], bank[bi][:, 0:n], AF.Sigmoid, (t_bank[bi], t_l), (t_sg[gi],), bias=gbvec[:, fo:fo + 1])
                zsrc = zT[:, fo, c0:c0 + n] if b < 4 else zS[:, fo, :]
                k.tt(sg[gi][:, 0:n], sg[gi][:, 0:n], zsrc, ALU.mult, (t_sg[gi], t_zT, t_zS), (t_sg[gi],))
                if b < 4:
                    cs = gi
                    k.tt(zst[cs][:, :], sg[gi][:, :], sgl[s_][:, c0:c0 + 512], ALU.mult, (t_sg[gi], t_sgl[s_]), (t_zst[cs],))
                    k.dma(catT[8 + fo, :, c0:c0 + 512], zst[cs][:, :], ds_zst[cs], (t_zst[cs],), (t_catT,))
                else:
                    k.tt(catS[:, 8 + fo, :], sg[gi][:, 0:NS], sgl[s_][:, T:TS], ALU.mult, (t_sg[gi], t_sgl[s_]), (t_catS,))

    if dbg.get("s5", 1):
        l0c()


    w_out_even = din("w_out_even", [D, D])
    x1_d = nc.dram_tensor("x1_d", [TS, D], F32).ap()
    t_x1d = Tok()

    def outproj(w_out, nkc, actT_d, actS, t_actTd, t_actS, resp, ress, r_toks, dstp, dsts, t_dst, nm):
        k.barrier()
        k.ptr = hT_base
        half = nkc // 16
        actR = k.sb(nm + "actR", [128, 16, TS], BF16)
        t_actR = Tok()
        ds_a = P.dsem(nm + "actR")
        k.ptr = region
        xin = [k.sb(nm + "xin%d" % i, [128, 256], F32) for i in range(2)]
        xo = [k.sb(nm + "xo%d" % i, [128, 256], F32) for i in range(2)]
        acc = k.sb(nm + "acc", [128, NT + 1, 256], F32) if half > 1 else None
        t_acc = Tok()
        t_xin, t_xo = [Tok(), Tok()], [Tok(), Tok()]
        ds_xin, ds_xo = [P.dsem(nm + "xin0"), P.dsem(nm + "xin1")], [P.dsem(nm + "xo0"), P.dsem(nm + "xo1")]
        cnt = 0
        for db in range(8):
            for hp in range(half):
                if db == 0 or half > 1:
                    for ft in range(16):
                        k.dma(actR[:, ft, 0:T], actT_d[hp * 16 + ft], ds_a, (t_actTd,), (t_actR,))
                    k.cp(actR[:, :, T:TS], actS[:, hp * 16:(hp + 1) * 16, :], (t_actS,), (t_actR,))
                slab, tslab = wload(w_out[hp * 2048:(hp + 1) * 2048, db * 256:(db + 1) * 256], 16, 256)
                for i in range(NT + 1):
                    np_ = 128 if i < NT else NS
                    c0 = i * 128
                    s_ = cnt % 2
                    cnt += 1
                    bk, tbk = bank[s_], t_bank[s_]
                    for kc in range(16):
                        k.mm(bk[:np_, 0:256], actR[:, kc, c0:c0 + np_], slab[:, kc, :], kc == 0, kc == 15, (t_actR, tslab), (tbk,))
                    if hp < half - 1:
                        k.cp(acc[:np_, i, :], bk[:np_, 0:256], (tbk,), (t_acc,), eng="act")
                        continue
                    rsrc = resp[c0:c0 + 128, db * 256:(db + 1) * 256] if i < NT else ress[:, db * 256:(db + 1) * 256]
                    k.dma(xin[s_][:np_, :], rsrc, ds_xin[s_], tuple(r_toks), (t_xin[s_],))
                    if half > 1:
                        k.tt(xin[s_][:np_, :], xin[s_][:np_, :], acc[:np_, i, :], ALU.add, (t_xin[s_], t_acc), (t_xin[s_],), eng=POOLC)
                    k.tt(xo[s_][:np_, :], bk[:np_, 0:256], xin[s_][:np_, :], ALU.add, (tbk, t_xin[s_]), (t_xo[s_],))
                    dd = dstp[c0:c0 + 128, db * 256:(db + 1) * 256] if i < NT else dsts[:, db * 256:(db + 1) * 256]
                    k.dma(dd, xo[s_][:np_, :], ds_xo[s_], (t_xo[s_],), (t_dst,))

    def l1a():
        k.barrier()
        k.ptr = region
        nonlocal_alloc = {}
        return nonlocal_alloc

    if dbg.get("l0d", 1) and dbg.get("s5", 1):
        outproj(w_out_even, 16, catT, catS, t_catT, t_catS, xp, xs, (), x1_d[0:T, :], x1_d[T:TS, :], t_x1d, "od0")
        k.barrier()
        k.ptr = region
        nw_bc = k.sb("nw_bc1", [128, D], F32)
        xt = [k.sb("xt1_%d" % i, [128, D], F32) for i in range(2)]
        hb = [k.sb("hb1_%d" % i, [128, D], BF16) for i in range(2)]
        junk = k.sb("junk1", [128, D], BF16)
        ss = k.sb("ss1", [128, NT + 1], F32)
        rstd = k.sb("rstd1", [128, NT + 1], F32)
        l0a(1, x1_d, x1_d[T:TS, :], rtoks=(t_x1d,))
    if "x1" in dbg:
        o = dout("dbg_x1", [TS, D], F32)
        dsd3 = P.dsem("dbg3")
        k.dma(o, x1_d, dsd3, (t_x1d,), ())
        o = dout("dbg_hT", [128, KC * TS], BF16)
        k.dma(o, hT[:].rearrange("p a b -> p (a b)"), dsd3, (t_hT,), ())


    w_in_odd = din("w_in_odd", [D, 10304])
    conv_w = din("conv_w", [4, 6144])
    conv_b = din("conv_b", [1, 6144])
    dt_bias = din("dt_bias", [1, 64])
    a_log = din("a_log", [1, 64])
    ssd_d = din("ssd_d", [1, 64])
    ssd_nw = din("ssd_nw", [1, 4096])
    w_out_odd = din("w_out_odd", [4096, D])
    y_p = dout("y_p", [T, D])
    y_s = dout("y_s", [NS, D])
    conv_p = dout("conv_p", [3, 6144])
    ssd_p = dout("ssd_p", [64 * 64, 128])
    st_conv = din("st_conv", [NS, 3, 6144])
    st_ssd = din("st_ssd", [NS, 4096, 128])
    conv_s = dout("conv_s", [NS, 3, 6144])
    ssd_s = dout("ssd_s", [NS, 4096, 128])
    ynT_d = nc.dram_tensor("ynT_d", [32, 128, T], BF16).ap()
    t_ynTd = Tok()
    ynS = k.sb("ynS", [128, 32, NS], BF16) if False else None

    def l1():
        k.barrier()
        k.ptr = region
        f32t = lambda name, shape: k.sb(name, shape, F32)
        bft = lambda name, shape: k.sb(name, shape, BF16)
        ds_l1 = P.dsem("l1ld")
        Uf, onesf, ident_f = f32t("Uf", [128, 128]), f32t("onesf", [128, 128]), f32t("identf1", [128, 128])
        t_c3 = Tok()
        k.dma(Uf[:], cst[:, C_U:C_U + 128], ds_l1, (), (t_c3,))
        k.dma(onesf[:], cst[:, C_ONE:C_ONE + 128], ds_l1, (), (t_c3,))
        k.dma(ident_f[:], cst[:, C_ID:C_ID + 128], ds_l1, (), (t_c3,))
        cm1 = bft("cm1", [128, 128])
        esel = bft("esel8", [8, 8, 128])
        k.dma(cm1[:], cst[:, C_CM:C_CM + 128], ds_l1, (), (t_c3,), eng="pool")
        for j in range(8):
            k.dma(esel[:, j, :], cst[0:8, C_ES + j * 128:C_ES + (j + 1) * 128], ds_l1, (), (t_c3,), eng="pool")
        dtb_bc, A_bc, D_bc = f32t("dtb_bc", [128, 64]), f32t("A_bc", [128, 64]), f32t("D_bc", [128, 64])
        k.dma(dtb_bc[:], dt_bias[0:1, :].partition_broadcast(128), ds_l1, (), (t_c3,))
        k.dma(A_bc[:], a_log[0:1, :].partition_broadcast(128), ds_l1, (), (t_c3,))
        k.dma(D_bc[:], ssd_d[0:1, :].partition_broadcast(128), ds_l1, (), (t_c3,))
        k.act(A_bc[:], A_bc[:], AF.Exp, (t_c3,), (t_c3,))
        k.ts(A_bc[:], A_bc[:], -1.0, None, ALU.mult, None, (t_c3,), (t_c3,))
        cw, cb = f32t("cw", [128, 4, 48]), f32t("cb", [128, 48])
        for k4 in range(4):
            k.dma(cw[:, k4, :], conv_w[k4:k4 + 1, :].rearrange("o (t p) -> p (o t)", p=128), ds_l1, (), (t_c3,), allow_slow_non_contiguous=True)
        k.dma(cb[:], conv_b.rearrange("o (t p) -> p (o t)", p=128), ds_l1, (), (t_c3,), allow_slow_non_contiguous=True)
        convst = f32t("convst", [128, 48, 3])
        t_convst = Tok()
        NTT = NT + 1
        dtv, nacs, atot, av = f32t("dtv", [128, NTT, 64]), f32t("nacs", [128, NT, 64]), f32t("atot", [128, NT, 64]), f32t("av", [128, NT, 64])
        t_sc = Tok()
        wdt, twdt = wload(w_in_odd[:, 10240:10304], KC, 64)
        for i in range(NTT):
            np_ = 128 if i < NT else NS
            c0 = i * 128
            bk, tbk = bank[i % 2], t_bank[i % 2]
            for kc in range(KC):
                k.mm(bk[:np_, 0:64], hT[:, kc, c0:c0 + np_], wdt[:, kc, :], kc == 0, kc == KC - 1, (t_hT, twdt), (tbk,))
            k.tt(dtv[:np_, i, :], bk[:np_, 0:64], dtb_bc[:np_, :], ALU.add, (tbk, t_c3), (t_sc,))
        dflat = dtv[:].rearrange("p a b -> p (a b)")
        k.act(dflat, dflat, AF.Exp, (t_sc,), (t_sc,))
        k.act(dflat, dflat, AF.Ln, (t_sc,), (t_sc,), bias=1.0)
        k.tt(av[:], dtv[:, 0:NT, :], A_bc[:, :].unsqueeze(1).broadcast_to([128, NT, 64]), ALU.mult, (t_sc, t_c3), (t_sc,))
        for c in range(NT):
            bk, tbk = bank[c % 2], t_bank[c % 2]
            k.mm(bk[:, 0:64], Uf[:, :], av[:, c, :], True, True, (t_c3, t_sc), (tbk,))
            k.mm(bk[:, 64:128], onesf[:, :], av[:, c, :], True, True, (t_c3, t_sc), (tbk,))
            k.ts(nacs[:, c, :], bk[:, 0:64], -1.0, None, ALU.mult, None, (tbk,), (t_sc,))
            k.cp(atot[:, c, :], bk[:, 64:128], (tbk,), (t_sc,), eng="act")
        xbcS, xcS, szS = f32t("xbcS", [128, 48, NS]), f32t("xcS", [128, 48, NS]), f32t("szS", [128, 32, NS])
        t_xS = Tok()
        cbuf0 = f32t("cbuf0", [NS, 3, 128])
        cbuf = [cbuf0, cbuf0]
        t_cbuf0 = Tok()
        t_cbuf = [t_cbuf0, t_cbuf0]
        ds_cbuf0 = P.dsem("cbuf0")
        ds_cbuf = [ds_cbuf0, ds_cbuf0]
        cacc = f32t("cacc", [128, NS])
        t_cacc = Tok()
        l1s_base = k.ptr
        xcT = bft("xcT", [128, 6, T])
        t_xcT = Tok()
        xpre0 = bft("xpre0", [128, 3 + T])
        xpre = [xpre0, xpre0]
        t_xp0 = Tok()
        t_xpre = [t_xp0, t_xp0]
        k.memset(xpre0[:, 0:3], 0.0, (t_xp0,))
        dg = [bft("dg%d" % i, [128, 4, 128]) for i in range(2)]
        t_dg = [Tok(), Tok()]
        sz = bft("sz", [128, NT, 512])
        t_sz = Tok()
        acsT = f32t("acsT", [8, 512])
        acsh, acsl = bft("acsh", [8, NT, 128]), bft("acsl", [8, NT, 128])
        t_acsT = Tok()
        nwg = f32t("nwg", [128, 512])
        t_nwg = Tok()
        ds_nwg = P.dsem("nwg")
        tok0 = bft("tok0", [128, 5, 128])
        tok = [tok0, tok0]
        t_tok0 = Tok()
        t_tok = [t_tok0, t_tok0]
        Lt = [f32t("Lt%d" % i, [128, 128]) for i in range(2)]
        t_Lt = [Tok(), Tok()]
        Mt0 = bft("Mt0", [128, 8, 128])
        Mt = [Mt0, Mt0]
        t_Mt0 = Tok()
        t_Mt = [t_Mt0, t_Mt0]
        xdt, xdd, xsD = bft("xdt", [128, 8, 64]), bft("xdd", [128, 8, 64]), bft("xsD", [128, 8, 64])
        t_xd = Tok()
        eac, decc, etc_ = f32t("eac", [128, 8]), f32t("decc", [128, 8]), f32t("etc", [128, 8])
        t_ec = Tok()
        yv, ygt = f32t("yv", [128, 512]), f32t("ygt", [128, 512])
        t_yv = Tok()
        ssn = f32t("ssn", [128, 2])
        ynb0 = bft("ynb0", [128, 512])
        ynb = [ynb0, ynb0]
        t_ynb0 = Tok()
        t_ynb = [t_ynb0, t_ynb0]
        ynst0 = bft("ynst0", [128, 4, 128])
        ynst = [ynst0, ynst0]
        t_ynst0 = Tok()
        t_ynst = [t_ynst0, t_ynst0]
        ds_ynst0 = P.dsem("ynst0")
        ds_ynst = [ds_ynst0, ds_ynst0]
        Hs, Hb = f32t("Hs", [128, 512]), bft("Hb", [128, 512])
        t_H = Tok()
        hout = yv[:, :].rearrange("p (a b) -> p a b", a=4)
        t_hout = t_yv
        ds_hout = P.dsem("hout")
        junk2 = bft("junk2", [128, 512])
        t_junk2 = Tok()
        pstb = pst[0]
        for g in (dbg["l1only"] if "l1only" in dbg else range(int(dbg.get("l1groups", 8)))):
            k.dma(nwg[:], ssd_nw[0:1, 512 * g:512 * (g + 1)].partition_broadcast(128), ds_nwg, (), (t_nwg,))
            tiles = [(4096 + 512 * g + 128 * j, 4 * g + j) for j in range(4)] + [(8192 + 128 * g, 32 + g), (9216 + 128 * g, 40 + g)]
            slabs = {}
            for ti, (col, tidx) in enumerate(tiles):
                if ti in (0, 2):
                    sl_, tsl_ = wload(w_in_odd[:, col:col + 256], KC, 256)
                    slabs[ti] = (sl_, tsl_, 0)
                    slabs[ti + 1] = (sl_, tsl_, 128)
                elif ti >= 4:
                    sl_, tsl_ = wload(w_in_odd[:, col:col + 128], KC, 128)
                    slabs[ti] = (sl_, tsl_, 0)
                wv, tw, off = slabs[ti]
                xs_ = ti % 2
                xp_ = xpre[xs_]
                for k4 in range(4):
                    k.ts(dg[xs_][:, k4, :], ident[:, :], cw[:, k4, tidx:tidx + 1], None, ALU.mult, None, (t_const, t_c3), (t_dg[xs_],))
                for b in range(4):
                    bk, tbk = bank[b % 2], t_bank[b % 2]
                    for kc in range(KC):
                        k.mm(bk[:, :], wv[:, kc, off:off + 128], hT[:, kc, b * 512:(b + 1) * 512], kc == 0, kc == KC - 1, (t_hT, tw), (tbk,))
                    k.cp(xp_[:, 3 + b * 512:3 + (b + 1) * 512], bk[:, :], (tbk,), (t_xpre[xs_],), eng="act")
                    if b == 3:
                        k.cp(convst[:, tidx, :], bk[:, 509:512], (tbk,), (t_convst,))
                for b in range(4):
                    bk, tbk = bank[2 + b % 2], t_bank[2 + b % 2]
                    for k4 in range(4):
                        k.mm(bk[:, :], dg[xs_][:, k4, :], xp_[:, b * 512 + k4:b * 512 + k4 + 512], k4 == 0, k4 == 3, (t_dg[xs_], t_xpre[xs_]), (tbk,))
                    k.act(xcT[:, ti, b * 512:(b + 1) * 512], bk[:, :], AF.Silu, (tbk, t_c3), (t_xcT,), bias=cb[:, tidx:tidx + 1])
                bk, tbk = bank[5], t_bank[5]
                for kc in range(KC):
                    k.mm(bk[:, 0:NS], wv[:, kc, off:off + 128], hT[:, kc, T:TS], kc == 0, kc == KC - 1, (t_hT, tw), (tbk,))
                k.cp(xbcS[:, tidx, :], bk[:, 0:NS], (tbk,), (t_xS,))
                c2 = tidx % 2
                k.dma(cbuf[c2][:, :, :], st_conv[:, :, tidx * 128:(tidx + 1) * 128], ds_cbuf[c2], (), (t_cbuf[c2],))
                for k3 in range(3):
                    k.tr(bk[:, 64 + k3 * NS:64 + (k3 + 1) * NS], cbuf[c2][:, k3, :], ident_f[0:NS, 0:NS], (t_cbuf[c2], t_c3), (tbk,))
                k.ts(cacc[:, :], bk[:, 64:64 + NS], cw[:, 0, tidx:tidx + 1], None, ALU.mult, None, (tbk, t_c3), (t_cacc,))
                for k3 in (1, 2):
                    k.stt(cacc[:, :], bk[:, 64 + k3 * NS:64 + (k3 + 1) * NS], cw[:, k3, tidx:tidx + 1], cacc[:, :], ALU.mult, ALU.add,
                          (tbk, t_c3, t_cacc), (t_cacc,))
                k.stt(cacc[:, :], xbcS[:, tidx, :], cw[:, 3, tidx:tidx + 1], cacc[:, :], ALU.mult, ALU.add, (t_xS, t_c3, t_cacc), (t_cacc,))
                k.act(xcS[:, tidx, :], cacc[:, :], AF.Silu, (t_cacc, t_c3), (t_xS,), bias=cb[:, tidx:tidx + 1])
            for zh in range(2):
                wz, twz = wload(w_in_odd[:, 512 * g + 256 * zh:512 * g + 256 * (zh + 1)], KC, 256)
                for c in range(NT):
                    bk, tbk = bank[c % 2], t_bank[c % 2]
                    for kc in range(KC):
                        k.mm(bk[:, 0:256], hT[:, kc, c * 128:(c + 1) * 128], wz[:, kc, :], kc == 0, kc == KC - 1, (t_hT, twz), (tbk,))
                    k.act(sz[:, c, 256 * zh:256 * (zh + 1)], bk[:, 0:256], AF.Silu, (tbk,), (t_sz,))
                for jj in range(2):
                    bk, tbk = bank[5], t_bank[5]
                    for kc in range(KC):
                        k.mm(bk[:, 0:NS], wz[:, kc, jj * 128:(jj + 1) * 128], hT[:, kc, T:TS], kc == 0, kc == KC - 1, (t_hT, twz), (tbk,))
                    k.act(szS[:, 4 * g + 2 * zh + jj, :], bk[:, 0:NS], AF.Silu, (tbk,), (t_xS,))
            for c4 in range(0, NT, 4):
                bk, tbk = bank[4], t_bank[4]
                for cl in range(4):
                    k.mm(bk[0:8, cl * 128:(cl + 1) * 128], av[:, c4 + cl, 8 * g:8 * g + 8], Uf[:, :], True, True, (t_sc, t_c3), (tbk,))
                hi = acsh[:, c4:c4 + 4, :].rearrange("p a b -> p (a b)")
                lo = acsl[:, c4:c4 + 4, :].rearrange("p a b -> p (a b)")
                k.cp(hi, bk[0:8, :], (tbk,), (t_acsT,))
                k.cp(acsT[:, :], hi, (t_acsT,), (t_acsT,))
                k.tt(acsT[:, :], bk[0:8, :], acsT[:, :], ALU.subtract, (tbk, t_acsT), (t_acsT,))
                k.cp(lo, acsT[:, :], (t_acsT,), (t_acsT,))
            for c in range(NT if g < int(dbg.get("l1p2", 8)) else 0):
                cs_ = c % 2
                cc = slice(c * 128, (c + 1) * 128)
                hs8 = slice(8 * g, 8 * g + 8)
                for j in range(5):
                    k.tr(pstb[:, j, :], xcT[:, j, cc], ident[:, :], (t_xcT, t_const), (t_pst[0],))
                k.cp(tok[cs_][:, :, :], pstb[:, 0:5, :], (t_pst[0],), (t_tok[cs_],), eng="act")
                xs_tok = tok[cs_][:, 0:4, :].rearrange("p a (h q) -> p (a h) q", q=64)
                k.act(eac[:, :], nacs[:, c, hs8], AF.Exp, (t_sc,), (t_ec,), scale=-1.0)
                k.tt(decc[:, :], atot[:, c, hs8], nacs[:, c, hs8], ALU.add, (t_sc,), (t_ec,))
                k.act(decc[:, :], decc[:, :], AF.Exp, (t_ec,), (t_ec,))
                k.act(etc_[:, :], atot[:, c, hs8], AF.Exp, (t_sc,), (t_ec,))
                bc8 = lambda t: t[:, :].unsqueeze(2).broadcast_to([128, 8, 64])
                k.tt(xdt[:], xs_tok, dtv[:, c, hs8].unsqueeze(2).broadcast_to([128, 8, 64]), ALU.mult, (t_tok[cs_], t_sc), (t_xd,), eng=POOLC)
                k.tt(xdd[:], xdt[:], bc8(decc), ALU.mult, (t_xd, t_ec), (t_xd,), eng=POOLC)
                k.tt(xsD[:], xs_tok, D_bc[:, hs8].unsqueeze(2).broadcast_to([128, 8, 64]), ALU.mult, (t_tok[cs_], t_c3), (t_xd,), eng=POOLC)
                bG, tG = bank[4], t_bank[4]
                k.mm(bG[:, 0:128], xcT[:, 4, cc], xcT[:, 5, cc], True, True, (t_xcT,), (tG,))
                for j in range(8):
                    bR, tR = bank[2 + j % 2], t_bank[2 + j % 2]
                    k.mm(bR[:, 0:128], esel[0:8, j, :], acsh[:, c, :], True, False, (t_c3, t_acsT), (tR,))
                    k.mm(bR[:, 0:128], esel[0:8, j, :], acsl[:, c, :], False, False, (t_c3, t_acsT), (tR,))
                    k.mm(bR[:, 0:128], ident[:, :], cm1[:, :], False, True, (t_const, t_c3), (tR,))
                    k.act(Lt[j % 2][:, :], bR[:, 0:128], AF.Exp, (tR, t_sc), (t_Lt[j % 2],), bias=nacs[:, c, 8 * g + j:8 * g + j + 1])
                    k.tt(Mt[cs_][:, j, :], bG[:, 0:128], Lt[j % 2][:, :], ALU.mult, (tG, t_Lt[j % 2]), (t_Mt[cs_],))
                bY, tY = bank[0], t_bank[0]
                for j in range(8):
                    js = slice(64 * j, 64 * (j + 1))
                    k.mm(bY[:, js], Mt[cs_][:, j, :], xdt[:, j, :], True, False, (t_Mt[cs_], t_xd), (tY,))
                    k.mm(bY[:, js], ident[:, :], xsD[:, j, :], False, True, (t_const, t_xd), (tY,))
                if c > 0:
                    bO, tO = bank[1], t_bank[1]
                    k.mm(bO[:, :], xcT[:, 5, cc], Hb[:, :], True, True, (t_xcT, t_H), (tO,))
                    k.tt(yv[:, :].rearrange("p (h q) -> p h q", q=64), bO[:, :].rearrange("p (h q) -> p h q", q=64), bc8(eac), ALU.mult,
                         (tO, t_ec), (t_yv,))
                    k.tt(yv[:, :], yv[:, :], bY[:, :], ALU.add, (t_yv, tY), (t_yv,))
                    k.tt(ygt[:, :], yv[:, :], sz[:, c, :], ALU.mult, (t_yv, t_sz), (t_yv,))
                else:
                    k.tt(ygt[:, :], bY[:, :], sz[:, c, :], ALU.mult, (tY, t_sz), (t_yv,))
                k.act(junk2[:, :], ygt[:, :], AF.Square, (t_yv,), (t_junk2, t_yv), accum=ssn[:, 0:1])
                k.ts(ssn[:, 1:2], ssn[:, 0:1], 1.0 / 512, EPS, ALU.mult, ALU.add, (t_yv,), (t_yv,))
                k.act(ssn[:, 1:2], ssn[:, 1:2], AF.Ln, (t_yv,), (t_yv,))
                k.act(ssn[:, 1:2], ssn[:, 1:2], AF.Exp, (t_yv,), (t_yv,), scale=-0.5)
                k.stt(ynb[cs_][:, :], ygt[:, :], ssn[:, 1:2], nwg[:, :], ALU.mult, ALU.mult, (t_yv, t_nwg), (t_ynb[cs_],))
                for j in range(4):
                    k.tr(pst[1][:, j, :], ynb[cs_][:, j * 128:(j + 1) * 128], ident[:, :], (t_ynb[cs_], t_const), (t_pst[1],))
                k.cp(ynst[cs_][:, :, :], pst[1][:, 0:4, :], (t_pst[1],), (t_ynst[cs_],), eng="act")
                k.dma(ynT_d[4 * g:4 * g + 4, :, cc].rearrange("a p t -> p a t"), ynst[cs_][:, :, :], ds_ynst[cs_], (t_ynst[cs_],), (t_ynTd,))
                bS, tS = bank[5], t_bank[5]
                k.mm(bS[:, :], tok[cs_][:, 4, :], xdd[:].rearrange("p h q -> p (h q)"), True, True, (t_tok[cs_], t_xd), (tS,))
                if c > 0:
                    k.tt(Hs[:, :].rearrange("p (h q) -> p h q", q=64), Hs[:, :].rearrange("p (h q) -> p h q", q=64), bc8(etc_), ALU.mult,
                         (t_H, t_ec), (t_H,), eng=POOLC)
                    k.tt(Hs[:, :], Hs[:, :], bS[:, :], ALU.add, (t_H, tS), (t_H,))
                else:
                    k.cp(Hs[:, :], bS[:, :], (tS,), (t_H,))
                if c < NT - 1:
                    k.cp(Hb[:, :], Hs[:, :], (t_H,), (t_H,), eng="act")
            if g >= int(dbg.get("l1p2", 8)):
                continue
            bT, tT = bank[4], t_bank[4]
            for j in range(4):
                k.tr(bT[:, j * 128:(j + 1) * 128], Hs[:, j * 128:(j + 1) * 128], ident_f[:, :], (t_H, t_c3), (tT,))
            k.cp(yv[:, :], bT[:, :], (tT,), (t_hout,), eng="act")
            k.dma(ssd_p[512 * g:512 * (g + 1), :].rearrange("(a p) n -> p a n", p=128), hout, ds_hout, (t_hout,), ())
        for k3 in range(3):
            k.dma(conv_p[k3:k3 + 1, :].rearrange("o (t p) -> p (o t)", p=128), convst[:, :, k3], ds_hout, (t_convst,), (), allow_slow_non_contiguous=True)

        k.barrier()
        k.ptr = l1s_base
        ds_ss = P.dsem("ss_ld")
        t_pl = Tok()
        D_pl, nw_pl = f32t("D_pl", [128, 32]), f32t("nw_pl", [128, 32])
        for h2 in range(2):
            hs = slice(h2 * 64, (h2 + 1) * 64)
            k.dma(D_pl[hs, :], ssd_d.rearrange("o (hp hh) -> hh o hp", hh=2)[h2].partition_broadcast(64), ds_ss, (), (t_pl,),
                  allow_slow_non_contiguous=True)
        k.dma(nw_pl[:, :], ssd_nw.rearrange("o (t p) -> p (o t)", p=128), ds_ss, (), (t_pl,), allow_slow_non_contiguous=True)
        k.dma(conv_s[:, 0:2, :], st_conv[:, 1:3, :], ds_ss, (), ())
        cst_ = [f32t("cst_%d" % i, [NS, 512]) for i in range(2)]
        t_cst = [Tok(), Tok()]
        ds_cst = [P.dsem("cst0"), P.dsem("cst1")]
        for r4 in range(12):
            bk, tbk = bank[r4 % 2], t_bank[r4 % 2]
            for j in range(4):
                k.tr(bk[0:NS, j * 128:(j + 1) * 128], xbcS[:, 4 * r4 + j, :], ident_f[:, :], (t_xS, t_c3), (tbk,))
            k.cp(cst_[r4 % 2][:, :], bk[0:NS, :], (tbk,), (t_cst[r4 % 2],), eng="act")
            k.dma(conv_s[:, 2, r4 * 512:(r4 + 1) * 512], cst_[r4 % 2][:, :], ds_cst[r4 % 2], (t_cst[r4 % 2],), ())
        a_s, dec_s = f32t("a_s", [NS, 64]), f32t("dec_s", [NS, 64])
        dexp = f32t("dexp", [NS, 32, NS])
        dt_pl, dec_pl, xdt_pl = f32t("dt_pl", [128, 32, NS]), f32t("dec_pl", [128, 32, NS]), f32t("xdt_pl", [128, 32, NS])
        k.tt(a_s[:, :], dtv[:NS, NT, :], A_bc[:NS, :], ALU.mult, (t_sc, t_c3), (t_pl,))
        k.act(dec_s[:, :], a_s[:, :], AF.Exp, (t_pl,), (t_pl,))
        id16 = ident_f[0:NS, 0:NS].unsqueeze(1).broadcast_to([NS, 32, NS])
        for src, dst, rt in ((dtv[:NS, NT, :], dt_pl, (t_sc,)), (dec_s[:, :], dec_pl, (t_pl,))):
            bk, tbk = bank[2], t_bank[2]
            for h2 in range(2):
                k.tt(dexp[:], src.rearrange("p (hp hh) -> p hp hh", hh=2)[:, :, h2:h2 + 1].broadcast_to([NS, 32, NS]), id16, ALU.mult,
                     rt + (t_c3,), (t_pl,))
                k.mm(bk[h2 * 64:(h2 + 1) * 64, :], onesf[0:NS, 0:64], dexp[:].rearrange("p a b -> p (a b)"), True, True, (t_c3, t_pl), (tbk,),
                     tp=(0, 64 * h2))
            k.cp(dst[:].rearrange("p a b -> p (a b)"), bk[:, :], (tbk,), (t_pl,))
        k.tt(xdt_pl[:], xcS[:, 0:32, :], dt_pl[:], ALU.mult, (t_xS, t_pl), (t_pl,))
        hst0 = f32t("hst0", [128, 32, 128])
        hst = [hst0, hst0]
        t_hst0 = Tok()
        t_hst = [t_hst0, t_hst0]
        ds_hst0 = P.dsem("hst0")
        ds_hst = [ds_hst0, ds_hst0]
        tmpS = f32t("tmpS", [128, 32, 128])
        t_tmpS = Tok()
        dB = f32t("dB", [128, 8, 128])
        t_dB = Tok()
        Bbc, Cbc = f32t("Bbc", [128, 8, 128]), f32t("Cbc", [128, 8, 128])
        t_bc = Tok()
        y_pl = f32t("y_pl", [128, 32, NS])
        t_ypl = Tok()
        idb = ident_f[:, :].unsqueeze(1).broadcast_to([128, 8, 128])
        for s_ in range(NS):
            hs_ = s_ % 2
            H = hst[hs_]
            k.dma(H[:, :, :], st_ssd[s_].rearrange("(hp q) n -> q hp n", q=128), ds_hst[hs_], (), (t_hst[hs_],))
            for which, dstbc in ((32, Bbc), (40, Cbc)):
                k.tt(dB[:], xcS[:, which:which + 8, s_:s_ + 1].broadcast_to([128, 8, 128]), idb, ALU.mult, (t_xS, t_c3), (t_dB,))
                for hf in range(2):
                    bk, tbk = bank[3 + hf], t_bank[3 + hf]
                    k.mm(bk[:, :], onesf[:, :], dB[:, 4 * hf:4 * hf + 4, :].rearrange("p a b -> p (a b)"), True, True, (t_c3, t_dB), (tbk,))
                    k.cp(dstbc[:, 4 * hf:4 * hf + 4, :].rearrange("p a b -> p (a b)"), bk[:, :], (tbk,), (t_bc,), eng="act")
            v4 = lambda t: t[:].rearrange("p (g a) n -> p g a n", a=4)
            k.tt(v4(tmpS), Bbc[:].unsqueeze(2).broadcast_to([128, 8, 4, 128]),
                 xdt_pl[:, :, s_:s_ + 1].rearrange("p (g a) o -> p g a o", a=4).broadcast_to([128, 8, 4, 128]), ALU.mult, (t_bc, t_pl), (t_tmpS,))
            k.tt(H[:], H[:], dec_pl[:, :, s_:s_ + 1].broadcast_to([128, 32, 128]), ALU.mult, (t_hst[hs_], t_pl), (t_hst[hs_],))
            k.tt(H[:], H[:], tmpS[:], ALU.add, (t_hst[hs_], t_tmpS), (t_hst[hs_],))
            k.dma(ssd_s[s_].rearrange("(hp q) n -> q hp n", q=128), H[:, :, :], ds_hst[hs_], (t_hst[hs_],), ())
            k.tt(v4(tmpS), v4(H), Cbc[:].unsqueeze(2).broadcast_to([128, 8, 4, 128]), ALU.mult, (t_hst[hs_], t_bc), (t_tmpS,))
            k.red(y_pl[:, :, s_], tmpS[:], ALU.add, (t_tmpS,), (t_ypl,))
        ygS, sqS = f32t("ygS", [128, 32, NS]), f32t("sqS", [128, 32, NS])
        ssS, rsS = f32t("ssS", [128, 8, NS]), f32t("rsS", [128, 8, NS])
        k.tt(ygS[:], xcS[:, 0:32, :], D_pl[:, :].unsqueeze(2).broadcast_to([128, 32, NS]), ALU.mult, (t_xS, t_pl), (t_ypl,))
        k.tt(ygS[:], ygS[:], y_pl[:], ALU.add, (t_ypl,), (t_ypl,))
        k.tt(ygS[:], ygS[:], szS[:], ALU.mult, (t_ypl, t_xS), (t_ypl,))
        k.tt(sqS[:], ygS[:], ygS[:], ALU.mult, (t_ypl,), (t_ypl,))
        bk, tbk = bank[0], t_bank[0]
        k.mm(bk[:, :], onesf[:, :], sqS[:].rearrange("p a b -> p (a b)"), True, True, (t_c3, t_ypl), (tbk,))
        k.red(ssS[:], bk[:, :].rearrange("p (g a s) -> p g s a", g=8, a=4), ALU.add, (tbk,), (t_ypl,))
        k.ts(ssS[:].rearrange("p a b -> p (a b)"), ssS[:].rearrange("p a b -> p (a b)"), 1.0 / 512, EPS, ALU.mult, ALU.add, (t_ypl,), (t_ypl,))
        k.act(rsS[:].rearrange("p a b -> p (a b)"), ssS[:].rearrange("p a b -> p (a b)"), AF.Ln, (t_ypl,), (t_ypl,))
        k.act(rsS[:].rearrange("p a b -> p (a b)"), rsS[:].rearrange("p a b -> p (a b)"), AF.Exp, (t_ypl,), (t_ypl,), scale=-0.5)
        k.tt(ygS[:].rearrange("p (g a) s -> p g a s", a=4), ygS[:].rearrange("p (g a) s -> p g a s", a=4),
             rsS[:].unsqueeze(2).broadcast_to([128, 8, 4, NS]), ALU.mult, (t_ypl,), (t_ypl,))
        k.tt(ynSd[:], ygS[:], nw_pl[:, :].unsqueeze(2).broadcast_to([128, 32, NS]), ALU.mult, (t_ypl, t_pl), (t_ynSd,))

    if dbg.get("l1", 1) and dbg.get("l0d", 1) and dbg.get("s5", 1):
        l1()
        outproj(w_out_odd, 32, ynT_d, ynSd, t_ynTd, t_ynSd, x1_d[0:T, :], x1_d[T:TS, :], (t_x1d,), y_p, y_s, Tok(), "od1")

    if "catT" in dbg:
        o = dout("dbg_catT", [16, 128, T], BF16)
        dsd2 = P.dsem("dbg2")
        k.dma(o, catT, dsd2, (t_catT,), ())
    if "hT" in dbg:
        o = dout("dbg_hT", [128, KC * TS], BF16)
        ds = P.dsem("dbg")
        k.dma(o, hT[:].rearrange("p a b -> p (a b)"), ds, (t_hT,), ())

    with ExitStack() as es:
        P.emit(es)
    return nc


def _core_inputs(c, a, cst, rope):
    s0, s1 = NS * c, NS * (c + 1)
    f = np.ascontiguousarray
    m = {
        "xp": f(a["x_prompt"][c]), "xs": f(a["x_sample"][s0:s1, 0]), "norm_w": f(a["norm_w"]),
        "w_in_even": f(a["w_in_even"][0]), "q_norm_w": f(a["q_norm_w"]), "k_norm_w": f(a["k_norm_w"]),
        "cst": cst, "rope": rope,
        "lam_re": f(a["s5_lambda_re"][0]), "lam_im": f(a["s5_lambda_im"][0]), "log_dt": f(a["s5_log_dt"]),
        "s5_b": f(a["s5_b"][0]).reshape(64, 64, 32), "s5_c": f(a["s5_c"][0]).reshape(64, 16, 128), "s5_d": f(a["s5_d"]),
        "glu_w": f(a["s5_glu_w"][0]), "glu_b": f(a["s5_glu_b"]), "st_s5": f(a["state_s5"][0, s0:s1]).reshape(NS, 8192),
        "w_out_even": f(a["w_out_even"][0]), "w_in_odd": f(a["w_in_odd"][0]), "conv_w": f(a["conv_w"][0]),
        "conv_b": f(a["conv_b"]), "dt_bias": f(a["ssd_dt_bias"]), "a_log": f(a["ssd_a_log"]), "ssd_d": f(a["ssd_d"]),
        "ssd_nw": f(a["ssd_norm_w"]), "w_out_odd": f(a["w_out_odd"][0]),
        "cache_k": a["cache_k"][0].reshape(2560 * 128, 1024), "cache_v": a["cache_v"][0].reshape(2560 * 128, 1024),
        "page_table": f(a["page_table"][s0:s1]).reshape(1, NS * 16).astype(np.int32),
        "st_conv": f(a["state_conv"][0, s0:s1]), "st_ssd": f(a["state_ssd"][0, s0:s1]).reshape(NS, 4096, 128),
    }
    return m


def kernel(**inputs):
    a = {k_: np.asarray(v) for k_, v in inputs.items()}
    nc = build()
    cst, rope = host_consts(), host_rope()
    in_maps = [_core_inputs(c, a, cst, rope) for c in range(NCORES)]
    names = set()
    for alloc in nc.allocations:
        try:
            if alloc.kind == "ExternalInput":
                names.add(alloc.memorylocations[0].name)
        except Exception:
            pass
    if names:
        in_maps = [{k_: v for k_, v in m.items() if k_ in names} for m in in_maps]
    res = run_bass_kernel_spmd(nc, in_maps, core_ids=list(range(NCORES))).results
    B = NCORES
    g = lambda name: [np.asarray(r[name]) for r in res]
    z = lambda *shape: np.zeros(shape, np.float32)
    y_prompt = np.stack(g("y_p"), 0)
    y_sample = np.concatenate(g("y_s"), 0).reshape(B * NS, 1, D)
    k_prompt = np.stack(g("k_p"), 0).reshape(1, B, T, 8, 128)
    v_prompt = np.stack(g("v_p"), 0).reshape(1, B, T, 8, 128)
    k_sample = np.concatenate(g("k_s"), 0).reshape(1, B * NS, 1, 8, 128)
    v_sample = np.concatenate(g("v_s"), 0).reshape(1, B * NS, 1, 8, 128)
    s5_prompt = np.stack(g("s5_p"), 0).reshape(1, B, 64, 64, 2)
    s5_sample = np.concatenate(g("s5_s"), 0).reshape(1, B * NS, 64, 64, 2)
    conv_prompt = np.stack(g("conv_p"), 0).reshape(1, B, 3, 6144)
    ssd_prompt = np.stack(g("ssd_p"), 0).reshape(1, B, 64, 64, 128)
    if "conv_s" in res[0]:
        conv_sample = np.concatenate(g("conv_s"), 0).reshape(1, B * NS, 3, 6144)
        ssd_sample = np.concatenate(g("ssd_s"), 0).reshape(1, B * NS, 64, 64, 128)
    else:
        conv_sample, ssd_sample = z(1, B * NS, 3, 6144), z(1, B * NS, 64, 64, 128)
    return (y_prompt, y_sample, k_prompt, v_prompt, k_sample, v_sample, s5_prompt, s5_sample,
            conv_prompt, conv_sample, ssd_prompt, ssd_sample)
```

```python
import numpy as np
from contextlib import ExitStack
import concourse.bass as bass
import concourse.mybir as mybir
from concourse.bass_utils import run_bass_kernel_spmd

F32 = mybir.dt.float32
BF16 = mybir.dt.bfloat16
I32 = mybir.dt.int32
AF = mybir.ActivationFunctionType
ALU = mybir.AluOpType
AX = mybir.AxisListType

NCORES = 8
T = 2048
NT = T // 128
NS = 16
TS = T + NS
D = 2048
KC = D // 128
EPS = 1e-6
NEG = -30000.0
ENGS = ("pe", "act", "dve", "pool", "sp")
EPOCH = 700
POOLC = "dve"


class Tok:
    __slots__ = ("w", "r")
    registry = []

    def __init__(self):
        self.w = None
        self.r = {}
        Tok.registry.append(self)


class DSem:
    __slots__ = ("h", "cnt", "name")

    def __init__(self, name):
        self.name = name
        self.h = None
        self.cnt = 0


class Op:
    __slots__ = ("eng", "fn", "waits", "sig", "signo", "dsem", "idx")


class Prog:
    def __init__(self, nc):
        self.nc = nc
        self.ops = {e: [] for e in ENGS}
        self.seen = {e: {} for e in ENGS}
        self.dsems = []
        Tok.registry = []
        self.t_phase = Tok()

    def barrier(self, eng, fn):
        toks = list(Tok.registry)
        return self.add(eng, fn, (), toks)

    def dsem(self, name):
        d = DSem(name)
        self.dsems.append(d)
        return d

    def add(self, eng, fn, reads=(), writes=(), dsem=None):
        op = Op()
        op.eng = eng
        op.fn = fn
        op.sig = False
        op.signo = 0
        op.dsem = dsem
        op.idx = len(self.ops[eng])
        deps = []
        if self.t_phase.w is not None:
            deps.append(self.t_phase.w)
        for t in reads:
            if t.w is not None:
                deps.append(t.w)
        for t in writes:
            if t.w is not None:
                deps.append(t.w)
            deps.extend(t.r.values())
        seen = self.seen[eng]
        waits = {}
        for d in deps:
            if d[0] == "e":
                o2 = d[1]
                if o2.eng == "pe" and eng == "pe":
                    continue
                key = ("e", o2.eng)
                val = o2.idx
            else:
                key = ("d", d[1])
                val = d[2]
            if seen.get(key, -1) >= val:
                continue
            if key not in waits or waits[key][0] < val:
                waits[key] = (val, d)
        op.waits = []
        for key, (val, d) in waits.items():
            seen[key] = val
            if d[0] == "e":
                d[1].sig = True
            op.waits.append(d)
        if dsem is not None:
            dsem.cnt += 16
            me = ("d", dsem, dsem.cnt)
            mkey = ("d", dsem)
        else:
            me = ("e", op)
            mkey = ("e", eng)
        for t in reads:
            t.r[mkey] = me
        for t in writes:
            t.w = me
            t.r = {}
        self.ops[eng].append(op)
        return op

    def emit(self, es):
        nc = self.nc
        sems = {}
        for e in ENGS:
            n = 0
            for op in self.ops[e]:
                if op.sig:
                    n += 1
                    op.signo = n
            print("[prog] %s: ops=%d signals=%d waits=%d" % (e, len(self.ops[e]), n, sum(len(o.waits) for o in self.ops[e])))
            nep = max(1, (n + EPOCH - 1) // EPOCH)
            sems[e] = [es.enter_context(nc.semaphore("s_%s_%d" % (e, i))) for i in range(nep)]
        print("[prog] dsems=%d engine_sems=%d maxdcnt=%d" % (len(self.dsems), sum(len(v) for v in sems.values()), max(d.cnt for d in self.dsems)))
        for d in self.dsems:
            d.h = es.enter_context(nc.semaphore("d_" + d.name))
        prog = self

        def run(ename, eng):
            for op in prog.ops[ename]:
                for d in op.waits:
                    if d[0] == "e":
                        s = d[1].signo
                        ep = (s - 1) // EPOCH
                        eng.wait_ge(sems[d[1].eng][ep], s - ep * EPOCH)
                    else:
                        eng.wait_ge(d[1].h, d[2])
                ins = op.fn(eng)
                if op.dsem is not None:
                    ins.then_inc(op.dsem.h, 16)
                elif op.sig:
                    ep = (op.signo - 1) // EPOCH
                    ins.then_inc(sems[ename][ep], 1)
            if ename == "sp":
                for d in prog.dsems:
                    if d.cnt:
                        eng.wait_ge(d.h, d.cnt)

        with nc.Block() as block:
            @block.tensor
            def _(eng):
                run("pe", eng)

            @block.scalar
            def _(eng):
                run("act", eng)

            @block.vector
            def _(eng):
                run("dve", eng)

            @block.gpsimd
            def _(eng):
                run("pool", eng)

            @block.sync
            def _(eng):
                run("sp", eng)


class K:
    def __init__(self, nc):
        self.nc = nc
        self.P = Prog(nc)
        self.ptr = self.SB_BASE
        self.nalloc = 0
        self.fdummy = self.sb("fdummy", [128, 8], F32)
        self.t_fd = Tok()

    SB_BASE = 16512
    SB_TOP = 229344

    def sb(self, name, shape, dt):
        esz = {F32: 4, BF16: 2, I32: 4}[dt]
        n = 1
        for d in shape[1:]:
            n *= d
        nbytes = (n * esz + 31) // 32 * 32
        off = self.ptr
        assert off + nbytes <= self.SB_TOP, "SBUF overflow at %s: %d" % (name, off + nbytes)
        self.ptr = off + nbytes
        self.nalloc += 1
        return self.nc.alloc_sbuf_tensor_at("%s_%d" % (name, self.nalloc), list(shape), dt, offset=off)

    def barrier(self, eng="dve"):
        d = self.fdummy
        return self.P.barrier(eng, lambda e: e.memset(d[0:1, 0:1], 0.0))

    def fence(self, old, new, eng="dve"):
        d = self.fdummy
        return self.P.add(eng, lambda e: e.memset(d[0:1, 0:1], 0.0), (), tuple(old) + tuple(new) + (self.t_fd,))

    def ps(self, name, shape, dt=F32):
        return self.nc.alloc_psum_tensor(name, list(shape), dt)

    def mm(self, out, lhsT, rhs, start, stop, r, w, tp=None):
        if tp is not None:
            return self.P.add("pe", lambda e: e.matmul(out, lhsT, rhs, start=start, stop=stop, tile_position=tp), r, w)
        return self.P.add("pe", lambda e: e.matmul(out, lhsT, rhs, start=start, stop=stop), r, w)

    def tr(self, out, in_, ident, r, w):
        return self.P.add("pe", lambda e: e.transpose(out, in_, ident), r, w)

    def act(self, out, in_, func, r, w, bias=None, scale=None, accum=None, eng="act"):
        kw = {}
        if bias is not None:
            kw["bias"] = bias
        if scale is not None:
            kw["scale"] = scale
        if accum is not None:
            kw["accum_out"] = accum
        return self.P.add(eng, lambda e: e.activation(out=out, in_=in_, func=func, **kw), r, w)

    def tt(self, out, in0, in1, op, r, w, eng="dve"):
        return self.P.add(eng, lambda e: e.tensor_tensor(out=out, in0=in0, in1=in1, op=op), r, w)

    def ts(self, out, in0, s1, s2, op0, op1, r, w, eng="dve", accum=None):
        kw = {}
        if accum is not None:
            kw["accum_out"] = accum
        if op1 is None:
            return self.P.add(eng, lambda e: e.tensor_scalar(out=out, in0=in0, scalar1=s1, scalar2=None, op0=op0, **kw), r, w)
        return self.P.add(eng, lambda e: e.tensor_scalar(out=out, in0=in0, scalar1=s1, scalar2=s2, op0=op0, op1=op1, **kw), r, w)

    def stt(self, out, in0, scalar, in1, op0, op1, r, w, eng="dve"):
        return self.P.add(eng, lambda e: e.scalar_tensor_tensor(out=out, in0=in0, scalar=scalar, in1=in1, op0=op0, op1=op1), r, w)

    def cp(self, out, in_, r, w, eng="dve"):
        if eng == "act":
            return self.P.add(eng, lambda e: e.copy(out=out, in_=in_), r, w)
        return self.P.add(eng, lambda e: e.tensor_copy(out=out, in_=in_), r, w)

    def red(self, out, in_, op, r, w, eng="dve", axis=AX.X):
        return self.P.add(eng, lambda e: e.tensor_reduce(out=out, in_=in_, axis=axis, op=op), r, w)

    def memset(self, ap, val, w, eng="dve"):
        return self.P.add(eng, lambda e: e.memset(ap, val), (), w)

    def dma(self, out, in_, dsem, r, w, eng="sp", **kw):
        return self.P.add(eng, lambda e: e.dma_start(out=out, in_=in_, **kw), r, w, dsem=dsem)


C_ID = 0
C_CM = C_ID + 128
C_ES = C_CM + 4 * 512
C_ONE = C_ES + 16 * 128
C_BT = C_ONE + 128
C_MM = C_BT + 2048
C_OWN = C_MM + 8 * 16
C_EV = C_OWN + 8 * 16
C_NV = C_EV + 17
C_U = C_NV + 128
C_PI = C_U + 128
C_DG = C_PI + 1
C_END = C_DG + 128


def host_consts():
    c = np.zeros((128, C_END), np.float32)
    c[:, C_ID:C_ID + 128] = np.eye(128, dtype=np.float32)
    key = np.arange(128)[:, None]
    q = np.arange(512)[None, :]
    for j in range(4):
        c[:, C_CM + j * 512:C_CM + (j + 1) * 512] = np.where(128 * j + key <= q, 0.0, NEG)
    for r in range(16):
        c[r, C_ES + r * 128:C_ES + (r + 1) * 128] = 1.0
    c[:, C_ONE:C_ONE + 128] = 1.0
    own = np.arange(T) // 256
    for j in range(2):
        for n in range(8):
            c[j * 8 + n, C_BT:C_BT + T] = np.where(n <= own, 0.0, NEG)
    for m in range(8):
        for j in range(2):
            for n in range(8):
                c[:, C_MM + m * 16 + j * 8 + n] = 0.0 if n < m else -1e30
                c[:, C_OWN + m * 16 + j * 8 + n] = 1.0 if n == m else 0.0
    c[:, C_EV:C_EV + 17] = np.arange(17, dtype=np.float32)[None, :]
    c[:, C_NV:C_NV + 128] = np.arange(128, dtype=np.float32)[None, :]
    c[:, C_U:C_U + 128] = (np.arange(128)[:, None] <= np.arange(128)[None, :]).astype(np.float32)
    c[:, C_PI] = np.arange(128, dtype=np.float32)
    for hh in range(8):
        for j in range(16):
            c[hh, C_DG + j * 8 + hh] = 1.0
    return c


def host_rope():
    half = 64
    inv = (10000.0 ** (-np.arange(half, dtype=np.float32) / half)).astype(np.float32)
    pos = np.arange(T + 1, dtype=np.float32)
    ang = pos[:, None] * inv[None, :]
    cos = np.cos(ang).astype(np.float32)
    sin = np.sin(ang).astype(np.float32)
    cc = np.concatenate([cos, cos], axis=1)
    ss = np.concatenate([-sin, sin], axis=1)
    return np.ascontiguousarray(np.stack([cc, ss], axis=1))


def build(dbg=None):
    nc = bass.Bass("TRN2", target_bir_lowering=False)
    k = K(nc)
    P = k.P
    dbg = dbg or {}

    def din(name, shape, dt=F32):
        return nc.dram_tensor(name, list(shape), dt, kind="ExternalInput").ap()

    def dout(name, shape, dt=F32):
        return nc.dram_tensor(name, list(shape), dt, kind="ExternalOutput").ap()

    xp = din("xp", [T, D])
    xs = din("xs", [NS, D])
    norm_w = din("norm_w", [2, D])
    w_in_even = din("w_in_even", [D, 6144])
    q_norm_w = din("q_norm_w", [1, 128])
    k_norm_w = din("k_norm_w", [1, 128])
    cst = din("cst", [128, C_END])
    rope = din("rope", [T + 1, 2, 128])

    k_p = dout("k_p", [T, 1024])
    v_p = dout("v_p", [T, 1024])
    k_s = dout("k_s", [NS, 1024])
    v_s = dout("v_s", [NS, 1024])

    ident = k.sb("ident", [128, 128], BF16)
    ones_bf = k.sb("ones_bf", [128, 128], BF16)
    t_const = Tok()
    ds_c = P.dsem("const")
    k.dma(ident[:], cst[:, C_ID:C_ID + 128], ds_c, (), (t_const,), eng="pool")
    k.dma(ones_bf[:], cst[:, C_ONE:C_ONE + 128], ds_c, (), (t_const,), eng="pool")

    NSLOT = 4
    wslot = [k.sb("wslot%d" % i, [128, KC * 256], BF16) for i in range(NSLOT)]
    catS = k.sb("catS", [128, 16, NS], BF16)
    ynSd = k.sb("ynSd", [128, 32, NS], BF16)
    t_ynSd = Tok()
    hT_base = k.ptr
    hT = k.sb("hT", [128, KC, TS], BF16)
    t_hT = Tok()
    region = k.ptr
    nw_bc = k.sb("nw_bc", [128, D], F32)
    t_nw = Tok()
    k.dma(nw_bc[:], norm_w[0:1, :].partition_broadcast(128), ds_c, (), (t_nw,))
    xt = [k.sb("xt%d" % i, [128, D], F32) for i in range(2)]
    t_xt = [Tok(), Tok()]
    ds_x = [P.dsem("x0"), P.dsem("x1")]
    hb = [k.sb("hb%d" % i, [128, D], BF16) for i in range(2)]
    t_hb = [Tok(), Tok()]
    junk = k.sb("junk", [128, D], BF16)
    t_junk = Tok()
    ss = k.sb("ss", [128, NT + 1], F32)
    rstd = k.sb("rstd", [128, NT + 1], F32)
    t_ss = [Tok() for _ in range(NT + 1)]
    pst = [k.ps("pst%d" % i, [128, 8, 128], BF16) for i in range(2)]
    t_pst = [Tok(), Tok()]

    def l0a(norm_row, srcp, srcs, rtoks=()):
        if norm_row:
            k.dma(nw_bc[:], norm_w[norm_row:norm_row + 1, :].partition_broadcast(128), ds_c, (), (t_nw,))
        for i in range(NT + 1):
            s = i % 2
            np_ = 128 if i < NT else NS
            src = srcp[i * 128:(i + 1) * 128, :] if i < NT else srcs
            k.dma(xt[s][:np_, :], src, ds_x[s], tuple(rtoks), (t_xt[s],))
            k.act(junk[:np_, :], xt[s][:np_, :], AF.Square, (t_xt[s],), (t_junk, t_ss[i]), accum=ss[:np_, i:i + 1])
            k.ts(rstd[:np_, i:i + 1], ss[:np_, i:i + 1], 1.0 / D, EPS, ALU.mult, ALU.add, (t_ss[i],), (t_ss[i],))
            k.act(rstd[:np_, i:i + 1], rstd[:np_, i:i + 1], AF.Ln, (t_ss[i],), (t_ss[i],))
            k.act(rstd[:np_, i:i + 1], rstd[:np_, i:i + 1], AF.Exp, (t_ss[i],), (t_ss[i],), scale=-0.5)
            k.stt(hb[s][:np_, :], xt[s][:np_, :], rstd[:np_, i:i + 1], nw_bc[:np_, :], ALU.mult, ALU.mult,
                  (t_xt[s], t_ss[i], t_nw), (t_hb[s],))
            for half in range(2):
                for j in range(8):
                    kc = half * 8 + j
                    k.tr(pst[half][:, j, :np_], hb[s][:np_, kc * 128:(kc + 1) * 128], ident[:np_, :np_],
                         (t_hb[s], t_const), (t_pst[half],))
                c0 = i * 128
                k.cp(hT[:, half * 8:(half + 1) * 8, c0:c0 + np_], pst[half][:, :, :np_], (t_pst[half],), (t_hT,),
                     eng="act" if half == 0 else "dve")

    l0a(0, xp, xs[:, :])


    t_wslot = [Tok() for _ in range(NSLOT)]
    ds_w = [P.dsem("w%d" % i) for i in range(NSLOT)]
    wctr = [0]

    def wload(src, kc, width):
        sl = wctr[0] % NSLOT
        wctr[0] += 1
        view = wslot[sl][:, 0:kc * width].rearrange("p (a b) -> p a b", a=kc)
        k.dma(view, src.rearrange("(a p) c -> p a c", p=128), ds_w[sl], (), (t_wslot[sl],), eng="pool")
        return view, t_wslot[sl]

    bank = [k.ps("bank%d" % i, [128, 512], F32) for i in range(6)]
    t_bank = [Tok() for _ in range(6)]

    catT = nc.dram_tensor("catT", [16, 128, T], BF16).ap()
    t_catT = Tok()
    t_catS = Tok()

    l0a_toks = [t_nw, t_junk] + t_xt + t_hb
    k.ptr = region
    qs_f = k.sb("qs_f", [NS, 8, 128], F32)
    ks_f = k.sb("ks_f", [NS, 8, 128], F32)
    vs_f = k.sb("vs_f", [NS, 8, 128], F32)
    t_qkvs = Tok()
    sgaS = k.sb("sgaS", [128, 8, NS], BF16)
    t_sgaS = Tok()
    ds_ks = P.dsem("ks_out")
    sattn_base = k.ptr
    qkw4 = k.sb("qkw4", [128, 4, 128], F32)
    cm = k.sb("cm", [128, 4, 512], BF16)
    esel = k.sb("esel", [16, 16, 128], BF16)
    qkw = k.sb("qkw", [128, 2, 128], F32)
    biasT0 = k.sb("biasT0", [16, T], BF16)
    mmask = k.sb("mmask", [128, 8, 16], F32)
    ownm = k.sb("ownm", [128, 8, 16], F32)
    t_acst = Tok()
    t_qkw = Tok()
    ropet = [k.sb("ropet%d" % i, [128, 2, 128], F32) for i in range(2)]
    t_ropet = [Tok(), Tok()]
    ds_rope = [P.dsem("rope0"), P.dsem("rope1")]
    sq_, ss4_, tn_, ut_, vt_ = [], [], [], [], []
    tq_ = {"sq": [], "ss4": [], "tn": [], "ut": [], "vt": []}
    for i2 in range(2):
        sq_.append(k.sb("sq%d" % i2, [128, 512], F32))
        ss4_.append(k.sb("ss4%d" % i2, [128, 4], F32))
        tn_.append(k.sb("tn%d" % i2, [128, 4, 128], F32))
        ut_.append(k.sb("ut%d" % i2, [128, 4, 128], F32))
        vt_.append(k.sb("vt%d" % i2, [128, 4, 128], F32))
        for kk in tq_:
            tq_[kk].append(Tok())
    qkb = [k.sb("qkb%d" % i, [128, 4, 128], BF16) for i in range(2)]
    t_qkb = [Tok(), Tok()]
    kf = [k.sb("kf%d" % i, [128, 256], F32) for i in range(2)]
    t_kf = [Tok(), Tok()]
    ds_kf = [P.dsem("kf0"), P.dsem("kf1")]
    vf = [k.sb("vf%d" % i, [128, 256], F32) for i in range(2)]
    t_vf = [Tok(), Tok()]
    ds_vf = [P.dsem("vf0"), P.dsem("vf1")]
    qkT = k.sb("qkT", [128, 4, T], BF16)
    t_qkT = Tok()
    Vg = k.sb("Vg", [128, NT, 256], BF16)
    t_Vg = Tok()
    sga = k.sb("sga", [128, 2, T], BF16)
    t_sga = Tok()

    l0b_toks = tq_["sq"] + tq_["ss4"] + tq_["tn"] + tq_["ut"] + tq_["vt"] + [t_qkT, t_Vg, t_sga, t_qkvs, t_sgaS] + t_ropet + t_qkb + t_kf + t_vf
    t_qkw4 = Tok()
    l0b_toks.append(t_qkw4)

    def l0b_proj(g, wq, wk, wv, tq, tk, tv):
        for i in range(NT + 1):
            s = i % 2
            np_ = 128 if i < NT else NS
            c0 = i * 128
            bq = bank[s]
            bv = bank[2]
            sq, ss4, tn, ut, vt = sq_[s], ss4_[s], tn_[s], ut_[s], vt_[s]
            t_sq, t_ss4, t_tn, t_ut, t_vt = tq_["sq"][s], tq_["ss4"][s], tq_["tn"][s], tq_["ut"][s], tq_["vt"][s]
            if i < NT:
                k.dma(ropet[s][:, :, :], rope[c0:c0 + 128, :, :], ds_rope[s], (), (t_ropet[s],))
            else:
                k.dma(ropet[s][:NS, :, :], rope[T:T + 1, :, :].partition_broadcast(NS), ds_rope[s], (), (t_ropet[s],))
            for kc in range(KC):
                k.mm(bq[:np_, 0:256], hT[:, kc, c0:c0 + np_], wq[:, kc, :], kc == 0, kc == KC - 1, (t_hT, tq), (t_bank[s],))
            for kc in range(KC):
                k.mm(bq[:np_, 256:512], hT[:, kc, c0:c0 + np_], wk[:, kc, :], kc == 0, kc == KC - 1, (t_hT, tk), (t_bank[s],))
            for kc in range(KC):
                k.mm(bv[:np_, s * 256:(s + 1) * 256], hT[:, kc, c0:c0 + np_], wv[:, kc, :], kc == 0, kc == KC - 1, (t_hT, tv), (t_bank[2],))
            k.act(sq[:np_, :], bq[:np_, :], AF.Square, (t_bank[s],), (t_sq,))
            k.red(ss4[:np_, :], sq[:np_, :].rearrange("p (a b) -> p a b", a=4), ALU.add, (t_sq,), (t_ss4,))
            k.act(ss4[:np_, :], ss4[:np_, :], AF.Ln, (t_ss4,), (t_ss4,), bias=128.0 * EPS)
            k.act(ss4[:np_, :], ss4[:np_, :], AF.Exp, (t_ss4,), (t_ss4,), scale=-0.5)
            k.tt(tn[:np_], bq[:np_, :].rearrange("p (a b) -> p a b", a=4),
                 ss4[:np_, :].unsqueeze(2).broadcast_to([np_, 4, 128]), ALU.mult, (t_bank[s], t_ss4), (t_tn,))
            k.tt(tn[:np_], tn[:np_], qkw4[:np_], ALU.mult, (t_tn, t_qkw4), (t_tn,))
            cc = ropet[s][:np_, 0:1, :].broadcast_to([np_, 4, 128])
            k.tt(ut[:np_], tn[:np_], cc, ALU.mult, (t_tn, t_ropet[s]), (t_ut,))
            k.tt(vt[:np_, :, 0:64], tn[:np_, :, 64:128], ropet[s][:np_, 1:2, 0:64].broadcast_to([np_, 4, 64]), ALU.mult,
                 (t_tn, t_ropet[s]), (t_vt,))
            k.tt(vt[:np_, :, 64:128], tn[:np_, :, 0:64], ropet[s][:np_, 1:2, 64:128].broadcast_to([np_, 4, 64]), ALU.mult,
                 (t_tn, t_ropet[s]), (t_vt,))
            if i < NT:
                k.tt(qkb[s][:, 0:2, :], ut[:, 0:2, :], vt[:, 0:2, :], ALU.add, (t_ut, t_vt), (t_qkb[s],))
                kfv = kf[s][:, :].rearrange("p (a b) -> p a b", a=2)
                k.tt(kfv, ut[:, 2:4, :], vt[:, 2:4, :], ALU.add, (t_ut, t_vt), (t_kf[s],))
                k.cp(qkb[s][:, 2:4, :], kfv, (t_kf[s],), (t_qkb[s],), eng="act")
                k.dma(k_p[c0:c0 + 128, g * 256:(g + 1) * 256], kf[s][:, :], ds_kf[s], (t_kf[s],), ())
                k.cp(vf[s][:, :], bv[:, s * 256:(s + 1) * 256], (t_bank[2],), (t_vf[s],), eng="act")
                k.dma(v_p[c0:c0 + 128, g * 256:(g + 1) * 256], vf[s][:, :], ds_vf[s], (t_vf[s],), ())
                k.cp(Vg[:, i, :], vf[s][:, :], (t_vf[s],), (t_Vg,), eng="act")
                for j in range(4):
                    k.tr(pst[s][:, j, :], qkb[s][:, j, :], ident[:, :], (t_qkb[s], t_const), (t_pst[s],))
                k.cp(qkT[:, :, c0:c0 + 128], pst[s][:, 0:4, :], (t_pst[s],), (t_qkT,), eng="act")
            else:
                k.tt(qs_f[:, 2 * g:2 * g + 2, :], ut[:NS, 0:2, :], vt[:NS, 0:2, :], ALU.add, (t_ut, t_vt), (t_qkvs,))
                k.tt(ks_f[:, 2 * g:2 * g + 2, :], ut[:NS, 2:4, :], vt[:NS, 2:4, :], ALU.add, (t_ut, t_vt), (t_qkvs,))
                k.cp(vs_f[:, 2 * g:2 * g + 2, :], bv[:NS, s * 256:(s + 1) * 256].rearrange("p (a b) -> p a b", a=2),
                     (t_bank[2],), (t_qkvs,), eng="act")

    def l0b_gate(g, wga, tga):
        for j in range(2):
            for b in range(5):
                c0 = b * 512
                n = 512 if b < 4 else NS
                bk = bank[3 + (b % 2)]
                tb = t_bank[3 + (b % 2)]
                for kc in range(KC):
                    k.mm(bk[:, 0:n], wga[:, kc, j * 128:(j + 1) * 128], hT[:, kc, c0:c0 + n], kc == 0, kc == KC - 1, (t_hT, tga), (tb,))
                if b < 4:
                    k.act(sga[:, j, c0:c0 + n], bk[:, 0:n], AF.Silu, (tb,), (t_sga,))
                else:
                    k.act(sgaS[:, 2 * g + j, :], bk[:, 0:n], AF.Silu, (tb,), (t_sgaS,))

    biasT = k.sb("biasT", [16, T], BF16)
    t_biasT = Tok()
    km = k.sb("km", [128, 2, 8], F32)
    kmf = k.sb("kmf", [128, 2, 8], F32)
    kmh = k.sb("kmh", [128, 2, 8], BF16)
    kml = k.sb("kml", [128, 2, 8], BF16)
    t_km = Tok()
    sbk = k.sb("sbk", [128, 16], F32)
    mx8 = k.sb("mx8", [128, 2, 8], F32)
    selt = k.sb("selt", [128, 16], F32)
    gbias = k.sb("gbias", [128, 16], BF16)
    t_gate = Tok()
    pT = [k.sb("pT%d" % i, [128, 512], BF16) for i in range(2)]
    t_pT = [Tok(), Tok()]
    rden = k.sb("rden", [128, 512], F32)
    attf = k.sb("attf", [128, 512], F32)
    t_att = Tok()
    catst = [k.sb("catst%d" % i, [128, 512], BF16) for i in range(2)]
    t_catst = [Tok(), Tok()]
    ds_cat = [P.dsem("cat0"), P.dsem("cat1")]
    l0b_toks += [t_biasT, t_km, t_gate, t_att] + t_pT + t_catst
    actr = [0]

    def l0b_gates(g):
        k.red(km[:], qkT[:, 2:4, :].rearrange("p j (n c) -> p j n c", n=8), ALU.add, (t_qkT,), (t_km,))
        k.cp(kmh[:], km[:], (t_km,), (t_km,))
        k.cp(kmf[:], kmh[:], (t_km,), (t_km,))
        k.tt(kmf[:], km[:], kmf[:], ALU.subtract, (t_km,), (t_km,))
        k.cp(kml[:], kmf[:], (t_km,), (t_km,))
        k.cp(biasT[:, :], biasT0[:, :], (t_const, t_acst,), (t_biasT,))
        for i in range(8, NT):
            m = i // 2
            c0 = i * 128
            sbp = bank[5]
            for j in range(2):
                k.mm(sbp[:, j * 8:(j + 1) * 8], qkT[:, j, c0:c0 + 128], kmh[:, j, :], True, False, (t_qkT, t_km), (t_bank[5],))
                k.mm(sbp[:, j * 8:(j + 1) * 8], qkT[:, j, c0:c0 + 128], kml[:, j, :], False, True, (t_qkT, t_km), (t_bank[5],))
            k.tt(sbk[:, :], sbp[:, 0:16], mmask[:, m, :], ALU.add, (t_bank[5], t_const, t_acst), (t_gate,))
            for j in range(2):
                P.add("dve", (lambda jj: (lambda e: e.max(out=mx8[:, jj, :], in_=sbk[:, jj * 8:(jj + 1) * 8])))(j), (t_gate,), (t_gate,))
            k.tt(selt[:, :].rearrange("p (a b) -> p a b", a=2), sbk[:, :].rearrange("p (a b) -> p a b", a=2),
                 mx8[:, :, 2:3].broadcast_to([128, 2, 8]), ALU.is_ge, (t_gate,), (t_gate,))
            k.tt(selt[:, :], selt[:, :], ownm[:, m, :], ALU.add, (t_gate, t_const, t_acst), (t_gate,))
            k.ts(gbias[:, :], selt[:, :], -1.0, -NEG, ALU.add, ALU.mult, (t_gate,), (t_gate,))
            s = i % 2
            k.tr(pst[s][:16, 0, :], gbias[:, :], ident[:, :], (t_gate, t_const, t_acst), (t_pst[s],))
            k.cp(biasT[:, c0:c0 + 128], pst[s][:16, 0, :], (t_pst[s],), (t_biasT,), eng="act")

    def l0b_attn(g):
        for j in range(2):
            for c in range(4):
                q0 = c * 512
                ai = actr[0] % 2
                actr[0] += 1
                A = bank[3 + ai]
                tA = t_bank[3 + ai]
                Dn = bank[2 if ai == 0 else 5]
                tD = t_bank[2 if ai == 0 else 5]
                nk = 4 * c + 4
                for kt in range(nk):
                    s = kt % 2
                    st = bank[s]
                    diag = kt >= 4 * c
                    k.mm(st[:, :], qkT[:, 2 + j, kt * 128:(kt + 1) * 128], qkT[:, j, q0:q0 + 512], True, False, (t_qkT,), (t_bank[s],))
                    k.mm(st[:, :], esel[:, j * 8 + kt // 2, :], biasT[:, q0:q0 + 512], False, not diag, (t_const, t_acst, t_biasT), (t_bank[s],))
                    if diag:
                        k.mm(st[:, :], ident[:, :], cm[:, kt - 4 * c, :], False, True, (t_const, t_acst,), (t_bank[s],))
                    k.act(pT[s][:, :], st[:, :], AF.Exp, (t_bank[s],), (t_pT[s],))
                    k.mm(A[:, :], Vg[:, kt, j * 128:(j + 1) * 128], pT[s][:, :], kt == 0, kt == nk - 1, (t_Vg, t_pT[s]), (tA,))
                    k.mm(Dn[:, :], ones_bf[:, :], pT[s][:, :], kt == 0, kt == nk - 1, (t_const, t_acst, t_pT[s]), (tD,))
                P.add("dve", (lambda d: (lambda e: e.reciprocal(out=rden[:, :], in_=d[:, :])))(Dn), (tD,), (t_att,))
                k.tt(attf[:, :], A[:, :], rden[:, :], ALU.mult, (tA, t_att), (t_att,))
                cs = (2 * g + j + c) % 2
                k.tt(catst[cs][:, :], attf[:, :], sga[:, j, q0:q0 + 512], ALU.mult, (t_att, t_sga), (t_catst[cs],))
                k.dma(catT[2 * g + j, :, q0:q0 + 512], catst[cs][:, :], ds_cat[cs], (t_catst[cs],), (t_catT,))

    def wcols(c0, w):
        return w_in_even[:, c0:c0 + w]

    l0b_toks += [t_acst, t_qkw]
    k.fence(l0a_toks, l0b_toks)
    k.dma(biasT0[:], cst[0:16, C_BT:C_BT + T], ds_c, (), (t_acst,), eng="pool")
    k.dma(mmask[:], cst[:, C_MM:C_MM + 128].rearrange("p (a b) -> p a b", a=8), ds_c, (), (t_acst,))
    k.dma(ownm[:], cst[:, C_OWN:C_OWN + 128].rearrange("p (a b) -> p a b", a=8), ds_c, (), (t_acst,))
    k.dma(cm[:], cst[:, C_CM:C_CM + 2048].rearrange("p (j q) -> p j q", j=4), ds_c, (), (t_acst,), eng="pool")
    k.dma(esel[:], cst[0:16, C_ES:C_ES + 2048].rearrange("p (j q) -> p j q", j=16), ds_c, (), (t_acst,), eng="pool")
    k.dma(qkw[:, 0, :], q_norm_w[0:1, :].partition_broadcast(128), ds_c, (), (t_qkw,))
    k.dma(qkw[:, 1, :], k_norm_w[0:1, :].partition_broadcast(128), ds_c, (), (t_qkw,))
    k.ts(qkw[:, 1, :], qkw[:, 1, :], float(np.sqrt(128.0)), None, ALU.mult, None, (t_qkw,), (t_qkw,))
    for j in range(4):
        k.cp(qkw4[:, j, :], qkw[:, j // 2, :], (t_qkw,), (t_qkw4,))
    for g in range(int(dbg.get("ngroups", 4))):
        wq, tq = wload(wcols(256 * g, 256), KC, 256)
        wk, tk = wload(wcols(1024 + 256 * g, 256), KC, 256)
        wv, tv = wload(wcols(2048 + 256 * g, 256), KC, 256)
        wga, tga = wload(wcols(3072 + 256 * g, 256), KC, 256)
        l0b_proj(g, wq, wk, wv, tq, tk, tv)
        l0b_gate(g, wga, tga)
        l0b_gates(g)
        l0b_attn(g)
        if "qkT" in dbg and g == 0:
            o = dout("dbg_qkT", [128, 4 * T], BF16)
            dsd = P.dsem("dbg1")
            k.dma(o, qkT[:].rearrange("p a b -> p (a b)"), dsd, (t_qkT,), ())
            o2 = dout("dbg_sga", [128, 2 * T], BF16)
            k.dma(o2, sga[:].rearrange("p a b -> p (a b)"), dsd, (t_sga,), ())

    k.dma(k_s[:, :], ks_f[:].rearrange("p a b -> p (a b)"), ds_ks, (t_qkvs,), ())
    k.dma(v_s[:, :], vs_f[:].rearrange("p a b -> p (a b)"), ds_ks, (t_qkvs,), ())

    if dbg.get("sattn", 1):
        cache_k = din("cache_k", [2560 * 128, 1024])
        cache_v = din("cache_v", [2560 * 128, 1024])
        page_table = din("page_table", [1, NS * 16], I32)
    qs_d = nc.dram_tensor("qs_d", [NS, 1024], F32).ap()
    att_d = nc.dram_tensor("att_d", [NS, 1024], F32).ap()
    den_d = nc.dram_tensor("den_d", [NS, 8], F32).ap()

    def sample_attn():
        k.barrier()
        k.ptr = sattn_base
        f32t = lambda name, shape: k.sb(name, shape, F32)
        bft = lambda name, shape: k.sb(name, shape, BF16)
        ds_sa = P.dsem("sa_ld")
        t_qsd = Tok()
        k.dma(qs_d[:, :], qs_f[:].rearrange("p a b -> p (a b)"), ds_sa, (t_qkvs,), (t_qsd,))
        pti = k.sb("pti", [128, NS * 16], I32)
        idx = k.sb("idx", [128, NS * 16], I32)
        pcol = f32t("pcol", [128, 1])
        ident_f = f32t("identf2", [128, 128])
        dgm = f32t("dgm", [8, 128])
        onesf = f32t("onesf2", [128, 128])
        t_sc0 = Tok()
        k.dma(pti[:, :], page_table[0:1, :].partition_broadcast(128), ds_sa, (), (t_sc0,))
        k.dma(pcol[:, :], cst[:, C_PI:C_PI + 1], ds_sa, (), (t_sc0,), allow_slow_non_contiguous=True)
        k.dma(ident_f[:, :], cst[:, C_ID:C_ID + 128], ds_sa, (), (t_sc0,))
        k.dma(onesf[:, :], cst[:, C_ONE:C_ONE + 128], ds_sa, (), (t_sc0,))
        k.dma(dgm[:, :], cst[0:8, C_DG:C_DG + 128], ds_sa, (), (t_sc0,))
        k.ts(idx[:, :], pti[:, :], 128.0, pcol[:, 0:1], ALU.mult, ALU.add, (t_sc0,), (t_sc0,))
        sprod = f32t("sprod", [NS, 8, 128])
        sself = f32t("sself", [NS, 8])
        pself = f32t("pself", [NS, 8])
        t_self = Tok()
        k.tt(sprod[:], qs_f[:], ks_f[:], ALU.mult, (t_qkvs,), (t_self,))
        k.red(sself[:, :], sprod[:], ALU.add, (t_self,), (t_self,))
        k.act(pself[:, :], sself[:, :], AF.Exp, (t_self,), (t_self,))
        NKR, NVR = 4, 16
        kring = [f32t("kring%d" % i, [128, 1024]) for i in range(NKR)]
        t_kr = [Tok() for _ in range(NKR)]
        ds_kr = [P.dsem("kr%d" % i) for i in range(NKR)]
        vring = [bft("vring%d" % i, [128, 1024]) for i in range(NVR)]
        t_vr = [Tok() for _ in range(NVR)]
        ds_vr = [P.dsem("vr%d" % i) for i in range(4)]
        qbc = [f32t("qbc%d" % i, [128, 1024]) for i in range(2)]
        t_qbc = [Tok(), Tok()]
        ds_qbc = [P.dsem("qbc0"), P.dsem("qbc1")]
        prod = f32t("prod", [128, 8, 128])
        t_prod = Tok()
        S_all = f32t("S_all", [128, 16, 8])
        Sb = f32t("Sb", [128, 16, 8])
        Pb = bft("Pb", [128, 16, 8])
        t_S = Tok()
        sblk = f32t("sblk", [8, 16])
        sb8 = f32t("sb8", [8, 8])
        mx8s = f32t("mx8s", [8, 8])
        gb8 = f32t("gb8", [8, 8])
        bexp = f32t("bexp", [8, 16, 8])
        t_g = Tok()
        att_row = [f32t("att_row%d" % i, [1, 1024]) for i in range(2)]
        den_row = [f32t("den_row%d" % i, [1, 8]) for i in range(2)]
        t_rows = [Tok(), Tok()]
        ds_rows = [P.dsem("rows0"), P.dsem("rows1")]
        t_attd = Tok()
        den128 = f32t("den128", [1, 128])
        t_row = Tok()
        ones_c = bft("ones_c", [128, 1])
        k.memset(ones_c[:, :], 1.0, (t_sc0,))
        u32 = mybir.dt.uint32
        for s_ in range(NS):
            qb = qbc[s_ % 2]
            k.dma(qb[:, :], qs_d[s_:s_ + 1, :].partition_broadcast(128), ds_qbc[s_ % 2], (t_qsd,), (t_qbc[s_ % 2],))
            for j in range(16):
                col = s_ * 16 + j
                r_ = (s_ * 16 + j) % NKR
                P.add("pool", (lambda dst, cc: (lambda e: e.indirect_dma_start(
                    out=dst[:, :], out_offset=None, in_=cache_k[:, :],
                    in_offset=bass.IndirectOffsetOnAxis(ap=idx[:, cc:cc + 1].bitcast(u32), axis=0))))(kring[r_], col),
                    (t_sc0,), (t_kr[r_],), dsem=ds_kr[r_])
                k.tt(prod[:], kring[r_][:, :].rearrange("p (h d) -> p h d", h=8), qb[:, :].rearrange("p (h d) -> p h d", h=8), ALU.mult,
                     (t_kr[r_], t_qbc[s_ % 2]), (t_prod,))
                k.red(S_all[:, j, :], prod[:], ALU.add, (t_prod,), (t_S,))
            for j in range(16):
                col = s_ * 16 + j
                P.add("pool", (lambda dst, cc: (lambda e: e.indirect_dma_start(
                    out=dst[:, :], out_offset=None, in_=cache_v[:, :],
                    in_offset=bass.IndirectOffsetOnAxis(ap=idx[:, cc:cc + 1].bitcast(u32), axis=0))))(vring[j], col),
                    (t_sc0,), (t_vr[j],), dsem=ds_vr[j % 4])
            bk, tbk = bank[0], t_bank[0]
            for j in range(16):
                k.mm(bk[0:8, j:j + 1], S_all[:, j, :], onesf[:, 0:1], True, True, (t_S, t_sc0), (tbk,))
            k.cp(sblk[:, :], bk[0:8, 0:16], (tbk,), (t_g,))
            k.tt(sb8[:, :], sblk[:, 0:16:2], sblk[:, 1:16:2], ALU.add, (t_g,), (t_g,))
            P.add("dve", lambda e: e.max(out=mx8s[:, :], in_=sb8[:, :]), (t_g,), (t_g,))
            k.tt(gb8[:, :], sb8[:, :], mx8s[:, 2:3].broadcast_to([8, 8]), ALU.is_ge, (t_g,), (t_g,))
            k.ts(gb8[:, :], gb8[:, :], -1.0, -NEG, ALU.add, ALU.mult, (t_g,), (t_g,))
            k.tt(bexp[:].rearrange("p (n t) h -> p n t h", t=2), gb8[:, :].unsqueeze(2).unsqueeze(3).broadcast_to([8, 8, 2, 8]),
                 dgm[:, :].rearrange("p (n t h) -> p n t h", n=8, t=2), ALU.mult, (t_g, t_sc0), (t_g,))
            bk2, tbk2 = bank[1], t_bank[1]
            k.mm(bk2[:, 0:128], onesf[0:8, :], bexp[:].rearrange("p j h -> p (j h)"), True, True, (t_sc0, t_g), (tbk2,))
            k.tt(Sb[:].rearrange("p j h -> p (j h)"), S_all[:].rearrange("p j h -> p (j h)"), bk2[:, 0:128], ALU.add, (t_S, tbk2), (t_S,))
            k.act(Pb[:].rearrange("p j h -> p (j h)"), Sb[:].rearrange("p j h -> p (j h)"), AF.Exp, (t_S,), (t_S,))
            ba, tba = bank[2 + s_ % 2], t_bank[2 + s_ % 2]
            ba2, tba2 = bank[4 + s_ % 2], t_bank[4 + s_ % 2]
            for j in range(16):
                for h in range(8):
                    bb, tbb = (ba, tba) if h < 4 else (ba2, tba2)
                    k.mm(bb[0:1, (h % 4) * 128:(h % 4 + 1) * 128], Pb[:, j, h:h + 1], vring[j][:, h * 128:(h + 1) * 128], j == 0, j == 15,
                         (t_S, t_vr[j]), (tbb,))
            rs = s_ % 2
            k.cp(att_row[rs][0:1, 0:512], ba[0:1, :], (tba,), (t_rows[rs],), eng="act")
            k.cp(att_row[rs][0:1, 512:1024], ba2[0:1, :], (tba2,), (t_rows[rs],), eng="act")
            k.mm(bk[0:1, 128:256], ones_c[:, 0:1], Pb[:].rearrange("p j h -> p (j h)"), True, True, (t_sc0, t_S), (tbk,))
            k.cp(den128[0:1, :], bk[0:1, 128:256], (tbk,), (t_row,))
            k.red(den_row[rs][0:1, :], den128[0:1, :].rearrange("p (j h) -> p h j", h=8), ALU.add, (t_row,), (t_rows[rs],))
            k.dma(att_d[s_:s_ + 1, :], att_row[rs][0:1, :], ds_rows[rs], (t_rows[rs],), (t_attd,))
            k.dma(den_d[s_:s_ + 1, :], den_row[rs][0:1, :], ds_rows[rs], (t_rows[rs],), (t_attd,))
        att_t = f32t("att_t", [NS, 8, 128])
        den_t = f32t("den_t", [NS, 8])
        t_fin = Tok()
        k.dma(att_t[:].rearrange("p a b -> p (a b)"), att_d[:, :], ds_sa, (t_attd,), (t_fin,))
        k.dma(den_t[:, :], den_d[:, :], ds_sa, (t_attd,), (t_fin,))
        k.tt(sprod[:], vs_f[:], pself[:, :].unsqueeze(2).broadcast_to([NS, 8, 128]), ALU.mult, (t_qkvs, t_self), (t_self,))
        k.tt(att_t[:], att_t[:], sprod[:], ALU.add, (t_fin, t_self), (t_fin,))
        k.tt(den_t[:, :], den_t[:, :], pself[:, :], ALU.add, (t_fin, t_self), (t_fin,))
        P.add("dve", lambda e: e.reciprocal(out=den_t[:, :], in_=den_t[:, :]), (t_fin,), (t_fin,))
        k.tt(att_t[:], att_t[:], den_t[:, :].unsqueeze(2).broadcast_to([NS, 8, 128]), ALU.mult, (t_fin,), (t_fin,))
        bk, tbk = bank[0], t_bank[0]
        for h in range(8):
            k.tr(bk[:, h * 16:(h + 1) * 16], att_t[:, h, :], ident_f[0:16, 0:16], (t_fin, t_sc0), (tbk,))
        k.tt(catS[:, 0:8, :], bk[:, 0:128].rearrange("p (h s) -> p h s", h=8), sgaS[:, :, :], ALU.mult, (tbk, t_sgaS), (t_catS,))

    if dbg.get("sattn", 1):
        sample_attn()


    lam_re = din("lam_re", [64, 64])
    lam_im = din("lam_im", [64, 64])
    log_dt = din("log_dt", [1, 64])
    s5_b = din("s5_b", [64, 64, 32])
    s5_c = din("s5_c", [64, 16, 128])
    s5_d = din("s5_d", [1, 1024])
    glu_w = din("glu_w", [1024, 1024])
    glu_b = din("glu_b", [1, 1024])
    st_s5 = din("st_s5", [NS, 8192])
    s5_p = dout("s5_p", [64, 64, 2])
    s5_s = dout("s5_s", [NS, 8192])
    uT_d = nc.dram_tensor("uT_d", [8, 128, TS], BF16).ap()
    sgbT_d = nc.dram_tensor("sgbT_d", [8, 128, TS], BF16).ap()
    zT_d = nc.dram_tensor("zT_d", [8, 128, T], BF16).ap()
    t_uTd, t_sgbTd, t_zTd = Tok(), Tok(), Tok()
    k.barrier()
    k.ptr = sattn_base
    ust = [k.sb("ust%d" % i, [128, TS], BF16) for i in range(2)]
    sgst = [k.sb("sgst%d" % i, [128, TS], BF16) for i in range(2)]
    t_ust, t_sgst = [Tok(), Tok()], [Tok(), Tok()]
    ds_ust, ds_sgst = [P.dsem("ust0"), P.dsem("ust1")], [P.dsem("sgst0"), P.dsem("sgst1")]
    if dbg.get("s5", 1):
        for f in range(8):
            s_ = f % 2
            wu, tu = wload(w_in_even[:, 4096 + 128 * f:4096 + 128 * (f + 1)], KC, 128)
            wgb, tgb = wload(w_in_even[:, 5120 + 128 * f:5120 + 128 * (f + 1)], KC, 128)
            for which, (ww, tw) in enumerate(((wu, tu), (wgb, tgb))):
                for b in range(5):
                    c0 = b * 512
                    n = 512 if b < 4 else NS
                    bi = (which * 5 + b) % 2
                    for kc in range(KC):
                        k.mm(bank[bi][:, 0:n], ww[:, kc, :], hT[:, kc, c0:c0 + n], kc == 0, kc == KC - 1, (t_hT, tw), (t_bank[bi],))
                    if which == 0:
                        k.act(ust[s_][:, c0:c0 + n], bank[bi][:, 0:n], AF.Copy, (t_bank[bi],), (t_ust[s_],))
                    else:
                        k.act(sgst[s_][:, c0:c0 + n], bank[bi][:, 0:n], AF.Silu, (t_bank[bi],), (t_sgst[s_],))
            k.dma(uT_d[f], ust[s_][:, :], ds_ust[s_], (t_ust[s_],), (t_uTd,))
            k.dma(sgbT_d[f], sgst[s_][:, :], ds_sgst[s_], (t_sgst[s_],), (t_sgbTd,))

    def l0c():
        k.barrier()
        k.ptr = hT_base
        f32t = lambda name, shape: k.sb(name, shape, F32)
        TWO_PI = float(2.0 * np.pi)
        ident_f = f32t("ident_f", [128, 128])
        evec = f32t("evec", [128, 17])
        nvec = f32t("nvec", [128, 128])
        t_c2 = Tok()
        ds_s5 = P.dsem("s5ld")
        k.dma(ident_f[:], cst[:, C_ID:C_ID + 128], ds_s5, (), (t_c2,))
        k.dma(evec[:], cst[:, C_EV:C_EV + 17], ds_s5, (), (t_c2,))
        k.dma(nvec[:], cst[:, C_NV:C_NV + 128], ds_s5, (), (t_c2,))
        dvec, gbvec = f32t("dvec", [128, 8]), f32t("gbvec", [128, 8])
        yf = [f32t("yf%d" % i, [128, 512]) for i in range(2)]
        zst = [k.sb("zst%d" % i, [128, 512], BF16) for i in range(2)]
        t_yf, t_zst = [Tok(), Tok()], [Tok(), Tok()]
        ds_zst = [P.dsem("zst0"), P.dsem("zst1")]
        zS = k.sb("zS", [128, 8, NS], BF16)
        t_zS = Tok()
        glu_base = k.ptr
        lr, li, lgdt = f32t("lr", [128, 32]), f32t("li", [128, 32]), f32t("lgdt", [128, 32])
        t_l = Tok()
        with nc.allow_non_contiguous_dma(reason="tiny parameter relayout"):
            for g2 in range(2):
                hs = slice(g2 * 64, (g2 + 1) * 64)
                k.dma(lr[hs, :], lam_re.rearrange("(q g) p -> g p q", g=2)[g2], ds_s5, (), (t_l,), allow_slow_non_contiguous=True)
                k.dma(li[hs, :], lam_im.rearrange("(q g) p -> g p q", g=2)[g2], ds_s5, (), (t_l,), allow_slow_non_contiguous=True)
                k.dma(lgdt[hs, :], log_dt.rearrange("o (q g) -> g o q", g=2)[g2].partition_broadcast(64), ds_s5, (), (t_l,), allow_slow_non_contiguous=True)
            k.dma(dvec[:, :], s5_d.rearrange("o (f p) -> p (o f)", p=128), ds_s5, (), (t_l,), allow_slow_non_contiguous=True)
            k.dma(gbvec[:, :], glu_b.rearrange("o (f p) -> p (o f)", p=128), ds_s5, (), (t_l,), allow_slow_non_contiguous=True)
        NE = 17 * 32
        tA, tB, tC, tD = (f32t("tmp%d" % i, [128, 544]) for i in range(4))
        tI = k.sb("tmpI", [128, 544], I32)
        t_tmp = Tok()

        def sincos(ang, n, out_s, out_c, r, w):
            for shift, out in ((0.0, out_s), (float(np.pi / 2), out_c)):
                if out is None:
                    continue
                k.ts(tA[:, 0:n], ang, shift, 1.0 / TWO_PI, ALU.add, ALU.mult, r, (t_tmp,))
                k.cp(tI[:, 0:n], tA[:, 0:n], (t_tmp,), (t_tmp,))
                k.cp(tA[:, 0:n], tI[:, 0:n], (t_tmp,), (t_tmp,))
                k.stt(tA[:, 0:n], tA[:, 0:n], -TWO_PI, ang, ALU.mult, ALU.add, r + (t_tmp,), (t_tmp,))
                k.act(out, tA[:, 0:n], AF.Sin, (t_tmp,), w, bias=shift) if shift == 0.0 else None
                if shift != 0.0:
                    k.ts(tA[:, 0:n], tA[:, 0:n], shift, None, ALU.add, None, (t_tmp,), (t_tmp,))
                    k.ts(tB[:, 0:n], tA[:, 0:n], float(np.pi), -TWO_PI, ALU.is_gt, ALU.mult, (t_tmp,), (t_tmp,))
                    k.tt(tA[:, 0:n], tA[:, 0:n], tB[:, 0:n], ALU.add, (t_tmp,), (t_tmp,))
                    k.act(out, tA[:, 0:n], AF.Sin, (t_tmp,), w)

        dtv, rho, th = f32t("dtv", [128, 32]), f32t("rho", [128, 32]), f32t("th", [128, 32])
        k.act(dtv[:, :], lgdt[:, :], AF.Exp, (t_l,), (t_l,))
        k.tt(rho[:, :], lr[:, :], dtv[:, :], ALU.mult, (t_l,), (t_l,))
        k.tt(th[:, :], li[:, :], dtv[:, :], ALU.mult, (t_l,), (t_l,))
        RH, TH, MAG = f32t("RH", [128, 17, 32]), f32t("TH", [128, 17, 32]), f32t("MAG", [128, 17, 32])
        APR, API = f32t("APR", [128, 17, 32]), f32t("API", [128, 17, 32])
        SN, CS = f32t("SN", [128, 17, 32]), f32t("CS", [128, 17, 32])
        t_tab = Tok()
        ev_b = evec[:, :].unsqueeze(2).broadcast_to([128, 17, 32])
        k.tt(RH[:], rho[:, :].unsqueeze(1).broadcast_to([128, 17, 32]), ev_b, ALU.mult, (t_l, t_c2), (t_tab,))
        k.tt(TH[:], th[:, :].unsqueeze(1).broadcast_to([128, 17, 32]), ev_b, ALU.mult, (t_l, t_c2), (t_tab,))
        flat = lambda t: t[:].rearrange("p a b -> p (a b)")
        k.act(flat(MAG), flat(RH), AF.Exp, (t_tab,), (t_tab,))
        sincos(flat(TH), NE, flat(SN), flat(CS), (t_tab,), (t_tab,))
        k.tt(APR[:], MAG[:], CS[:], ALU.mult, (t_tab,), (t_tab,))
        k.tt(API[:], MAG[:], SN[:], ALU.mult, (t_tab,), (t_tab,))
        ph16 = f32t("ph16", [128, 32])
        k.ts(tA[:, 0:32], TH[:, 16, :], 1.0 / TWO_PI, None, ALU.mult, None, (t_tab,), (t_tmp,))
        k.cp(tI[:, 0:32], tA[:, 0:32], (t_tmp,), (t_tmp,))
        k.cp(tA[:, 0:32], tI[:, 0:32], (t_tmp,), (t_tmp,))
        k.stt(ph16[:, :], tA[:, 0:32], -TWO_PI, TH[:, 16, :], ALU.mult, ALU.add, (t_tmp, t_tab), (t_tab,))
        cr, ci = f32t("cr", [128, 32]), f32t("ci", [128, 32])
        e1, e2, e3 = tA[:, 0:32], tB[:, 0:32], tC[:, 0:32]
        k.tt(e1, lr[:, :], lr[:, :], ALU.mult, (t_l,), (t_tmp,))
        k.tt(e2, li[:, :], li[:, :], ALU.mult, (t_l,), (t_tmp,))
        k.tt(e1, e1, e2, ALU.add, (t_tmp,), (t_tmp,))
        P.add("dve", lambda e: e.reciprocal(out=tD[:, 0:32], in_=tA[:, 0:32]), (t_tmp,), (t_tmp,))
        k.ts(e3, APR[:, 1, :], -1.0, None, ALU.add, None, (t_tab,), (t_tmp,))
        k.tt(e1, e3, lr[:, :], ALU.mult, (t_tmp, t_l), (t_tmp,))
        k.tt(e2, API[:, 1, :], li[:, :], ALU.mult, (t_tab, t_l), (t_tmp,))
        k.tt(e1, e1, e2, ALU.add, (t_tmp,), (t_tmp,))
        k.tt(cr[:, :], e1, tD[:, 0:32], ALU.mult, (t_tmp,), (t_tab,))
        k.tt(e1, API[:, 1, :], lr[:, :], ALU.mult, (t_tab, t_l), (t_tmp,))
        k.tt(e2, e3, li[:, :], ALU.mult, (t_tmp, t_l), (t_tmp,))
        k.tt(e1, e1, e2, ALU.subtract, (t_tmp,), (t_tmp,))
        k.tt(ci[:, :], e1, tD[:, 0:32], ALU.mult, (t_tmp,), (t_tab,))
        if dbg.get("s5stop") == 1:
            return
        Braw = f32t("Braw", [128, 32, 32])
        Bbr, Bbi = f32t("Bbr", [128, 32, 16]), f32t("Bbi", [128, 32, 16])
        t_B = Tok()
        for g2 in range(2):
            hs = slice(g2 * 64, (g2 + 1) * 64)
            k.dma(Braw[hs, :, :], s5_b.rearrange("(q g) p x -> g p q x", g=2)[g2], ds_s5, (), (t_B,))
        Bre = Braw[:, :, 0:32:2]
        Bim = Braw[:, :, 1:32:2]
        crb = cr[:, :].unsqueeze(2).broadcast_to([128, 32, 16])
        cib = ci[:, :].unsqueeze(2).broadcast_to([128, 32, 16])
        v3 = lambda t: t[:, 0:512].rearrange("p (a b) -> p a b", a=32)
        k.tt(v3(tA), Bre, crb, ALU.mult, (t_B, t_tab), (t_tmp,))
        k.tt(v3(tB), Bim, cib, ALU.mult, (t_B, t_tab), (t_tmp,))
        k.tt(Bbr[:], v3(tA), v3(tB), ALU.subtract, (t_tmp,), (t_B,))
        k.tt(v3(tA), Bim, crb, ALU.mult, (t_B, t_tab), (t_tmp,))
        k.tt(v3(tB), Bre, cib, ALU.mult, (t_B, t_tab), (t_tmp,))
        k.tt(Bbi[:], v3(tA), v3(tB), ALU.add, (t_tmp,), (t_B,))
        if dbg.get("s5stop") == 2:
            return
        big_base = k.ptr
        big = f32t("big", [16, 8192])
        t_big = Tok()
        k.ptr = big_base
        W4 = [128, 17, 4, 16]
        t1, t2 = f32t("t1", W4), f32t("t2", W4)
        Wr, Wi = f32t("Wr", [128, 16, 4, 2, 16]), f32t("Wi", [128, 16, 4, 2, 16])
        Kw = k.sb("Kw", [128, 16, 128], BF16)
        Win = k.sb("Win", [128, 16, 2, 128], BF16)
        assert k.ptr >= big_base + 32768
        CreT, CimT = f32t("CreT", [128, 32, 16]), f32t("CimT", [128, 32, 16])
        X0r, X0i = f32t("X0r", [128, 32, 16]), f32t("X0i", [128, 32, 16])
        CrePad, NCimPad = f32t("CrePad", [128, 32, 2, 16]), f32t("NCimPad", [128, 32, 2, 16])
        t_C = Tok()
        bview = big[:, :].rearrange("c (q x) -> c q x", q=32)

        def to_pair_layout(dst_r, dst_i, wtok):
            for ri, dst in ((0, dst_r), (1, dst_i)):
                pb = bank[ri]
                pbv = pb[:, :].rearrange("p (q c) -> p q c", q=32)
                for q in range(32):
                    k.tr(pbv[:, q, :], bview[:, q, ri:256:2], ident_f[0:16, 0:16], (t_big, t_c2), (t_bank[ri],))
                k.cp(dst[:], pbv, (t_bank[ri],), (wtok,), eng="act")

        k.dma(big[:, :].rearrange("c (q g x) -> c q g x", q=32, g=2), s5_c.rearrange("(q g) c x -> c q g x", g=2), ds_s5, (), (t_big,))
        to_pair_layout(CreT, CimT, t_C)
        k.dma(big[:, :], st_s5[:, :], ds_s5, (), (t_big,))
        to_pair_layout(X0r, X0i, t_C)
        k.memset(CrePad[:].rearrange("p a b c -> p (a b c)"), 0.0, (t_C,))
        k.memset(NCimPad[:].rearrange("p a b c -> p (a b c)"), 0.0, (t_C,))
        for g2 in range(2):
            hs = slice(g2 * 64, (g2 + 1) * 64)
            k.cp(CrePad[hs, :, g2, :], CreT[hs, :, :], (t_C,), (t_C,))
            k.ts(NCimPad[hs, :, g2, :], CimT[hs, :, :], -1.0, None, ALU.mult, None, (t_C,), (t_C,))

        if dbg.get("s5stop") == 3:
            return
        Vr, Vi = k.sb("Vr", [128, 17, 4, 2, 16], BF16), k.sb("Vi", [128, 17, 4, 2, 16], BF16)
        t_w = Tok()
        t_t12 = Tok()
        k.fence([t_big], [t_w, t_t12])
        for tz in (Wr, Wi):
            k.memset(tz[:].rearrange("p a b c d -> p (a b c d)"), 0.0, (t_w,))
        for tz in (Vr, Vi):
            k.memset(tz[:].rearrange("p a b c d -> p (a b c d)"), 0.0, (t_w,))
        k.memset(Kw[:].rearrange("p a b -> p (a b)"), 0.0, (t_w,))
        uTb = [k.sb("uTb%d" % i, [128, TS], BF16) for i in range(2)]
        t_uTb = [Tok(), Tok()]
        ds_uTb = [P.dsem("uTb0"), P.dsem("uTb1")]
        Sre, Sim = f32t("Sre", [128, 4, 128]), f32t("Sim", [128, 4, 128])
        cosn, sinn = f32t("cosn", [128, 4, 128]), f32t("sinn", [128, 4, 128])
        m1, m2 = f32t("m1", [128, 4, 128]), f32t("m2", [128, 4, 128])
        Zr, Zi = f32t("Zr", [128, 4, 128]), f32t("Zi", [128, 4, 128])
        R0 = f32t("R0", [128, 4, 128])
        Xpr, Xpi = k.sb("Xpr", [128, 4, 128], BF16), k.sb("Xpi", [128, 4, 128], BF16)
        t_scan = Tok()
        k.memset(Xpr[:].rearrange("p a b -> p (a b)"), 0.0, (t_scan,))
        k.memset(Xpi[:].rearrange("p a b -> p (a b)"), 0.0, (t_scan,))
        s5st_p = f32t("s5st_p", [128, 32, 2])
        s5st_s = f32t("s5st_s", [128, 32, 2, 16])
        t_st = Tok()
        xsn_r, xsn_i = f32t("xsn_r", [128, 4, 16]), f32t("xsn_i", [128, 4, 16])
        xsb_r, xsb_i = k.sb("xsb_r", [128, 4, 16], BF16), k.sb("xsb_i", [128, 4, 16], BF16)
        t_xs = Tok()
        yfs = f32t("yfs", [128, 16])
        xsps = f32t("xsps", [128, 4, 2, 16])
        t_xsps = Tok()
        f2 = lambda t: t[:].rearrange("p a b -> p (a b)")
        bc_e = lambda tab, ne, qs: tab[:, 0:ne, qs].unsqueeze(3).broadcast_to([128, ne, 4, 16])
        bc_x = lambda src, ne, qs: src[:, qs, :].unsqueeze(1).broadcast_to([128, ne, 4, 16])

        for f in range(8):
            qs = slice(4 * f, 4 * f + 4)
            s_ = f % 2
            k.dma(uTb[s_][:, :], uT_d[f], ds_uTb[s_], (t_uTd,), (t_uTb[s_],))
            for dst, (ta, xa, tb, xb, op) in ((Wr, (APR, Bbr, API, Bbi, ALU.subtract)), (Wi, (APR, Bbi, API, Bbr, ALU.add))):
                k.tt(t1[:, 0:16], bc_e(ta, 16, qs), bc_x(xa, 16, qs), ALU.mult, (t_tab, t_B), (t_t12,))
                k.tt(t2[:, 0:16], bc_e(tb, 16, qs), bc_x(xb, 16, qs), ALU.mult, (t_tab, t_B), (t_t12,))
                for g2 in range(2):
                    hs = slice(g2 * 64, (g2 + 1) * 64)
                    k.tt(dst[hs, :, :, g2, :], t1[hs, 0:16], t2[hs, 0:16], op, (t_t12,), (t_w,))
            kb = bank[2]
            kbv = kb[:, :].rearrange("p (a b) -> p a b", a=4)
            for t0 in range(0, 16, 4):
                for tl in range(4):
                    tau = t0 + tl
                    for qi in range(4):
                        ps_ = slice(32 * qi, 32 * qi + 32)
                        k.mm(kbv[ps_, tl, ps_], Wr[:, tau, qi, :, :].rearrange("p a b -> p (a b)"),
                             CrePad[:, 4 * f + qi, :, :].rearrange("p a b -> p (a b)"), True, False, (t_w, t_C), (t_bank[2],), tp=(0, 32 * qi))
                        k.mm(kbv[ps_, tl, ps_], Wi[:, tau, qi, :, :].rearrange("p a b -> p (a b)"),
                             NCimPad[:, 4 * f + qi, :, :].rearrange("p a b -> p (a b)"), False, True, (t_w, t_C), (t_bank[2],), tp=(0, 32 * qi))
                for qi in range(4):
                    ps_ = slice(32 * qi, 32 * qi + 32)
                    k.cp(Kw[ps_, t0:t0 + 4, ps_], kbv[ps_, :, ps_], (t_bank[2],), (t_w,), eng="act")
            cnt = 0
            for i in range(16):
                for ri, src in ((0, Wr), (1, Wi)):
                    wb = bank[3 + (cnt // 4) % 2]
                    twb = t_bank[3 + (cnt // 4) % 2]
                    k.tr(wb[:, (cnt % 4) * 128:(cnt % 4 + 1) * 128], src[:, 15 - i, :, :, :].rearrange("p a b c -> p (a b c)"),
                         ident_f[:, :], (t_w, t_c2), (twb,))
                    cnt += 1
                    if cnt % 4 == 0:
                        i0 = (cnt - 4) // 2
                        k.cp(Win[:, i0:i0 + 2, :, :].rearrange("p a b c -> p (a b c)"), wb[:, :], (twb,), (t_w,), eng="act")
            k.tt(t1[:], bc_e(APR, 17, qs), bc_x(CreT, 17, qs), ALU.mult, (t_tab, t_C), (t_t12,))
            k.tt(t2[:], bc_e(API, 17, qs), bc_x(CimT, 17, qs), ALU.mult, (t_tab, t_C), (t_t12,))
            for g2 in range(2):
                hs = slice(g2 * 64, (g2 + 1) * 64)
                k.tt(Vr[hs, :, :, g2, :], t1[hs], t2[hs], ALU.subtract, (t_t12,), (t_w,))
            k.tt(t1[:], bc_e(API, 17, qs), bc_x(CreT, 17, qs), ALU.mult, (t_tab, t_C), (t_t12,))
            k.tt(t2[:], bc_e(APR, 17, qs), bc_x(CimT, 17, qs), ALU.mult, (t_tab, t_C), (t_t12,))
            for g2 in range(2):
                hs = slice(g2 * 64, (g2 + 1) * 64)
                k.stt(Vi[hs, :, :, g2, :], t1[hs], -1.0, t2[hs], ALU.mult, ALU.subtract, (t_t12,), (t_w,))
            if dbg.get("s5stop") == 4:
                return
            u = uTb[s_]
            for qi in range(4):
                ps_ = slice(32 * qi, 32 * qi + 32)
                for ri in range(2):
                    for i in range(16):
                        k.mm(bank[qi][:, ri * 128:(ri + 1) * 128], Win[ps_, i, ri, :], u[ps_, i:T:16], i == 0, i == 15,
                             (t_w, t_uTb[s_]), (t_bank[qi],), tp=(32 * qi, 0))
                k.cp(Sre[:, qi, :], bank[qi][:, 0:128], (t_bank[qi],), (t_scan,), eng="act")
                k.cp(Sim[:, qi, :], bank[qi][:, 128:256], (t_bank[qi],), (t_scan,), eng="act")
            if dbg.get("s5stop") == 41:
                return
            k.tt(m1[:], ph16[:, qs].unsqueeze(2).broadcast_to([128, 4, 128]), nvec[:, :].unsqueeze(1).broadcast_to([128, 4, 128]),
                 ALU.mult, (t_tab, t_c2), (t_scan,))
            k.ts(f2(m1), f2(m1), float(128 * 2 * np.pi), None, ALU.add, None, (t_scan,), (t_scan,))
            sincos(f2(m1), 512, f2(sinn), f2(cosn), (t_scan,), (t_scan,))
            if dbg.get("s5stop") == 42:
                return
            k.cp(R0[:], MAG[:, 16, qs].unsqueeze(2).broadcast_to([128, 4, 128]), (t_tab,), (t_scan,))
            k.memset(R0[:, :, 0:1], 0.0, (t_scan,))
            k.tt(m1[:], Sre[:], cosn[:], ALU.mult, (t_scan,), (t_scan,))
            k.tt(m2[:], Sim[:], sinn[:], ALU.mult, (t_scan,), (t_scan,))
            k.tt(m1[:], m1[:], m2[:], ALU.add, (t_scan,), (t_scan,))
            k.tt(m2[:], Sim[:], cosn[:], ALU.mult, (t_scan,), (t_scan,))
            k.tt(Sim[:], Sre[:], sinn[:], ALU.mult, (t_scan,), (t_scan,))
            k.tt(m2[:], m2[:], Sim[:], ALU.subtract, (t_scan,), (t_scan,))
            if dbg.get("s5stop") == 43:
                return
            P.add("dve", lambda e: e.tensor_tensor_scan(out=f2(Zr), data0=f2(R0), data1=f2(m1), initial=0.0, op0=ALU.mult, op1=ALU.add),
                  (t_scan,), (t_scan,))
            P.add("dve", lambda e: e.tensor_tensor_scan(out=f2(Zi), data0=f2(R0), data1=f2(m2), initial=0.0, op0=ALU.mult, op1=ALU.add),
                  (t_scan,), (t_scan,))
            if dbg.get("s5stop") == 44:
                return
            k.tt(m1[:], Zr[:], cosn[:], ALU.mult, (t_scan,), (t_scan,))
            k.tt(m2[:], Zi[:], sinn[:], ALU.mult, (t_scan,), (t_scan,))
            k.tt(Sre[:], m1[:], m2[:], ALU.subtract, (t_scan,), (t_scan,))
            k.tt(m1[:], Zr[:], sinn[:], ALU.mult, (t_scan,), (t_scan,))
            k.tt(m2[:], Zi[:], cosn[:], ALU.mult, (t_scan,), (t_scan,))
            k.tt(Sim[:], m1[:], m2[:], ALU.add, (t_scan,), (t_scan,))
            if dbg.get("s5stop") == 45:
                return
            k.cp(s5st_p[:, qs, 0:1], Sre[:, :, 127:128], (t_scan,), (t_st,))
            k.cp(s5st_p[:, qs, 1:2], Sim[:, :, 127:128], (t_scan,), (t_st,))
            k.cp(Xpr[:, :, 1:128], Sre[:, :, 0:127], (t_scan,), (t_scan,))
            k.cp(Xpi[:, :, 1:128], Sim[:, :, 0:127], (t_scan,), (t_scan,))
            if dbg.get("s5stop") == 5:
                return
            for b in range(4):
                yb = bank[4 + b % 2]
                tyb = t_bank[4 + b % 2]
                ybv = yb[:, :].rearrange("p (n j) -> p n j", j=16)
                ubv = u[:, b * 512:(b + 1) * 512].rearrange("p (n j) -> p n j", j=16)
                for tau in range(16):
                    k.mm(ybv[:, :, tau:16], Kw[:, tau, :], ubv[:, :, 0:16 - tau], tau == 0, False, (t_w, t_uTb[s_]), (tyb,))
                for qi in range(4):
                    ps_ = slice(32 * qi, 32 * qi + 32)
                    for j in range(16):
                        last = (qi == 3 and j == 15)
                        k.mm(yb[ps_, j:512:16], Vr[:, j + 1, qi, :, :].rearrange("p a b -> p (a b)"), Xpr[:, qi, b * 32:(b + 1) * 32],
                             False, False, (t_w, t_scan), (tyb,), tp=(0, 32 * qi))
                        k.mm(yb[ps_, j:512:16], Vi[:, j + 1, qi, :, :].rearrange("p a b -> p (a b)"), Xpi[:, qi, b * 32:(b + 1) * 32],
                             False, last, (t_w, t_scan), (tyb,), tp=(0, 32 * qi))
                ys_ = (4 * f + b) % 2
                k.stt(yf[ys_][:, :], u[:, b * 512:(b + 1) * 512], dvec[:, f:f + 1], yb[:, :], ALU.mult, ALU.add,
                      (t_uTb[s_], t_l, tyb), (t_yf[ys_],))
                k.act(zst[ys_][:, :], yf[ys_][:, :], AF.Gelu_apprx_tanh, (t_yf[ys_],), (t_zst[ys_],))
                k.dma(zT_d[f, :, b * 512:(b + 1) * 512], zst[ys_][:, :], ds_zst[ys_], (t_zst[ys_],), (t_zTd,))
            if dbg.get("s5stop") == 6:
                return
            xbv = xsps[:]
            for qi in range(4):
                ps_ = slice(32 * qi, 32 * qi + 32)
                for ri in range(2):
                    k.mm(bank[qi][:, 256 + ri * 16:256 + (ri + 1) * 16], Win[ps_, 15, ri, :], u[ps_, T:TS], True, True,
                         (t_w, t_uTb[s_]), (t_bank[qi],), tp=(32 * qi, 0))
                k.cp(xsps[:, qi, :, :], bank[qi][:, 256:288].rearrange("p (r s) -> p r s", r=2), (t_bank[qi],), (t_xsps,), eng="act")
            a1r = APR[:, 1, qs].unsqueeze(2).broadcast_to([128, 4, 16])
            a1i = API[:, 1, qs].unsqueeze(2).broadcast_to([128, 4, 16])
            k.tt(xsn_r[:], X0r[:, qs, :], a1r, ALU.mult, (t_C, t_tab), (t_xs,))
            k.tt(xsn_i[:], X0i[:, qs, :], a1i, ALU.mult, (t_C, t_tab), (t_xs,))
            k.tt(xsn_r[:], xsn_r[:], xsn_i[:], ALU.subtract, (t_xs,), (t_xs,))
            k.tt(s5st_s[:, qs, 0, :], xsn_r[:], xbv[:, :, 0, :], ALU.add, (t_xs, t_xsps), (t_st,))
            k.tt(xsn_r[:], X0r[:, qs, :], a1i, ALU.mult, (t_C, t_tab), (t_xs,))
            k.tt(xsn_i[:], X0i[:, qs, :], a1r, ALU.mult, (t_C, t_tab), (t_xs,))
            k.tt(xsn_r[:], xsn_r[:], xsn_i[:], ALU.add, (t_xs,), (t_xs,))
            k.tt(s5st_s[:, qs, 1, :], xsn_r[:], xbv[:, :, 1, :], ALU.add, (t_xs, t_xsps), (t_st,))
            k.cp(xsb_r[:], s5st_s[:, qs, 0, :], (t_st,), (t_xs,))
            k.cp(xsb_i[:], s5st_s[:, qs, 1, :], (t_st,), (t_xs,))
            yb = bank[3]
            for qi in range(4):
                ps_ = slice(32 * qi, 32 * qi + 32)
                k.mm(yb[ps_, 0:16], Vr[:, 0, qi, :, :].rearrange("p a b -> p (a b)"), xsb_r[:, qi, :], True, False, (t_w, t_xs), (t_bank[3],), tp=(0, 32 * qi))
                k.mm(yb[ps_, 0:16], Vi[:, 0, qi, :, :].rearrange("p a b -> p (a b)"), xsb_i[:, qi, :], False, True, (t_w, t_xs), (t_bank[3],), tp=(0, 32 * qi))
            k.stt(yfs[:, :], u[:, T:TS], dvec[:, f:f + 1], yb[:, 0:16], ALU.mult, ALU.add, (t_uTb[s_], t_l, t_bank[3]), (t_xs,))
            k.act(zS[:, f, :], yfs[:, :], AF.Gelu_apprx_tanh, (t_xs,), (t_zS,))

        ds_so = P.dsem("s5out")
        with nc.allow_non_contiguous_dma(reason="small state relayout"):
            for g2 in range(2):
                hs = slice(g2 * 64, (g2 + 1) * 64)
                k.dma(s5_p.rearrange("(q g) p r -> g p q r", g=2)[g2], s5st_p[hs, :, :], ds_so, (t_st,), (), allow_slow_non_contiguous=True)
        k.fence([t_w, t_t12], [t_big])
        for ri in range(2):
            pb = bank[ri]
            pbv = pb[0:16, :].rearrange("s (q x) -> s q x", q=4)
            for q0 in range(0, 32, 4):
                for ql in range(4):
                    k.tr(pbv[:, ql, :], s5st_s[:, q0 + ql, ri, :], ident_f[:, :], (t_st, t_c2), (t_bank[ri],))
                k.cp(bview[:, q0:q0 + 4, ri:256:2], pbv, (t_bank[ri],), (t_big,), eng="act")
        k.dma(s5_s[:, :], big[:, :], ds_so, (t_big,), ())

        k.barrier()
        k.ptr = glu_base
        zT = k.sb("zT", [128, 8, T], BF16)
        t_zT = Tok()
        ds_zl = P.dsem("zload")
        for f in range(8):
            k.dma(zT[:, f, :], zT_d[f], ds_zl, (t_zTd,), (t_zT,))
        sgl = [k.sb("sgl%d" % i, [128, TS], BF16) for i in range(2)]
        t_sgl = [Tok(), Tok()]
        ds_sgl = [P.dsem("sgl0"), P.dsem("sgl1")]
        sg = [f32t("sg%d" % i, [128, 512]) for i in range(2)]
        t_sg = [Tok(), Tok()]
        g_ct = [0]
        for fo in range(8):
            s_ = fo % 2
            wg, twg = wload(glu_w[:, fo * 128:(fo + 1) * 128], 8, 128)
            k.dma(sgl[s_][:, :], sgbT_d[fo], ds_sgl[s_], (t_sgbTd,), (t_sgl[s_],))
            for b in range(5):
                n = 512 if b < 4 else NS
                c0 = b * 512
                bi = 4 + b % 2
                for kc in range(8):
                    rhs = zT[:, kc, c0:c0 + n] if b < 4 else zS[:, kc, :]
                    k.mm(bank[bi][:, 0:n], wg[:, kc, :], rhs, kc == 0, kc == 7, (twg, t_zT, t_zS), (t_bank[bi],))
                gi = g_ct[0] % 2
                g_ct[0] += 1
                k.act(sg[gi][:, 0:n], bank[bi][:, 0:n], AF.Sigmoid, (t_bank[bi], t_l), (t_sg[gi],), bias=gbvec[:, fo:fo + 1])
                zsrc = zT[:, fo, c0:c0 + n] if b < 4 else zS[:, fo, :]
                k.tt(sg[gi][:, 0:n], sg[gi][:, 0:n], zsrc, ALU.mult, (t_sg[gi], t_zT, t_zS), (t_sg[gi],))
                if b < 4:
                    cs = gi
                    k.tt(zst[cs][:, :], sg[gi][:, :], sgl[s_][:, c0:c0 + 512], ALU.mult, (t_sg[gi], t_sgl[s_]), (t_zst[cs],))
                    k.dma(catT[8 + fo, :, c0:c0 + 512], zst[cs][:, :], ds_zst[cs], (t_zst[cs],), (t_catT,))
                else:
                    k.tt(catS[:, 8 + fo, :], sg[gi][:, 0:NS], sgl[s_][:, T:TS], ALU.mult, (t_sg[gi], t_sgl[s_]), (t_catS,))

    if dbg.get("s5", 1):
        l0c()


    w_out_even = din("w_out_even", [D, D])
    x1_d = nc.dram_tensor("x1_d", [TS, D], F32).ap()
    t_x1d = Tok()

    def outproj(w_out, nkc, actT_d, actS, t_actTd, t_actS, resp, ress, r_toks, dstp, dsts, t_dst, nm):
        k.barrier()
        k.ptr = hT_base
        nh = nkc // 16
        actR = [k.sb(nm + "actR%d" % h, [128, 16, TS], BF16) for h in range(nh)]
        t_actR = Tok()
        ds_a = P.dsem(nm + "actR")
        k.ptr = max(k.ptr, region)
        xin = [k.sb(nm + "xin%d" % i, [128, 256], F32) for i in range(3)]
        xo = [k.sb(nm + "xo%d" % i, [128, 256], F32) for i in range(2)]
        t_xin, t_xo = [Tok() for _ in range(3)], [Tok(), Tok()]
        ds_xin, ds_xo = [P.dsem(nm + "xin%d" % i) for i in range(3)], [P.dsem(nm + "xo0"), P.dsem(nm + "xo1")]
        for h in range(nh):
            for ft in range(16):
                k.dma(actR[h][:, ft, 0:T], actT_d[h * 16 + ft], ds_a, (t_actTd,), (t_actR,))
            k.cp(actR[h][:, :, T:TS], actS[:, h * 16:(h + 1) * 16, :], (t_actS,), (t_actR,))
        steps = [(db, i) for db in range(8) for i in range(NT + 1)]

        def load_res(n):
            db, i = steps[n]
            np_ = 128 if i < NT else NS
            c0 = i * 128
            rsrc = resp[c0:c0 + 128, db * 256:(db + 1) * 256] if i < NT else ress[:, db * 256:(db + 1) * 256]
            k.dma(xin[n % 3][:np_, :], rsrc, ds_xin[n % 3], tuple(r_toks), (t_xin[n % 3],))

        load_res(0)
        load_res(1)
        slabs = None
        for n, (db, i) in enumerate(steps):
            if i == 0:
                slabs = [wload(w_out[h * 2048:(h + 1) * 2048, db * 256:(db + 1) * 256], 16, 256) for h in range(nh)]
            if n + 2 < len(steps):
                load_res(n + 2)
            np_ = 128 if i < NT else NS
            c0 = i * 128
            s_ = n % 2
            bk, tbk = bank[s_], t_bank[s_]
            for h in range(nh):
                slab, tslab = slabs[h]
                for kc in range(16):
                    k.mm(bk[:np_, 0:256], actR[h][:, kc, c0:c0 + np_], slab[:, kc, :], h == 0 and kc == 0, h == nh - 1 and kc == 15,
                         (t_actR, tslab), (tbk,))
            k.tt(xo[s_][:np_, :], bk[:np_, 0:256], xin[n % 3][:np_, :], ALU.add, (tbk, t_xin[n % 3]), (t_xo[s_],))
            dd = dstp[c0:c0 + 128, db * 256:(db + 1) * 256] if i < NT else dsts[:, db * 256:(db + 1) * 256]
            k.dma(dd, xo[s_][:np_, :], ds_xo[s_], (t_xo[s_],), (t_dst,))

    def l1a():
        k.barrier()
        k.ptr = region
        nonlocal_alloc = {}
        return nonlocal_alloc

    if dbg.get("l0d", 1) and dbg.get("s5", 1):
        outproj(w_out_even, 16, catT, catS, t_catT, t_catS, xp, xs, (), x1_d[0:T, :], x1_d[T:TS, :], t_x1d, "od0")
        k.barrier()
        k.ptr = region
        nw_bc = k.sb("nw_bc1", [128, D], F32)
        xt = [k.sb("xt1_%d" % i, [128, D], F32) for i in range(2)]
        hb = [k.sb("hb1_%d" % i, [128, D], BF16) for i in range(2)]
        junk = k.sb("junk1", [128, D], BF16)
        ss = k.sb("ss1", [128, NT + 1], F32)
        rstd = k.sb("rstd1", [128, NT + 1], F32)
        l0a(1, x1_d, x1_d[T:TS, :], rtoks=(t_x1d,))
    if "x1" in dbg:
        o = dout("dbg_x1", [TS, D], F32)
        dsd3 = P.dsem("dbg3")
        k.dma(o, x1_d, dsd3, (t_x1d,), ())
        o = dout("dbg_hT", [128, KC * TS], BF16)
        k.dma(o, hT[:].rearrange("p a b -> p (a b)"), dsd3, (t_hT,), ())


    w_in_odd = din("w_in_odd", [D, 10304])
    conv_w = din("conv_w", [4, 6144])
    conv_b = din("conv_b", [1, 6144])
    dt_bias = din("dt_bias", [1, 64])
    a_log = din("a_log", [1, 64])
    ssd_d = din("ssd_d", [1, 64])
    ssd_nw = din("ssd_nw", [1, 4096])
    w_out_odd = din("w_out_odd", [4096, D])
    y_p = dout("y_p", [T, D])
    y_s = dout("y_s", [NS, D])
    conv_p = dout("conv_p", [3, 6144])
    ssd_p = dout("ssd_p", [64 * 64, 128])
    st_conv = din("st_conv", [NS, 3, 6144])
    st_ssd = din("st_ssd", [NS, 4096, 128])
    conv_s = dout("conv_s", [NS, 3, 6144])
    ssd_s = dout("ssd_s", [NS, 4096, 128])
    ynT_d = nc.dram_tensor("ynT_d", [32, 128, T], BF16).ap()
    t_ynTd = Tok()
    ynS = k.sb("ynS", [128, 32, NS], BF16) if False else None

    def l1():
        k.barrier()
        k.ptr = region
        f32t = lambda name, shape: k.sb(name, shape, F32)
        bft = lambda name, shape: k.sb(name, shape, BF16)
        ds_l1 = P.dsem("l1ld")
        Uf, onesf, ident_f = f32t("Uf", [128, 128]), f32t("onesf", [128, 128]), f32t("identf1", [128, 128])
        t_c3 = Tok()
        k.dma(Uf[:], cst[:, C_U:C_U + 128], ds_l1, (), (t_c3,))
        k.dma(onesf[:], cst[:, C_ONE:C_ONE + 128], ds_l1, (), (t_c3,))
        k.dma(ident_f[:], cst[:, C_ID:C_ID + 128], ds_l1, (), (t_c3,))
        cm1 = bft("cm1", [128, 128])
        esel = bft("esel8", [8, 8, 128])
        k.dma(cm1[:], cst[:, C_CM:C_CM + 128], ds_l1, (), (t_c3,), eng="pool")
        for j in range(8):
            k.dma(esel[:, j, :], cst[0:8, C_ES + j * 128:C_ES + (j + 1) * 128], ds_l1, (), (t_c3,), eng="pool")
        dtb_bc, A_bc, D_bc = f32t("dtb_bc", [128, 64]), f32t("A_bc", [128, 64]), f32t("D_bc", [128, 64])
        k.dma(dtb_bc[:], dt_bias[0:1, :].partition_broadcast(128), ds_l1, (), (t_c3,))
        k.dma(A_bc[:], a_log[0:1, :].partition_broadcast(128), ds_l1, (), (t_c3,))
        k.dma(D_bc[:], ssd_d[0:1, :].partition_broadcast(128), ds_l1, (), (t_c3,))
        k.act(A_bc[:], A_bc[:], AF.Exp, (t_c3,), (t_c3,))
        k.ts(A_bc[:], A_bc[:], -1.0, None, ALU.mult, None, (t_c3,), (t_c3,))
        cw, cb = f32t("cw", [128, 4, 48]), f32t("cb", [128, 48])
        for k4 in range(4):
            k.dma(cw[:, k4, :], conv_w[k4:k4 + 1, :].rearrange("o (t p) -> p (o t)", p=128), ds_l1, (), (t_c3,), allow_slow_non_contiguous=True)
        k.dma(cb[:], conv_b.rearrange("o (t p) -> p (o t)", p=128), ds_l1, (), (t_c3,), allow_slow_non_contiguous=True)
        convst = f32t("convst", [128, 48, 3])
        t_convst = Tok()
        NTT = NT + 1
        dtv, nacs, atot, av = f32t("dtv", [128, NTT, 64]), f32t("nacs", [128, NT, 64]), f32t("atot", [128, NT, 64]), f32t("av", [128, NT, 64])
        t_sc = Tok()
        wdt, twdt = wload(w_in_odd[:, 10240:10304], KC, 64)
        for i in range(NTT):
            np_ = 128 if i < NT else NS
            c0 = i * 128
            bk, tbk = bank[i % 2], t_bank[i % 2]
            for kc in range(KC):
                k.mm(bk[:np_, 0:64], hT[:, kc, c0:c0 + np_], wdt[:, kc, :], kc == 0, kc == KC - 1, (t_hT, twdt), (tbk,))
            k.tt(dtv[:np_, i, :], bk[:np_, 0:64], dtb_bc[:np_, :], ALU.add, (tbk, t_c3), (t_sc,))
        dflat = dtv[:].rearrange("p a b -> p (a b)")
        k.act(dflat, dflat, AF.Exp, (t_sc,), (t_sc,))
        k.act(dflat, dflat, AF.Ln, (t_sc,), (t_sc,), bias=1.0)
        k.tt(av[:], dtv[:, 0:NT, :], A_bc[:, :].unsqueeze(1).broadcast_to([128, NT, 64]), ALU.mult, (t_sc, t_c3), (t_sc,))
        for c in range(NT):
            bk, tbk = bank[c % 2], t_bank[c % 2]
            k.mm(bk[:, 0:64], Uf[:, :], av[:, c, :], True, True, (t_c3, t_sc), (tbk,))
            k.mm(bk[:, 64:128], onesf[:, :], av[:, c, :], True, True, (t_c3, t_sc), (tbk,))
            k.ts(nacs[:, c, :], bk[:, 0:64], -1.0, None, ALU.mult, None, (tbk,), (t_sc,))
            k.cp(atot[:, c, :], bk[:, 64:128], (tbk,), (t_sc,), eng="act")
        xbcS, xcS, szS = f32t("xbcS", [128, 48, NS]), f32t("xcS", [128, 48, NS]), f32t("szS", [128, 32, NS])
        t_xS = Tok()
        cbuf0 = f32t("cbuf0", [NS, 3, 128])
        cbuf = [cbuf0, cbuf0]
        t_cbuf0 = Tok()
        t_cbuf = [t_cbuf0, t_cbuf0]
        ds_cbuf0 = P.dsem("cbuf0")
        ds_cbuf = [ds_cbuf0, ds_cbuf0]
        cacc = f32t("cacc", [128, NS])
        t_cacc = Tok()
        l1s_base = k.ptr
        xcT = bft("xcT", [128, 6, T])
        t_xcT = Tok()
        xpre0 = bft("xpre0", [128, 3 + T])
        xpre = [xpre0, xpre0]
        t_xp0 = Tok()
        t_xpre = [t_xp0, t_xp0]
        k.memset(xpre0[:, 0:3], 0.0, (t_xp0,))
        dg0 = bft("dg0", [128, 4, 128])
        dg = [dg0, dg0]
        t_dg0 = Tok()
        t_dg = [t_dg0, t_dg0]
        sz = bft("sz", [128, NT, 512])
        t_sz = Tok()
        acsT = f32t("acsT", [8, 512])
        acsh, acsl = bft("acsh", [8, NT, 128]), bft("acsl", [8, NT, 128])
        t_acsT = Tok()
        nwg = f32t("nwg", [128, 512])
        t_nwg = Tok()
        ds_nwg = P.dsem("nwg")
        tok0 = bft("tok0", [128, 5, 128])
        tok = [tok0, tok0]
        t_tok0 = Tok()
        t_tok = [t_tok0, t_tok0]
        Lt = [f32t("Lt%d" % i, [128, 128]) for i in range(2)]
        t_Lt = [Tok(), Tok()]
        Mt0 = bft("Mt0", [128, 8, 128])
        Mt = [Mt0, Mt0]
        t_Mt0 = Tok()
        t_Mt = [t_Mt0, t_Mt0]
        xdt_, xdd_, xsD_ = [bft("xdt%d" % i, [128, 8, 64]) for i in range(2)], [bft("xdd%d" % i, [128, 8, 64]) for i in range(2)], [bft("xsD%d" % i, [128, 8, 64]) for i in range(2)]
        t_xd_ = [Tok(), Tok()]
        eac, decc, etc_ = f32t("eac", [128, 8]), f32t("decc", [128, 8]), f32t("etc", [128, 8])
        t_ec = Tok()
        yv, ygt = f32t("yv", [128, 512]), f32t("ygt", [128, 512])
        t_yv = Tok()
        ssn = f32t("ssn", [128, 2])
        ynb0 = bft("ynb0", [128, 512])
        ynb = [ynb0, ynb0]
        t_ynb0 = Tok()
        t_ynb = [t_ynb0, t_ynb0]
        ynst0 = bft("ynst0", [128, 4, 128])
        ynst = [ynst0, ynst0]
        t_ynst0 = Tok()
        t_ynst = [t_ynst0, t_ynst0]
        ds_ynst0 = P.dsem("ynst0")
        ds_ynst = [ds_ynst0, ds_ynst0]
        Hs, Hb = f32t("Hs", [128, 512]), bft("Hb", [128, 512])
        t_H = Tok()
        hout = yv[:, :].rearrange("p (a b) -> p a b", a=4)
        t_hout = t_yv
        ds_hout = P.dsem("hout")
        junk2 = bft("junk2", [128, 512])
        t_junk2 = Tok()
        pstb = pst[0]
        for g in (dbg["l1only"] if "l1only" in dbg else range(int(dbg.get("l1groups", 8)))):
            k.dma(nwg[:], ssd_nw[0:1, 512 * g:512 * (g + 1)].partition_broadcast(128), ds_nwg, (), (t_nwg,))
            tiles = [(4096 + 512 * g + 128 * j, 4 * g + j) for j in range(4)] + [(8192 + 128 * g, 32 + g), (9216 + 128 * g, 40 + g)]
            slabs = {}
            for ti, (col, tidx) in enumerate(tiles):
                if ti in (0, 2):
                    sl_, tsl_ = wload(w_in_odd[:, col:col + 256], KC, 256)
                    slabs[ti] = (sl_, tsl_, 0)
                    slabs[ti + 1] = (sl_, tsl_, 128)
                elif ti >= 4:
                    sl_, tsl_ = wload(w_in_odd[:, col:col + 128], KC, 128)
                    slabs[ti] = (sl_, tsl_, 0)
                wv, tw, off = slabs[ti]
                xs_ = ti % 2
                xp_ = xpre[xs_]
                for k4 in range(4):
                    k.ts(dg[xs_][:, k4, :], ident[:, :], cw[:, k4, tidx:tidx + 1], None, ALU.mult, None, (t_const, t_c3), (t_dg[xs_],))
                for b in range(4):
                    bk, tbk = bank[b % 2], t_bank[b % 2]
                    for kc in range(KC):
                        k.mm(bk[:, :], wv[:, kc, off:off + 128], hT[:, kc, b * 512:(b + 1) * 512], kc == 0, kc == KC - 1, (t_hT, tw), (tbk,))
                    k.cp(xp_[:, 3 + b * 512:3 + (b + 1) * 512], bk[:, :], (tbk,), (t_xpre[xs_],), eng="act")
                    if b == 3:
                        k.cp(convst[:, tidx, :], bk[:, 509:512], (tbk,), (t_convst,))
                for b in range(4):
                    bk, tbk = bank[2 + b % 2], t_bank[2 + b % 2]
                    for k4 in range(4):
                        k.mm(bk[:, :], dg[xs_][:, k4, :], xp_[:, b * 512 + k4:b * 512 + k4 + 512], k4 == 0, k4 == 3, (t_dg[xs_], t_xpre[xs_]), (tbk,))
                    k.act(xcT[:, ti, b * 512:(b + 1) * 512], bk[:, :], AF.Silu, (tbk, t_c3), (t_xcT,), bias=cb[:, tidx:tidx + 1])
                bk, tbk = bank[5], t_bank[5]
                for kc in range(KC):
                    k.mm(bk[:, 0:NS], wv[:, kc, off:off + 128], hT[:, kc, T:TS], kc == 0, kc == KC - 1, (t_hT, tw), (tbk,))
                k.cp(xbcS[:, tidx, :], bk[:, 0:NS], (tbk,), (t_xS,))
                c2 = tidx % 2
                k.dma(cbuf[c2][:, :, :], st_conv[:, :, tidx * 128:(tidx + 1) * 128], ds_cbuf[c2], (), (t_cbuf[c2],))
                for k3 in range(3):
                    k.tr(bk[:, 64 + k3 * NS:64 + (k3 + 1) * NS], cbuf[c2][:, k3, :], ident_f[0:NS, 0:NS], (t_cbuf[c2], t_c3), (tbk,))
                k.ts(cacc[:, :], bk[:, 64:64 + NS], cw[:, 0, tidx:tidx + 1], None, ALU.mult, None, (tbk, t_c3), (t_cacc,))
                for k3 in (1, 2):
                    k.stt(cacc[:, :], bk[:, 64 + k3 * NS:64 + (k3 + 1) * NS], cw[:, k3, tidx:tidx + 1], cacc[:, :], ALU.mult, ALU.add,
                          (tbk, t_c3, t_cacc), (t_cacc,))
                k.stt(cacc[:, :], xbcS[:, tidx, :], cw[:, 3, tidx:tidx + 1], cacc[:, :], ALU.mult, ALU.add, (t_xS, t_c3, t_cacc), (t_cacc,))
                k.act(xcS[:, tidx, :], cacc[:, :], AF.Silu, (t_cacc, t_c3), (t_xS,), bias=cb[:, tidx:tidx + 1])
            for zh in range(2):
                wz, twz = wload(w_in_odd[:, 512 * g + 256 * zh:512 * g + 256 * (zh + 1)], KC, 256)
                for c in range(NT):
                    bk, tbk = bank[c % 2], t_bank[c % 2]
                    for kc in range(KC):
                        k.mm(bk[:, 0:256], hT[:, kc, c * 128:(c + 1) * 128], wz[:, kc, :], kc == 0, kc == KC - 1, (t_hT, twz), (tbk,))
                    k.act(sz[:, c, 256 * zh:256 * (zh + 1)], bk[:, 0:256], AF.Silu, (tbk,), (t_sz,))
                for jj in range(2):
                    bk, tbk = bank[5], t_bank[5]
                    for kc in range(KC):
                        k.mm(bk[:, 0:NS], wz[:, kc, jj * 128:(jj + 1) * 128], hT[:, kc, T:TS], kc == 0, kc == KC - 1, (t_hT, twz), (tbk,))
                    k.act(szS[:, 4 * g + 2 * zh + jj, :], bk[:, 0:NS], AF.Silu, (tbk,), (t_xS,))
            for c4 in range(0, NT, 4):
                bk, tbk = bank[4], t_bank[4]
                for cl in range(4):
                    k.mm(bk[0:8, cl * 128:(cl + 1) * 128], av[:, c4 + cl, 8 * g:8 * g + 8], Uf[:, :], True, True, (t_sc, t_c3), (tbk,))
                hi = acsh[:, c4:c4 + 4, :].rearrange("p a b -> p (a b)")
                lo = acsl[:, c4:c4 + 4, :].rearrange("p a b -> p (a b)")
                k.cp(hi, bk[0:8, :], (tbk,), (t_acsT,))
                k.cp(acsT[:, :], hi, (t_acsT,), (t_acsT,))
                k.tt(acsT[:, :], bk[0:8, :], acsT[:, :], ALU.subtract, (tbk, t_acsT), (t_acsT,))
                k.cp(lo, acsT[:, :], (t_acsT,), (t_acsT,))
            for c in range(NT if g < int(dbg.get("l1p2", 8)) else 0):
                cs_ = c % 2
                cc = slice(c * 128, (c + 1) * 128)
                hs8 = slice(8 * g, 8 * g + 8)
                xdt, xdd, xsD, t_xd = xdt_[cs_], xdd_[cs_], xsD_[cs_], t_xd_[cs_]
                for j in range(5):
                    k.tr(pstb[:, j, :], xcT[:, j, cc], ident[:, :], (t_xcT, t_const), (t_pst[0],))
                k.cp(tok[cs_][:, :, :], pstb[:, 0:5, :], (t_pst[0],), (t_tok[cs_],), eng="act")
                xs_tok = tok[cs_][:, 0:4, :].rearrange("p a (h q) -> p (a h) q", q=64)
                k.act(eac[:, :], nacs[:, c, hs8], AF.Exp, (t_sc,), (t_ec,), scale=-1.0)
                k.tt(decc[:, :], atot[:, c, hs8], nacs[:, c, hs8], ALU.add, (t_sc,), (t_ec,))
                k.act(decc[:, :], decc[:, :], AF.Exp, (t_ec,), (t_ec,))
                k.act(etc_[:, :], atot[:, c, hs8], AF.Exp, (t_sc,), (t_ec,))
                bc8 = lambda t: t[:, :].unsqueeze(2).broadcast_to([128, 8, 64])
                k.tt(xdt[:], xs_tok, dtv[:, c, hs8].unsqueeze(2).broadcast_to([128, 8, 64]), ALU.mult, (t_tok[cs_], t_sc), (t_xd,), eng=POOLC)
                k.tt(xdd[:], xdt[:], bc8(decc), ALU.mult, (t_xd, t_ec), (t_xd,), eng=POOLC)
                k.tt(xsD[:], xs_tok, D_bc[:, hs8].unsqueeze(2).broadcast_to([128, 8, 64]), ALU.mult, (t_tok[cs_], t_c3), (t_xd,), eng=POOLC)
                bG, tG = bank[4], t_bank[4]
                k.mm(bG[:, 0:128], xcT[:, 4, cc], xcT[:, 5, cc], True, True, (t_xcT,), (tG,))
                for j in range(8):
                    bR, tR = bank[2 + j % 2], t_bank[2 + j % 2]
                    k.mm(bR[:, 0:128], esel[0:8, j, :], acsh[:, c, :], True, False, (t_c3, t_acsT), (tR,))
                    k.mm(bR[:, 0:128], esel[0:8, j, :], acsl[:, c, :], False, False, (t_c3, t_acsT), (tR,))
                    k.mm(bR[:, 0:128], ident[:, :], cm1[:, :], False, True, (t_const, t_c3), (tR,))
                    k.act(Lt[j % 2][:, :], bR[:, 0:128], AF.Exp, (tR, t_sc), (t_Lt[j % 2],), bias=nacs[:, c, 8 * g + j:8 * g + j + 1])
                    k.tt(Mt[cs_][:, j, :], bG[:, 0:128], Lt[j % 2][:, :], ALU.mult, (tG, t_Lt[j % 2]), (t_Mt[cs_],))
                bY, tY = bank[0], t_bank[0]
                for j in range(8):
                    js = slice(64 * j, 64 * (j + 1))
                    k.mm(bY[:, js], Mt[cs_][:, j, :], xdt[:, j, :], True, False, (t_Mt[cs_], t_xd), (tY,))
                    k.mm(bY[:, js], ident[:, :], xsD[:, j, :], False, True, (t_const, t_xd), (tY,))
                if c > 0:
                    bO, tO = bank[1], t_bank[1]
                    k.mm(bO[:, :], xcT[:, 5, cc], Hb[:, :], True, True, (t_xcT, t_H), (tO,))
                    k.tt(yv[:, :].rearrange("p (h q) -> p h q", q=64), bO[:, :].rearrange("p (h q) -> p h q", q=64), bc8(eac), ALU.mult,
                         (tO, t_ec), (t_yv,))
                    k.tt(yv[:, :], yv[:, :], bY[:, :], ALU.add, (t_yv, tY), (t_yv,))
                    k.tt(ygt[:, :], yv[:, :], sz[:, c, :], ALU.mult, (t_yv, t_sz), (t_yv,))
                else:
                    k.tt(ygt[:, :], bY[:, :], sz[:, c, :], ALU.mult, (tY, t_sz), (t_yv,))
                k.act(junk2[:, :], ygt[:, :], AF.Square, (t_yv,), (t_junk2, t_yv), accum=ssn[:, 0:1])
                k.ts(ssn[:, 1:2], ssn[:, 0:1], 1.0 / 512, EPS, ALU.mult, ALU.add, (t_yv,), (t_yv,))
                k.act(ssn[:, 1:2], ssn[:, 1:2], AF.Ln, (t_yv,), (t_yv,))
                k.act(ssn[:, 1:2], ssn[:, 1:2], AF.Exp, (t_yv,), (t_yv,), scale=-0.5)
                k.stt(ynb[cs_][:, :], ygt[:, :], ssn[:, 1:2], nwg[:, :], ALU.mult, ALU.mult, (t_yv, t_nwg), (t_ynb[cs_],))
                for j in range(4):
                    k.tr(pst[1][:, j, :], ynb[cs_][:, j * 128:(j + 1) * 128], ident[:, :], (t_ynb[cs_], t_const), (t_pst[1],))
                k.cp(ynst[cs_][:, :, :], pst[1][:, 0:4, :], (t_pst[1],), (t_ynst[cs_],), eng="act")
                k.dma(ynT_d[4 * g:4 * g + 4, :, cc].rearrange("a p t -> p a t"), ynst[cs_][:, :, :], ds_ynst[cs_], (t_ynst[cs_],), (t_ynTd,))
                bS, tS = bank[5], t_bank[5]
                k.mm(bS[:, :], tok[cs_][:, 4, :], xdd[:].rearrange("p h q -> p (h q)"), True, True, (t_tok[cs_], t_xd), (tS,))
                if c > 0:
                    k.tt(Hs[:, :].rearrange("p (h q) -> p h q", q=64), Hs[:, :].rearrange("p (h q) -> p h q", q=64), bc8(etc_), ALU.mult,
                         (t_H, t_ec), (t_H,), eng=POOLC)
                    k.tt(Hs[:, :], Hs[:, :], bS[:, :], ALU.add, (t_H, tS), (t_H,))
                else:
                    k.cp(Hs[:, :], bS[:, :], (tS,), (t_H,))
                if c < NT - 1:
                    k.cp(Hb[:, :], Hs[:, :], (t_H,), (t_H,), eng="act")
            if g >= int(dbg.get("l1p2", 8)):
                continue
            bT, tT = bank[4], t_bank[4]
            for j in range(4):
                k.tr(bT[:, j * 128:(j + 1) * 128], Hs[:, j * 128:(j + 1) * 128], ident_f[:, :], (t_H, t_c3), (tT,))
            k.cp(yv[:, :], bT[:, :], (tT,), (t_hout,), eng="act")
            k.dma(ssd_p[512 * g:512 * (g + 1), :].rearrange("(a p) n -> p a n", p=128), hout, ds_hout, (t_hout,), ())
        for k3 in range(3):
            k.dma(conv_p[k3:k3 + 1, :].rearrange("o (t p) -> p (o t)", p=128), convst[:, :, k3], ds_hout, (t_convst,), (), allow_slow_non_contiguous=True)

        k.barrier()
        k.ptr = l1s_base
        ds_ss = P.dsem("ss_ld")
        t_pl = Tok()
        D_pl, nw_pl = f32t("D_pl", [128, 32]), f32t("nw_pl", [128, 32])
        for h2 in range(2):
            hs = slice(h2 * 64, (h2 + 1) * 64)
            k.dma(D_pl[hs, :], ssd_d.rearrange("o (hp hh) -> hh o hp", hh=2)[h2].partition_broadcast(64), ds_ss, (), (t_pl,),
                  allow_slow_non_contiguous=True)
        k.dma(nw_pl[:, :], ssd_nw.rearrange("o (t p) -> p (o t)", p=128), ds_ss, (), (t_pl,), allow_slow_non_contiguous=True)
        k.dma(conv_s[:, 0:2, :], st_conv[:, 1:3, :], ds_ss, (), ())
        cst_ = [f32t("cst_%d" % i, [NS, 512]) for i in range(2)]
        t_cst = [Tok(), Tok()]
        ds_cst = [P.dsem("cst0"), P.dsem("cst1")]
        for r4 in range(12):
            bk, tbk = bank[r4 % 2], t_bank[r4 % 2]
            for j in range(4):
                k.tr(bk[0:NS, j * 128:(j + 1) * 128], xbcS[:, 4 * r4 + j, :], ident_f[:, :], (t_xS, t_c3), (tbk,))
            k.cp(cst_[r4 % 2][:, :], bk[0:NS, :], (tbk,), (t_cst[r4 % 2],), eng="act")
            k.dma(conv_s[:, 2, r4 * 512:(r4 + 1) * 512], cst_[r4 % 2][:, :], ds_cst[r4 % 2], (t_cst[r4 % 2],), ())
        a_s, dec_s = f32t("a_s", [NS, 64]), f32t("dec_s", [NS, 64])
        dexp = f32t("dexp", [NS, 32, NS])
        dt_pl, dec_pl, xdt_pl = f32t("dt_pl", [128, 32, NS]), f32t("dec_pl", [128, 32, NS]), f32t("xdt_pl", [128, 32, NS])
        k.tt(a_s[:, :], dtv[:NS, NT, :], A_bc[:NS, :], ALU.mult, (t_sc, t_c3), (t_pl,))
        k.act(dec_s[:, :], a_s[:, :], AF.Exp, (t_pl,), (t_pl,))
        id16 = ident_f[0:NS, 0:NS].unsqueeze(1).broadcast_to([NS, 32, NS])
        for src, dst, rt in ((dtv[:NS, NT, :], dt_pl, (t_sc,)), (dec_s[:, :], dec_pl, (t_pl,))):
            bk, tbk = bank[2], t_bank[2]
            for h2 in range(2):
                k.tt(dexp[:], src.rearrange("p (hp hh) -> p hp hh", hh=2)[:, :, h2:h2 + 1].broadcast_to([NS, 32, NS]), id16, ALU.mult,
                     rt + (t_c3,), (t_pl,))
                k.mm(bk[h2 * 64:(h2 + 1) * 64, :], onesf[0:NS, 0:64], dexp[:].rearrange("p a b -> p (a b)"), True, True, (t_c3, t_pl), (tbk,),
                     tp=(0, 64 * h2))
            k.cp(dst[:].rearrange("p a b -> p (a b)"), bk[:, :], (tbk,), (t_pl,))
        k.tt(xdt_pl[:], xcS[:, 0:32, :], dt_pl[:], ALU.mult, (t_xS, t_pl), (t_pl,))
        hst0 = f32t("hst0", [128, 32, 128])
        hst = [hst0, hst0]
        t_hst0 = Tok()
        t_hst = [t_hst0, t_hst0]
        ds_hst0 = P.dsem("hst0")
        ds_hst = [ds_hst0, ds_hst0]
        tmpS = f32t("tmpS", [128, 32, 128])
        t_tmpS = Tok()
        dB = f32t("dB", [128, 8, 128])
        t_dB = Tok()
        Bbc, Cbc = f32t("Bbc", [128, 8, 128]), f32t("Cbc", [128, 8, 128])
        t_bc = Tok()
        y_pl = f32t("y_pl", [128, 32, NS])
        t_ypl = Tok()
        idb = ident_f[:, :].unsqueeze(1).broadcast_to([128, 8, 128])
        for s_ in range(NS):
            hs_ = s_ % 2
            H = hst[hs_]
            k.dma(H[:, :, :], st_ssd[s_].rearrange("(hp q) n -> q hp n", q=128), ds_hst[hs_], (), (t_hst[hs_],))
            for which, dstbc in ((32, Bbc), (40, Cbc)):
                k.tt(dB[:], xcS[:, which:which + 8, s_:s_ + 1].broadcast_to([128, 8, 128]), idb, ALU.mult, (t_xS, t_c3), (t_dB,))
                for hf in range(2):
                    bk, tbk = bank[3 + hf], t_bank[3 + hf]
                    k.mm(bk[:, :], onesf[:, :], dB[:, 4 * hf:4 * hf + 4, :].rearrange("p a b -> p (a b)"), True, True, (t_c3, t_dB), (tbk,))
                    k.cp(dstbc[:, 4 * hf:4 * hf + 4, :].rearrange("p a b -> p (a b)"), bk[:, :], (tbk,), (t_bc,), eng="act")
            v4 = lambda t: t[:].rearrange("p (g a) n -> p g a n", a=4)
            k.tt(v4(tmpS), Bbc[:].unsqueeze(2).broadcast_to([128, 8, 4, 128]),
                 xdt_pl[:, :, s_:s_ + 1].rearrange("p (g a) o -> p g a o", a=4).broadcast_to([128, 8, 4, 128]), ALU.mult, (t_bc, t_pl), (t_tmpS,))
            k.tt(H[:], H[:], dec_pl[:, :, s_:s_ + 1].broadcast_to([128, 32, 128]), ALU.mult, (t_hst[hs_], t_pl), (t_hst[hs_],))
            k.tt(H[:], H[:], tmpS[:], ALU.add, (t_hst[hs_], t_tmpS), (t_hst[hs_],))
            k.dma(ssd_s[s_].rearrange("(hp q) n -> q hp n", q=128), H[:, :, :], ds_hst[hs_], (t_hst[hs_],), ())
            k.tt(v4(tmpS), v4(H), Cbc[:].unsqueeze(2).broadcast_to([128, 8, 4, 128]), ALU.mult, (t_hst[hs_], t_bc), (t_tmpS,))
            k.red(y_pl[:, :, s_], tmpS[:], ALU.add, (t_tmpS,), (t_ypl,))
        ygS, sqS = f32t("ygS", [128, 32, NS]), f32t("sqS", [128, 32, NS])
        ssS, rsS = f32t("ssS", [128, 8, NS]), f32t("rsS", [128, 8, NS])
        k.tt(ygS[:], xcS[:, 0:32, :], D_pl[:, :].unsqueeze(2).broadcast_to([128, 32, NS]), ALU.mult, (t_xS, t_pl), (t_ypl,))
        k.tt(ygS[:], ygS[:], y_pl[:], ALU.add, (t_ypl,), (t_ypl,))
        k.tt(ygS[:], ygS[:], szS[:], ALU.mult, (t_ypl, t_xS), (t_ypl,))
        k.tt(sqS[:], ygS[:], ygS[:], ALU.mult, (t_ypl,), (t_ypl,))
        bk, tbk = bank[0], t_bank[0]
        k.mm(bk[:, :], onesf[:, :], sqS[:].rearrange("p a b -> p (a b)"), True, True, (t_c3, t_ypl), (tbk,))
        k.red(ssS[:], bk[:, :].rearrange("p (g a s) -> p g s a", g=8, a=4), ALU.add, (tbk,), (t_ypl,))
        k.ts(ssS[:].rearrange("p a b -> p (a b)"), ssS[:].rearrange("p a b -> p (a b)"), 1.0 / 512, EPS, ALU.mult, ALU.add, (t_ypl,), (t_ypl,))
        k.act(rsS[:].rearrange("p a b -> p (a b)"), ssS[:].rearrange("p a b -> p (a b)"), AF.Ln, (t_ypl,), (t_ypl,))
        k.act(rsS[:].rearrange("p a b -> p (a b)"), rsS[:].rearrange("p a b -> p (a b)"), AF.Exp, (t_ypl,), (t_ypl,), scale=-0.5)
        k.tt(ygS[:].rearrange("p (g a) s -> p g a s", a=4), ygS[:].rearrange("p (g a) s -> p g a s", a=4),
             rsS[:].unsqueeze(2).broadcast_to([128, 8, 4, NS]), ALU.mult, (t_ypl,), (t_ypl,))
        k.tt(ynSd[:], ygS[:], nw_pl[:, :].unsqueeze(2).broadcast_to([128, 32, NS]), ALU.mult, (t_ypl, t_pl), (t_ynSd,))

    if dbg.get("l1", 1) and dbg.get("l0d", 1) and dbg.get("s5", 1):
        l1()
        outproj(w_out_odd, 32, ynT_d, ynSd, t_ynTd, t_ynSd, x1_d[0:T, :], x1_d[T:TS, :], (t_x1d,), y_p, y_s, Tok(), "od1")

    if "catT" in dbg:
        o = dout("dbg_catT", [16, 128, T], BF16)
        dsd2 = P.dsem("dbg2")
        k.dma(o, catT, dsd2, (t_catT,), ())
    if "hT" in dbg:
        o = dout("dbg_hT", [128, KC * TS], BF16)
        ds = P.dsem("dbg")
        k.dma(o, hT[:].rearrange("p a b -> p (a b)"), ds, (t_hT,), ())

    with ExitStack() as es:
        P.emit(es)
    return nc


def _core_inputs(c, a, cst, rope):
    s0, s1 = NS * c, NS * (c + 1)
    f = np.ascontiguousarray
    m = {
        "xp": f(a["x_prompt"][c]), "xs": f(a["x_sample"][s0:s1, 0]), "norm_w": f(a["norm_w"]),
        "w_in_even": f(a["w_in_even"][0]), "q_norm_w": f(a["q_norm_w"]), "k_norm_w": f(a["k_norm_w"]),
        "cst": cst, "rope": rope,
        "lam_re": f(a["s5_lambda_re"][0]), "lam_im": f(a["s5_lambda_im"][0]), "log_dt": f(a["s5_log_dt"]),
        "s5_b": f(a["s5_b"][0]).reshape(64, 64, 32), "s5_c": f(a["s5_c"][0]).reshape(64, 16, 128), "s5_d": f(a["s5_d"]),
        "glu_w": f(a["s5_glu_w"][0]), "glu_b": f(a["s5_glu_b"]), "st_s5": f(a["state_s5"][0, s0:s1]).reshape(NS, 8192),
        "w_out_even": f(a["w_out_even"][0]), "w_in_odd": f(a["w_in_odd"][0]), "conv_w": f(a["conv_w"][0]),
        "conv_b": f(a["conv_b"]), "dt_bias": f(a["ssd_dt_bias"]), "a_log": f(a["ssd_a_log"]), "ssd_d": f(a["ssd_d"]),
        "ssd_nw": f(a["ssd_norm_w"]), "w_out_odd": f(a["w_out_odd"][0]),
        "cache_k": a["cache_k"][0].reshape(2560 * 128, 1024), "cache_v": a["cache_v"][0].reshape(2560 * 128, 1024),
        "page_table": f(a["page_table"][s0:s1]).reshape(1, NS * 16).astype(np.int32),
        "st_conv": f(a["state_conv"][0, s0:s1]), "st_ssd": f(a["state_ssd"][0, s0:s1]).reshape(NS, 4096, 128),
    }
    return m


def kernel(**inputs):
    a = {k_: np.asarray(v) for k_, v in inputs.items()}
    nc = build()
    cst, rope = host_consts(), host_rope()
    in_maps = [_core_inputs(c, a, cst, rope) for c in range(NCORES)]
    names = set()
    for alloc in nc.allocations:
        try:
            if alloc.kind == "ExternalInput":
                names.add(alloc.memorylocations[0].name)
        except Exception:
            pass
    if names:
        in_maps = [{k_: v for k_, v in m.items() if k_ in names} for m in in_maps]
    res = run_bass_kernel_spmd(nc, in_maps, core_ids=list(range(NCORES))).results
    B = NCORES
    g = lambda name: [np.asarray(r[name]) for r in res]
    z = lambda *shape: np.zeros(shape, np.float32)
    y_prompt = np.stack(g("y_p"), 0)
    y_sample = np.concatenate(g("y_s"), 0).reshape(B * NS, 1, D)
    k_prompt = np.stack(g("k_p"), 0).reshape(1, B, T, 8, 128)
    v_prompt = np.stack(g("v_p"), 0).reshape(1, B, T, 8, 128)
    k_sample = np.concatenate(g("k_s"), 0).reshape(1, B * NS, 1, 8, 128)
    v_sample = np.concatenate(g("v_s"), 0).reshape(1, B * NS, 1, 8, 128)
    s5_prompt = np.stack(g("s5_p"), 0).reshape(1, B, 64, 64, 2)
    s5_sample = np.concatenate(g("s5_s"), 0).reshape(1, B * NS, 64, 64, 2)
    conv_prompt = np.stack(g("conv_p"), 0).reshape(1, B, 3, 6144)
    ssd_prompt = np.stack(g("ssd_p"), 0).reshape(1, B, 64, 64, 128)
    if "conv_s" in res[0]:
        conv_sample = np.concatenate(g("conv_s"), 0).reshape(1, B * NS, 3, 6144)
        ssd_sample = np.concatenate(g("ssd_s"), 0).reshape(1, B * NS, 64, 64, 128)
    else:
        conv_sample, ssd_sample = z(1, B * NS, 3, 6144), z(1, B * NS, 64, 64, 128)
    return (y_prompt, y_sample, k_prompt, v_prompt, k_sample, v_sample, s5_prompt, s5_sample,
            conv_prompt, conv_sample, ssd_prompt, ssd_sample)
```

```python
import numpy as np
from contextlib import ExitStack
import concourse.bass as bass
import concourse.mybir as mybir
from concourse.bass_utils import run_bass_kernel_spmd

F32 = mybir.dt.float32
BF16 = mybir.dt.bfloat16
I32 = mybir.dt.int32
AF = mybir.ActivationFunctionType
ALU = mybir.AluOpType
AX = mybir.AxisListType

NCORES = 8
T = 2048
NT = T // 128
NS = 16
TS = T + NS
D = 2048
KC = D // 128
EPS = 1e-6
NEG = -30000.0
ENGS = ("pe", "act", "dve", "pool", "sp")
EPOCH = 700
POOLC = "dve"


class Tok:
    __slots__ = ("w", "r")
    registry = []

    def __init__(self):
        self.w = None
        self.r = {}
        Tok.registry.append(self)


class DSem:
    __slots__ = ("h", "cnt", "name")

    def __init__(self, name):
        self.name = name
        self.h = None
        self.cnt = 0


class Op:
    __slots__ = ("eng", "fn", "waits", "sig", "signo", "dsem", "idx")


class Prog:
    def __init__(self, nc):
        self.nc = nc
        self.ops = {e: [] for e in ENGS}
        self.seen = {e: {} for e in ENGS}
        self.dsems = []
        Tok.registry = []
        self.t_phase = Tok()

    def barrier(self, eng, fn):
        toks = list(Tok.registry)
        return self.add(eng, fn, (), toks)

    def dsem(self, name):
        d = DSem(name)
        self.dsems.append(d)
        return d

    def add(self, eng, fn, reads=(), writes=(), dsem=None):
        op = Op()
        op.eng = eng
        op.fn = fn
        op.sig = False
        op.signo = 0
        op.dsem = dsem
        op.idx = len(self.ops[eng])
        deps = []
        if self.t_phase.w is not None:
            deps.append(self.t_phase.w)
        for t in reads:
            if t.w is not None:
                deps.append(t.w)
        for t in writes:
            if t.w is not None:
                deps.append(t.w)
            deps.extend(t.r.values())
        seen = self.seen[eng]
        waits = {}
        for d in deps:
            if d[0] == "e":
                o2 = d[1]
                if o2.eng == "pe" and eng == "pe":
                    continue
                key = ("e", o2.eng)
                val = o2.idx
            else:
                key = ("d", d[1])
                val = d[2]
            if seen.get(key, -1) >= val:
                continue
            if key not in waits or waits[key][0] < val:
                waits[key] = (val, d)
        op.waits = []
        for key, (val, d) in waits.items():
            seen[key] = val
            if d[0] == "e":
                d[1].sig = True
            op.waits.append(d)
        if dsem is not None:
            dsem.cnt += 16
            me = ("d", dsem, dsem.cnt)
            mkey = ("d", dsem)
        else:
            me = ("e", op)
            mkey = ("e", eng)
        for t in reads:
            t.r[mkey] = me
        for t in writes:
            t.w = me
            t.r = {}
        self.ops[eng].append(op)
        return op

    def emit(self, es):
        nc = self.nc
        sems = {}
        for e in ENGS:
            n = 0
            for op in self.ops[e]:
                if op.sig:
                    n += 1
                    op.signo = n
            print("[prog] %s: ops=%d signals=%d waits=%d" % (e, len(self.ops[e]), n, sum(len(o.waits) for o in self.ops[e])))
            nep = max(1, (n + EPOCH - 1) // EPOCH)
            sems[e] = [es.enter_context(nc.semaphore("s_%s_%d" % (e, i))) for i in range(nep)]
        print("[prog] dsems=%d engine_sems=%d maxdcnt=%d" % (len(self.dsems), sum(len(v) for v in sems.values()), max(d.cnt for d in self.dsems)))
        for d in self.dsems:
            d.h = es.enter_context(nc.semaphore("d_" + d.name))
        prog = self

        def run(ename, eng):
            for op in prog.ops[ename]:
                for d in op.waits:
                    if d[0] == "e":
                        s = d[1].signo
                        ep = (s - 1) // EPOCH
                        eng.wait_ge(sems[d[1].eng][ep], s - ep * EPOCH)
                    else:
                        eng.wait_ge(d[1].h, d[2])
                ins = op.fn(eng)
                if op.dsem is not None:
                    ins.then_inc(op.dsem.h, 16)
                elif op.sig:
                    ep = (op.signo - 1) // EPOCH
                    ins.then_inc(sems[ename][ep], 1)
            if ename == "sp":
                for d in prog.dsems:
                    if d.cnt:
                        eng.wait_ge(d.h, d.cnt)

        with nc.Block() as block:
            @block.tensor
            def _(eng):
                run("pe", eng)

            @block.scalar
            def _(eng):
                run("act", eng)

            @block.vector
            def _(eng):
                run("dve", eng)

            @block.gpsimd
            def _(eng):
                run("pool", eng)

            @block.sync
            def _(eng):
                run("sp", eng)


class K:
    def __init__(self, nc):
        self.nc = nc
        self.P = Prog(nc)
        self.ptr = self.SB_BASE
        self.nalloc = 0
        self.fdummy = self.sb("fdummy", [128, 8], F32)
        self.t_fd = Tok()

    SB_BASE = 16512
    SB_TOP = 229344

    def sb(self, name, shape, dt):
        esz = {F32: 4, BF16: 2, I32: 4}[dt]
        n = 1
        for d in shape[1:]:
            n *= d
        nbytes = (n * esz + 31) // 32 * 32
        off = self.ptr
        assert off + nbytes <= self.SB_TOP, "SBUF overflow at %s: %d" % (name, off + nbytes)
        self.ptr = off + nbytes
        self.nalloc += 1
        return self.nc.alloc_sbuf_tensor_at("%s_%d" % (name, self.nalloc), list(shape), dt, offset=off)

    def barrier(self, eng="dve"):
        d = self.fdummy
        return self.P.barrier(eng, lambda e: e.memset(d[0:1, 0:1], 0.0))

    def fence(self, old, new, eng="dve"):
        d = self.fdummy
        return self.P.add(eng, lambda e: e.memset(d[0:1, 0:1], 0.0), (), tuple(old) + tuple(new) + (self.t_fd,))

    def ps(self, name, shape, dt=F32):
        return self.nc.alloc_psum_tensor(name, list(shape), dt)

    def mm(self, out, lhsT, rhs, start, stop, r, w, tp=None):
        if tp is not None:
            return self.P.add("pe", lambda e: e.matmul(out, lhsT, rhs, start=start, stop=stop, tile_position=tp), r, w)
        return self.P.add("pe", lambda e: e.matmul(out, lhsT, rhs, start=start, stop=stop), r, w)

    def tr(self, out, in_, ident, r, w):
        return self.P.add("pe", lambda e: e.transpose(out, in_, ident), r, w)

    def act(self, out, in_, func, r, w, bias=None, scale=None, accum=None, eng="act"):
        kw = {}
        if bias is not None:
            kw["bias"] = bias
        if scale is not None:
            kw["scale"] = scale
        if accum is not None:
            kw["accum_out"] = accum
        return self.P.add(eng, lambda e: e.activation(out=out, in_=in_, func=func, **kw), r, w)

    def tt(self, out, in0, in1, op, r, w, eng="dve"):
        return self.P.add(eng, lambda e: e.tensor_tensor(out=out, in0=in0, in1=in1, op=op), r, w)

    def ts(self, out, in0, s1, s2, op0, op1, r, w, eng="dve", accum=None):
        kw = {}
        if accum is not None:
            kw["accum_out"] = accum
        if op1 is None:
            return self.P.add(eng, lambda e: e.tensor_scalar(out=out, in0=in0, scalar1=s1, scalar2=None, op0=op0, **kw), r, w)
        return self.P.add(eng, lambda e: e.tensor_scalar(out=out, in0=in0, scalar1=s1, scalar2=s2, op0=op0, op1=op1, **kw), r, w)

    def stt(self, out, in0, scalar, in1, op0, op1, r, w, eng="dve"):
        return self.P.add(eng, lambda e: e.scalar_tensor_tensor(out=out, in0=in0, scalar=scalar, in1=in1, op0=op0, op1=op1), r, w)

    def cp(self, out, in_, r, w, eng="dve"):
        if eng == "act":
            return self.P.add(eng, lambda e: e.copy(out=out, in_=in_), r, w)
        return self.P.add(eng, lambda e: e.tensor_copy(out=out, in_=in_), r, w)

    def red(self, out, in_, op, r, w, eng="dve", axis=AX.X):
        return self.P.add(eng, lambda e: e.tensor_reduce(out=out, in_=in_, axis=axis, op=op), r, w)

    def memset(self, ap, val, w, eng="dve"):
        return self.P.add(eng, lambda e: e.memset(ap, val), (), w)

    def dma(self, out, in_, dsem, r, w, eng="sp", **kw):
        return self.P.add(eng, lambda e: e.dma_start(out=out, in_=in_, **kw), r, w, dsem=dsem)


C_ID = 0
C_CM = C_ID + 128
C_ES = C_CM + 4 * 512
C_ONE = C_ES + 16 * 128
C_BT = C_ONE + 128
C_MM = C_BT + 2048
C_OWN = C_MM + 8 * 16
C_EV = C_OWN + 8 * 16
C_NV = C_EV + 17
C_U = C_NV + 128
C_PI = C_U + 128
C_DG = C_PI + 1
C_END = C_DG + 128


def host_consts():
    c = np.zeros((128, C_END), np.float32)
    c[:, C_ID:C_ID + 128] = np.eye(128, dtype=np.float32)
    key = np.arange(128)[:, None]
    q = np.arange(512)[None, :]
    for j in range(4):
        c[:, C_CM + j * 512:C_CM + (j + 1) * 512] = np.where(128 * j + key <= q, 0.0, NEG)
    for r in range(16):
        c[r, C_ES + r * 128:C_ES + (r + 1) * 128] = 1.0
    c[:, C_ONE:C_ONE + 128] = 1.0
    own = np.arange(T) // 256
    for j in range(2):
        for n in range(8):
            c[j * 8 + n, C_BT:C_BT + T] = np.where(n <= own, 0.0, NEG)
    for m in range(8):
        for j in range(2):
            for n in range(8):
                c[:, C_MM + m * 16 + j * 8 + n] = 0.0 if n < m else -1e30
                c[:, C_OWN + m * 16 + j * 8 + n] = 1.0 if n == m else 0.0
    c[:, C_EV:C_EV + 17] = np.arange(17, dtype=np.float32)[None, :]
    c[:, C_NV:C_NV + 128] = np.arange(128, dtype=np.float32)[None, :]
    c[:, C_U:C_U + 128] = (np.arange(128)[:, None] <= np.arange(128)[None, :]).astype(np.float32)
    c[:, C_PI] = np.arange(128, dtype=np.float32)
    for hh in range(8):
        for j in range(16):
            c[hh, C_DG + j * 8 + hh] = 1.0
    return c


def host_rope():
    half = 64
    inv = (10000.0 ** (-np.arange(half, dtype=np.float32) / half)).astype(np.float32)
    pos = np.arange(T + 1, dtype=np.float32)
    ang = pos[:, None] * inv[None, :]
    cos = np.cos(ang).astype(np.float32)
    sin = np.sin(ang).astype(np.float32)
    cc = np.concatenate([cos, cos], axis=1)
    ss = np.concatenate([-sin, sin], axis=1)
    return np.ascontiguousarray(np.stack([cc, ss], axis=1))


def build(dbg=None):
    nc = bass.Bass("TRN2", target_bir_lowering=False)
    k = K(nc)
    P = k.P
    dbg = dbg or {}

    def din(name, shape, dt=F32):
        return nc.dram_tensor(name, list(shape), dt, kind="ExternalInput").ap()

    def dout(name, shape, dt=F32):
        return nc.dram_tensor(name, list(shape), dt, kind="ExternalOutput").ap()

    xp = din("xp", [T, D])
    xs = din("xs", [NS, D])
    norm_w = din("norm_w", [2, D])
    w_in_even = din("w_in_even", [D, 6144])
    q_norm_w = din("q_norm_w", [1, 128])
    k_norm_w = din("k_norm_w", [1, 128])
    cst = din("cst", [128, C_END])
    rope = din("rope", [T + 1, 2, 128])

    k_p = dout("k_p", [T, 1024])
    v_p = dout("v_p", [T, 1024])
    k_s = dout("k_s", [NS, 1024])
    v_s = dout("v_s", [NS, 1024])

    ident = k.sb("ident", [128, 128], BF16)
    ones_bf = k.sb("ones_bf", [128, 128], BF16)
    t_const = Tok()
    ds_c = P.dsem("const")
    k.dma(ident[:], cst[:, C_ID:C_ID + 128], ds_c, (), (t_const,), eng="pool")
    k.dma(ones_bf[:], cst[:, C_ONE:C_ONE + 128], ds_c, (), (t_const,), eng="pool")

    NSLOT = 4
    wslot = [k.sb("wslot%d" % i, [128, KC * 256], BF16) for i in range(NSLOT)]
    catS = k.sb("catS", [128, 16, NS], BF16)
    ynSd = k.sb("ynSd", [128, 32, NS], BF16)
    t_ynSd = Tok()
    hT_base = k.ptr
    hT = k.sb("hT", [128, KC, TS], BF16)
    t_hT = Tok()
    region = k.ptr
    nw_bc = k.sb("nw_bc", [128, D], F32)
    t_nw = Tok()
    k.dma(nw_bc[:], norm_w[0:1, :].partition_broadcast(128), ds_c, (), (t_nw,))
    xt = [k.sb("xt%d" % i, [128, D], F32) for i in range(2)]
    t_xt = [Tok(), Tok()]
    ds_x = [P.dsem("x0"), P.dsem("x1")]
    hb = [k.sb("hb%d" % i, [128, D], BF16) for i in range(2)]
    t_hb = [Tok(), Tok()]
    junk = k.sb("junk", [128, D], BF16)
    t_junk = Tok()
    ss = k.sb("ss", [128, NT + 1], F32)
    rstd = k.sb("rstd", [128, NT + 1], F32)
    t_ss = [Tok() for _ in range(NT + 1)]
    pst = [k.ps("pst%d" % i, [128, 8, 128], BF16) for i in range(2)]
    t_pst = [Tok(), Tok()]

    def l0a(norm_row, srcp, srcs, rtoks=()):
        if norm_row:
            k.dma(nw_bc[:], norm_w[norm_row:norm_row + 1, :].partition_broadcast(128), ds_c, (), (t_nw,))
        for i in range(NT + 1):
            s = i % 2
            np_ = 128 if i < NT else NS
            src = srcp[i * 128:(i + 1) * 128, :] if i < NT else srcs
            k.dma(xt[s][:np_, :], src, ds_x[s], tuple(rtoks), (t_xt[s],))
            k.act(junk[:np_, :], xt[s][:np_, :], AF.Square, (t_xt[s],), (t_junk, t_ss[i]), accum=ss[:np_, i:i + 1])
            k.ts(rstd[:np_, i:i + 1], ss[:np_, i:i + 1], 1.0 / D, EPS, ALU.mult, ALU.add, (t_ss[i],), (t_ss[i],))
            k.act(rstd[:np_, i:i + 1], rstd[:np_, i:i + 1], AF.Ln, (t_ss[i],), (t_ss[i],))
            k.act(rstd[:np_, i:i + 1], rstd[:np_, i:i + 1], AF.Exp, (t_ss[i],), (t_ss[i],), scale=-0.5)
            k.stt(hb[s][:np_, :], xt[s][:np_, :], rstd[:np_, i:i + 1], nw_bc[:np_, :], ALU.mult, ALU.mult,
                  (t_xt[s], t_ss[i], t_nw), (t_hb[s],))
            for half in range(2):
                for j in range(8):
                    kc = half * 8 + j
                    k.tr(pst[half][:, j, :np_], hb[s][:np_, kc * 128:(kc + 1) * 128], ident[:np_, :np_],
                         (t_hb[s], t_const), (t_pst[half],))
                c0 = i * 128
                k.cp(hT[:, half * 8:(half + 1) * 8, c0:c0 + np_], pst[half][:, :, :np_], (t_pst[half],), (t_hT,),
                     eng="act" if half == 0 else "dve")

    l0a(0, xp, xs[:, :])


    t_wslot = [Tok() for _ in range(NSLOT)]
    ds_w = [P.dsem("w%d" % i) for i in range(NSLOT)]
    wctr = [0]

    def wload(src, kc, width):
        sl = wctr[0] % NSLOT
        wctr[0] += 1
        view = wslot[sl][:, 0:kc * width].rearrange("p (a b) -> p a b", a=kc)
        k.dma(view, src.rearrange("(a p) c -> p a c", p=128), ds_w[sl], (), (t_wslot[sl],), eng="pool")
        return view, t_wslot[sl]

    bank = [k.ps("bank%d" % i, [128, 512], F32) for i in range(6)]
    t_bank = [Tok() for _ in range(6)]

    catT = nc.dram_tensor("catT", [16, 128, T], BF16).ap()
    t_catT = Tok()
    t_catS = Tok()

    l0a_toks = [t_nw, t_junk] + t_xt + t_hb
    k.ptr = region
    qs_f = k.sb("qs_f", [NS, 8, 128], F32)
    ks_f = k.sb("ks_f", [NS, 8, 128], F32)
    vs_f = k.sb("vs_f", [NS, 8, 128], F32)
    t_qkvs = Tok()
    sgaS = k.sb("sgaS", [128, 8, NS], BF16)
    t_sgaS = Tok()
    ds_ks = P.dsem("ks_out")
    sattn_base = k.ptr
    qkw4 = k.sb("qkw4", [128, 4, 128], F32)
    cm = k.sb("cm", [128, 4, 512], BF16)
    esel = k.sb("esel", [16, 16, 128], BF16)
    qkw = k.sb("qkw", [128, 2, 128], F32)
    biasT0 = k.sb("biasT0", [16, T], BF16)
    mmask = k.sb("mmask", [128, 8, 16], F32)
    ownm = k.sb("ownm", [128, 8, 16], F32)
    t_acst = Tok()
    t_qkw = Tok()
    ropet = [k.sb("ropet%d" % i, [128, 2, 128], F32) for i in range(2)]
    t_ropet = [Tok(), Tok()]
    ds_rope = [P.dsem("rope0"), P.dsem("rope1")]
    sq_, ss4_, tn_, ut_, vt_ = [], [], [], [], []
    tq_ = {"sq": [], "ss4": [], "tn": [], "ut": [], "vt": []}
    for i2 in range(2):
        sq_.append(k.sb("sq%d" % i2, [128, 512], F32))
        ss4_.append(k.sb("ss4%d" % i2, [128, 4], F32))
        tn_.append(k.sb("tn%d" % i2, [128, 4, 128], F32))
        ut_.append(k.sb("ut%d" % i2, [128, 4, 128], F32))
        vt_.append(k.sb("vt%d" % i2, [128, 4, 128], F32))
        for kk in tq_:
            tq_[kk].append(Tok())
    qkb = [k.sb("qkb%d" % i, [128, 4, 128], BF16) for i in range(2)]
    t_qkb = [Tok(), Tok()]
    kf = [k.sb("kf%d" % i, [128, 256], F32) for i in range(2)]
    t_kf = [Tok(), Tok()]
    ds_kf = [P.dsem("kf0"), P.dsem("kf1")]
    vf = [k.sb("vf%d" % i, [128, 256], F32) for i in range(2)]
    t_vf = [Tok(), Tok()]
    ds_vf = [P.dsem("vf0"), P.dsem("vf1")]
    qkT = k.sb("qkT", [128, 4, T], BF16)
    t_qkT = Tok()
    Vg = k.sb("Vg", [128, NT, 256], BF16)
    t_Vg = Tok()
    sga = k.sb("sga", [128, 2, T], BF16)
    t_sga = Tok()

    l0b_toks = tq_["sq"] + tq_["ss4"] + tq_["tn"] + tq_["ut"] + tq_["vt"] + [t_qkT, t_Vg, t_sga, t_qkvs, t_sgaS] + t_ropet + t_qkb + t_kf + t_vf
    t_qkw4 = Tok()
    l0b_toks.append(t_qkw4)

    def l0b_proj(g, wq, wk, wv, tq, tk, tv):
        for i in range(NT + 1):
            s = i % 2
            np_ = 128 if i < NT else NS
            c0 = i * 128
            bq = bank[s]
            bv = bank[2]
            sq, ss4, tn, ut, vt = sq_[s], ss4_[s], tn_[s], ut_[s], vt_[s]
            t_sq, t_ss4, t_tn, t_ut, t_vt = tq_["sq"][s], tq_["ss4"][s], tq_["tn"][s], tq_["ut"][s], tq_["vt"][s]
            def rope_load(ii):
                ss_ = ii % 2
                if ii < NT:
                    k.dma(ropet[ss_][:, :, :], rope[ii * 128:(ii + 1) * 128, :, :], ds_rope[ss_], (), (t_ropet[ss_],))
                else:
                    k.dma(ropet[ss_][:NS, :, :], rope[T:T + 1, :, :].partition_broadcast(NS), ds_rope[ss_], (), (t_ropet[ss_],))
            if i == 0:
                rope_load(0)
            if i + 1 <= NT:
                rope_load(i + 1)
            for kc in range(KC):
                k.mm(bq[:np_, 0:256], hT[:, kc, c0:c0 + np_], wq[:, kc, :], kc == 0, kc == KC - 1, (t_hT, tq), (t_bank[s],))
            for kc in range(KC):
                k.mm(bq[:np_, 256:512], hT[:, kc, c0:c0 + np_], wk[:, kc, :], kc == 0, kc == KC - 1, (t_hT, tk), (t_bank[s],))
            for kc in range(KC):
                k.mm(bv[:np_, s * 256:(s + 1) * 256], hT[:, kc, c0:c0 + np_], wv[:, kc, :], kc == 0, kc == KC - 1, (t_hT, tv), (t_bank[2],))
            k.act(sq[:np_, :], bq[:np_, :], AF.Square, (t_bank[s],), (t_sq,))
            k.red(ss4[:np_, :], sq[:np_, :].rearrange("p (a b) -> p a b", a=4), ALU.add, (t_sq,), (t_ss4,))
            k.act(ss4[:np_, :], ss4[:np_, :], AF.Ln, (t_ss4,), (t_ss4,), bias=128.0 * EPS)
            k.act(ss4[:np_, :], ss4[:np_, :], AF.Exp, (t_ss4,), (t_ss4,), scale=-0.5)
            k.tt(tn[:np_], bq[:np_, :].rearrange("p (a b) -> p a b", a=4),
                 ss4[:np_, :].unsqueeze(2).broadcast_to([np_, 4, 128]), ALU.mult, (t_bank[s], t_ss4), (t_tn,))
            k.tt(tn[:np_], tn[:np_], qkw4[:np_], ALU.mult, (t_tn, t_qkw4), (t_tn,))
            cc = ropet[s][:np_, 0:1, :].broadcast_to([np_, 4, 128])
            k.tt(ut[:np_], tn[:np_], cc, ALU.mult, (t_tn, t_ropet[s]), (t_ut,))
            k.tt(vt[:np_, :, 0:64], tn[:np_, :, 64:128], ropet[s][:np_, 1:2, 0:64].broadcast_to([np_, 4, 64]), ALU.mult,
                 (t_tn, t_ropet[s]), (t_vt,))
            k.tt(vt[:np_, :, 64:128], tn[:np_, :, 0:64], ropet[s][:np_, 1:2, 64:128].broadcast_to([np_, 4, 64]), ALU.mult,
                 (t_tn, t_ropet[s]), (t_vt,))
            if i < NT:
                k.tt(qkb[s][:, 0:2, :], ut[:, 0:2, :], vt[:, 0:2, :], ALU.add, (t_ut, t_vt), (t_qkb[s],))
                kfv = kf[s][:, :].rearrange("p (a b) -> p a b", a=2)
                k.tt(kfv, ut[:, 2:4, :], vt[:, 2:4, :], ALU.add, (t_ut, t_vt), (t_kf[s],))
                k.cp(qkb[s][:, 2:4, :], kfv, (t_kf[s],), (t_qkb[s],), eng="act")
                k.dma(k_p[c0:c0 + 128, g * 256:(g + 1) * 256], kf[s][:, :], ds_kf[s], (t_kf[s],), ())
                k.cp(vf[s][:, :], bv[:, s * 256:(s + 1) * 256], (t_bank[2],), (t_vf[s],), eng="act")
                k.dma(v_p[c0:c0 + 128, g * 256:(g + 1) * 256], vf[s][:, :], ds_vf[s], (t_vf[s],), ())
                k.cp(Vg[:, i, :], vf[s][:, :], (t_vf[s],), (t_Vg,), eng="act")
                for j in range(4):
                    k.tr(pst[s][:, j, :], qkb[s][:, j, :], ident[:, :], (t_qkb[s], t_const), (t_pst[s],))
                k.cp(qkT[:, :, c0:c0 + 128], pst[s][:, 0:4, :], (t_pst[s],), (t_qkT,), eng="act")
            else:
                k.tt(qs_f[:, 2 * g:2 * g + 2, :], ut[:NS, 0:2, :], vt[:NS, 0:2, :], ALU.add, (t_ut, t_vt), (t_qkvs,))
                k.tt(ks_f[:, 2 * g:2 * g + 2, :], ut[:NS, 2:4, :], vt[:NS, 2:4, :], ALU.add, (t_ut, t_vt), (t_qkvs,))
                k.cp(vs_f[:, 2 * g:2 * g + 2, :], bv[:NS, s * 256:(s + 1) * 256].rearrange("p (a b) -> p a b", a=2),
                     (t_bank[2],), (t_qkvs,), eng="act")

    def l0b_gate(g, wga, tga):
        for j in range(2):
            for b in range(5):
                c0 = b * 512
                n = 512 if b < 4 else NS
                bk = bank[3 + (b % 2)]
                tb = t_bank[3 + (b % 2)]
                for kc in range(KC):
                    k.mm(bk[:, 0:n], wga[:, kc, j * 128:(j + 1) * 128], hT[:, kc, c0:c0 + n], kc == 0, kc == KC - 1, (t_hT, tga), (tb,))
                if b < 4:
                    k.act(sga[:, j, c0:c0 + n], bk[:, 0:n], AF.Silu, (tb,), (t_sga,))
                else:
                    k.act(sgaS[:, 2 * g + j, :], bk[:, 0:n], AF.Silu, (tb,), (t_sgaS,))

    biasT = k.sb("biasT", [16, T], BF16)
    t_biasT = Tok()
    km = k.sb("km", [128, 2, 8], F32)
    kmf = k.sb("kmf", [128, 2, 8], F32)
    kmh = k.sb("kmh", [128, 2, 8], BF16)
    kml = k.sb("kml", [128, 2, 8], BF16)
    t_km = Tok()
    sbk = k.sb("sbk", [128, 16], F32)
    mx8 = k.sb("mx8", [128, 2, 8], F32)
    selt = k.sb("selt", [128, 16], F32)
    gbias = k.sb("gbias", [128, 16], BF16)
    t_gate = Tok()
    pT = [k.sb("pT%d" % i, [128, 512], BF16) for i in range(2)]
    t_pT = [Tok(), Tok()]
    rden = k.sb("rden", [128, 512], F32)
    attf = k.sb("attf", [128, 512], F32)
    t_att = Tok()
    catst = [k.sb("catst%d" % i, [128, 512], BF16) for i in range(2)]
    t_catst = [Tok(), Tok()]
    ds_cat = [P.dsem("cat0"), P.dsem("cat1")]
    l0b_toks += [t_biasT, t_km, t_gate, t_att] + t_pT + t_catst
    actr = [0]

    def l0b_gates(g):
        k.red(km[:], qkT[:, 2:4, :].rearrange("p j (n c) -> p j n c", n=8), ALU.add, (t_qkT,), (t_km,))
        k.cp(kmh[:], km[:], (t_km,), (t_km,))
        k.cp(kmf[:], kmh[:], (t_km,), (t_km,))
        k.tt(kmf[:], km[:], kmf[:], ALU.subtract, (t_km,), (t_km,))
        k.cp(kml[:], kmf[:], (t_km,), (t_km,))
        k.cp(biasT[:, :], biasT0[:, :], (t_const, t_acst,), (t_biasT,))
        for i in range(8, NT):
            m = i // 2
            c0 = i * 128
            sbp = bank[5]
            for j in range(2):
                k.mm(sbp[:, j * 8:(j + 1) * 8], qkT[:, j, c0:c0 + 128], kmh[:, j, :], True, False, (t_qkT, t_km), (t_bank[5],))
                k.mm(sbp[:, j * 8:(j + 1) * 8], qkT[:, j, c0:c0 + 128], kml[:, j, :], False, True, (t_qkT, t_km), (t_bank[5],))
            k.tt(sbk[:, :], sbp[:, 0:16], mmask[:, m, :], ALU.add, (t_bank[5], t_const, t_acst), (t_gate,))
            for j in range(2):
                P.add("dve", (lambda jj: (lambda e: e.max(out=mx8[:, jj, :], in_=sbk[:, jj * 8:(jj + 1) * 8])))(j), (t_gate,), (t_gate,))
            k.tt(selt[:, :].rearrange("p (a b) -> p a b", a=2), sbk[:, :].rearrange("p (a b) -> p a b", a=2),
                 mx8[:, :, 2:3].broadcast_to([128, 2, 8]), ALU.is_ge, (t_gate,), (t_gate,))
            k.tt(selt[:, :], selt[:, :], ownm[:, m, :], ALU.add, (t_gate, t_const, t_acst), (t_gate,))
            k.ts(gbias[:, :], selt[:, :], -1.0, -NEG, ALU.add, ALU.mult, (t_gate,), (t_gate,))
            s = i % 2
            k.tr(pst[s][:16, 0, :], gbias[:, :], ident[:, :], (t_gate, t_const, t_acst), (t_pst[s],))
            k.cp(biasT[:, c0:c0 + 128], pst[s][:16, 0, :], (t_pst[s],), (t_biasT,), eng="act")

    def l0b_attn(g):
        for j in range(2):
            for c in range(4):
                q0 = c * 512
                ai = actr[0] % 2
                actr[0] += 1
                A = bank[3 + ai]
                tA = t_bank[3 + ai]
                Dn = bank[2 if ai == 0 else 5]
                tD = t_bank[2 if ai == 0 else 5]
                nk = 4 * c + 4
                for kt in range(nk):
                    s = kt % 2
                    st = bank[s]
                    diag = kt >= 4 * c
                    k.mm(st[:, :], qkT[:, 2 + j, kt * 128:(kt + 1) * 128], qkT[:, j, q0:q0 + 512], True, False, (t_qkT,), (t_bank[s],))
                    k.mm(st[:, :], esel[:, j * 8 + kt // 2, :], biasT[:, q0:q0 + 512], False, not diag, (t_const, t_acst, t_biasT), (t_bank[s],))
                    if diag:
                        k.mm(st[:, :], ident[:, :], cm[:, kt - 4 * c, :], False, True, (t_const, t_acst,), (t_bank[s],))
                    k.act(pT[s][:, :], st[:, :], AF.Exp, (t_bank[s],), (t_pT[s],))
                    k.mm(A[:, :], Vg[:, kt, j * 128:(j + 1) * 128], pT[s][:, :], kt == 0, kt == nk - 1, (t_Vg, t_pT[s]), (tA,))
                    k.mm(Dn[:, :], ones_bf[:, :], pT[s][:, :], kt == 0, kt == nk - 1, (t_const, t_acst, t_pT[s]), (tD,))
                P.add("dve", (lambda d: (lambda e: e.reciprocal(out=rden[:, :], in_=d[:, :])))(Dn), (tD,), (t_att,))
                k.tt(attf[:, :], A[:, :], rden[:, :], ALU.mult, (tA, t_att), (t_att,))
                cs = (2 * g + j + c) % 2
                k.tt(catst[cs][:, :], attf[:, :], sga[:, j, q0:q0 + 512], ALU.mult, (t_att, t_sga), (t_catst[cs],))
                k.dma(catT[2 * g + j, :, q0:q0 + 512], catst[cs][:, :], ds_cat[cs], (t_catst[cs],), (t_catT,))

    def wcols(c0, w):
        return w_in_even[:, c0:c0 + w]

    l0b_toks += [t_acst, t_qkw]
    k.fence(l0a_toks, l0b_toks)
    k.dma(biasT0[:], cst[0:16, C_BT:C_BT + T], ds_c, (), (t_acst,), eng="pool")
    k.dma(mmask[:], cst[:, C_MM:C_MM + 128].rearrange("p (a b) -> p a b", a=8), ds_c, (), (t_acst,))
    k.dma(ownm[:], cst[:, C_OWN:C_OWN + 128].rearrange("p (a b) -> p a b", a=8), ds_c, (), (t_acst,))
    k.dma(cm[:], cst[:, C_CM:C_CM + 2048].rearrange("p (j q) -> p j q", j=4), ds_c, (), (t_acst,), eng="pool")
    k.dma(esel[:], cst[0:16, C_ES:C_ES + 2048].rearrange("p (j q) -> p j q", j=16), ds_c, (), (t_acst,), eng="pool")
    k.dma(qkw[:, 0, :], q_norm_w[0:1, :].partition_broadcast(128), ds_c, (), (t_qkw,))
    k.dma(qkw[:, 1, :], k_norm_w[0:1, :].partition_broadcast(128), ds_c, (), (t_qkw,))
    k.ts(qkw[:, 1, :], qkw[:, 1, :], float(np.sqrt(128.0)), None, ALU.mult, None, (t_qkw,), (t_qkw,))
    for j in range(4):
        k.cp(qkw4[:, j, :], qkw[:, j // 2, :], (t_qkw,), (t_qkw4,))
    for g in range(int(dbg.get("ngroups", 4))):
        wq, tq = wload(wcols(256 * g, 256), KC, 256)
        wk, tk = wload(wcols(1024 + 256 * g, 256), KC, 256)
        wv, tv = wload(wcols(2048 + 256 * g, 256), KC, 256)
        wga, tga = wload(wcols(3072 + 256 * g, 256), KC, 256)
        l0b_proj(g, wq, wk, wv, tq, tk, tv)
        l0b_gate(g, wga, tga)
        l0b_gates(g)
        l0b_attn(g)
        if "qkT" in dbg and g == 0:
            o = dout("dbg_qkT", [128, 4 * T], BF16)
            dsd = P.dsem("dbg1")
            k.dma(o, qkT[:].rearrange("p a b -> p (a b)"), dsd, (t_qkT,), ())
            o2 = dout("dbg_sga", [128, 2 * T], BF16)
            k.dma(o2, sga[:].rearrange("p a b -> p (a b)"), dsd, (t_sga,), ())

    k.dma(k_s[:, :], ks_f[:].rearrange("p a b -> p (a b)"), ds_ks, (t_qkvs,), ())
    k.dma(v_s[:, :], vs_f[:].rearrange("p a b -> p (a b)"), ds_ks, (t_qkvs,), ())

    if dbg.get("sattn", 1):
        cache_k = din("cache_k", [2560 * 128, 1024])
        cache_v = din("cache_v", [2560 * 128, 1024])
        page_table = din("page_table", [1, NS * 16], I32)
    qs_d = nc.dram_tensor("qs_d", [NS, 1024], F32).ap()
    att_d = nc.dram_tensor("att_d", [NS, 1024], F32).ap()
    den_d = nc.dram_tensor("den_d", [NS, 8], F32).ap()

    def sample_attn():
        k.barrier()
        k.ptr = sattn_base
        f32t = lambda name, shape: k.sb(name, shape, F32)
        bft = lambda name, shape: k.sb(name, shape, BF16)
        ds_sa = P.dsem("sa_ld")
        t_qsd = Tok()
        k.dma(qs_d[:, :], qs_f[:].rearrange("p a b -> p (a b)"), ds_sa, (t_qkvs,), (t_qsd,))
        pti = k.sb("pti", [128, NS * 16], I32)
        idx = k.sb("idx", [128, NS * 16], I32)
        pcol = f32t("pcol", [128, 1])
        ident_f = f32t("identf2", [128, 128])
        dgm = f32t("dgm", [8, 128])
        onesf = f32t("onesf2", [128, 128])
        t_sc0 = Tok()
        k.dma(pti[:, :], page_table[0:1, :].partition_broadcast(128), ds_sa, (), (t_sc0,))
        k.dma(pcol[:, :], cst[:, C_PI:C_PI + 1], ds_sa, (), (t_sc0,), allow_slow_non_contiguous=True)
        k.dma(ident_f[:, :], cst[:, C_ID:C_ID + 128], ds_sa, (), (t_sc0,))
        k.dma(onesf[:, :], cst[:, C_ONE:C_ONE + 128], ds_sa, (), (t_sc0,))
        k.dma(dgm[:, :], cst[0:8, C_DG:C_DG + 128], ds_sa, (), (t_sc0,))
        k.ts(idx[:, :], pti[:, :], 128.0, pcol[:, 0:1], ALU.mult, ALU.add, (t_sc0,), (t_sc0,))
        sprod = f32t("sprod", [NS, 8, 128])
        sself = f32t("sself", [NS, 8])
        pself = f32t("pself", [NS, 8])
        t_self = Tok()
        k.tt(sprod[:], qs_f[:], ks_f[:], ALU.mult, (t_qkvs,), (t_self,))
        k.red(sself[:, :], sprod[:], ALU.add, (t_self,), (t_self,))
        k.act(pself[:, :], sself[:, :], AF.Exp, (t_self,), (t_self,))
        NKR, NVR = 4, 16
        kring = [f32t("kring%d" % i, [128, 1024]) for i in range(NKR)]
        t_kr = [Tok() for _ in range(NKR)]
        ds_kr = [P.dsem("kr%d" % i) for i in range(NKR)]
        vring = [bft("vring%d" % i, [128, 1024]) for i in range(NVR)]
        t_vr = [Tok() for _ in range(NVR)]
        ds_vr = [P.dsem("vr%d" % i) for i in range(4)]
        qbc = [f32t("qbc%d" % i, [128, 1024]) for i in range(2)]
        t_qbc = [Tok(), Tok()]
        ds_qbc = [P.dsem("qbc0"), P.dsem("qbc1")]
        prod = f32t("prod", [128, 8, 128])
        t_prod = Tok()
        S_all = f32t("S_all", [128, 16, 8])
        Sb = f32t("Sb", [128, 16, 8])
        Pb = bft("Pb", [128, 16, 8])
        t_S = Tok()
        sblk = f32t("sblk", [8, 16])
        sb8 = f32t("sb8", [8, 8])
        mx8s = f32t("mx8s", [8, 8])
        gb8 = f32t("gb8", [8, 8])
        bexp = f32t("bexp", [8, 16, 8])
        t_g = Tok()
        att_row = [f32t("att_row%d" % i, [1, 1024]) for i in range(2)]
        den_row = [f32t("den_row%d" % i, [1, 8]) for i in range(2)]
        t_rows = [Tok(), Tok()]
        ds_rows = [P.dsem("rows0"), P.dsem("rows1")]
        t_attd = Tok()
        den128 = f32t("den128", [1, 128])
        t_row = Tok()
        ones_c = bft("ones_c", [128, 1])
        k.memset(ones_c[:, :], 1.0, (t_sc0,))
        u32 = mybir.dt.uint32
        for s_ in range(NS):
            qb = qbc[s_ % 2]
            k.dma(qb[:, :], qs_d[s_:s_ + 1, :].partition_broadcast(128), ds_qbc[s_ % 2], (t_qsd,), (t_qbc[s_ % 2],))
            for j in range(16):
                col = s_ * 16 + j
                r_ = (s_ * 16 + j) % NKR
                P.add("pool", (lambda dst, cc: (lambda e: e.indirect_dma_start(
                    out=dst[:, :], out_offset=None, in_=cache_k[:, :],
                    in_offset=bass.IndirectOffsetOnAxis(ap=idx[:, cc:cc + 1].bitcast(u32), axis=0))))(kring[r_], col),
                    (t_sc0,), (t_kr[r_],), dsem=ds_kr[r_])
                k.tt(prod[:], kring[r_][:, :].rearrange("p (h d) -> p h d", h=8), qb[:, :].rearrange("p (h d) -> p h d", h=8), ALU.mult,
                     (t_kr[r_], t_qbc[s_ % 2]), (t_prod,))
                k.red(S_all[:, j, :], prod[:], ALU.add, (t_prod,), (t_S,))
            for j in range(16):
                col = s_ * 16 + j
                P.add("pool", (lambda dst, cc: (lambda e: e.indirect_dma_start(
                    out=dst[:, :], out_offset=None, in_=cache_v[:, :],
                    in_offset=bass.IndirectOffsetOnAxis(ap=idx[:, cc:cc + 1].bitcast(u32), axis=0))))(vring[j], col),
                    (t_sc0,), (t_vr[j],), dsem=ds_vr[j % 4])
            bk, tbk = bank[0], t_bank[0]
            for j in range(16):
                k.mm(bk[0:8, j:j + 1], S_all[:, j, :], onesf[:, 0:1], True, True, (t_S, t_sc0), (tbk,))
            k.cp(sblk[:, :], bk[0:8, 0:16], (tbk,), (t_g,))
            k.tt(sb8[:, :], sblk[:, 0:16:2], sblk[:, 1:16:2], ALU.add, (t_g,), (t_g,))
            P.add("dve", lambda e: e.max(out=mx8s[:, :], in_=sb8[:, :]), (t_g,), (t_g,))
            k.tt(gb8[:, :], sb8[:, :], mx8s[:, 2:3].broadcast_to([8, 8]), ALU.is_ge, (t_g,), (t_g,))
            k.ts(gb8[:, :], gb8[:, :], -1.0, -NEG, ALU.add, ALU.mult, (t_g,), (t_g,))
            k.tt(bexp[:].rearrange("p (n t) h -> p n t h", t=2), gb8[:, :].unsqueeze(2).unsqueeze(3).broadcast_to([8, 8, 2, 8]),
                 dgm[:, :].rearrange("p (n t h) -> p n t h", n=8, t=2), ALU.mult, (t_g, t_sc0), (t_g,))
            bk2, tbk2 = bank[1], t_bank[1]
            k.mm(bk2[:, 0:128], onesf[0:8, :], bexp[:].rearrange("p j h -> p (j h)"), True, True, (t_sc0, t_g), (tbk2,))
            k.tt(Sb[:].rearrange("p j h -> p (j h)"), S_all[:].rearrange("p j h -> p (j h)"), bk2[:, 0:128], ALU.add, (t_S, tbk2), (t_S,))
            k.act(Pb[:].rearrange("p j h -> p (j h)"), Sb[:].rearrange("p j h -> p (j h)"), AF.Exp, (t_S,), (t_S,))
            ba, tba = bank[2 + s_ % 2], t_bank[2 + s_ % 2]
            ba2, tba2 = bank[4 + s_ % 2], t_bank[4 + s_ % 2]
            for j in range(16):
                for h in range(8):
                    bb, tbb = (ba, tba) if h < 4 else (ba2, tba2)
                    k.mm(bb[0:1, (h % 4) * 128:(h % 4 + 1) * 128], Pb[:, j, h:h + 1], vring[j][:, h * 128:(h + 1) * 128], j == 0, j == 15,
                         (t_S, t_vr[j]), (tbb,))
            rs = s_ % 2
            k.cp(att_row[rs][0:1, 0:512], ba[0:1, :], (tba,), (t_rows[rs],), eng="act")
            k.cp(att_row[rs][0:1, 512:1024], ba2[0:1, :], (tba2,), (t_rows[rs],), eng="act")
            k.mm(bk[0:1, 128:256], ones_c[:, 0:1], Pb[:].rearrange("p j h -> p (j h)"), True, True, (t_sc0, t_S), (tbk,))
            k.cp(den128[0:1, :], bk[0:1, 128:256], (tbk,), (t_row,))
            k.red(den_row[rs][0:1, :], den128[0:1, :].rearrange("p (j h) -> p h j", h=8), ALU.add, (t_row,), (t_rows[rs],))
            k.dma(att_d[s_:s_ + 1, :], att_row[rs][0:1, :], ds_rows[rs], (t_rows[rs],), (t_attd,))
            k.dma(den_d[s_:s_ + 1, :], den_row[rs][0:1, :], ds_rows[rs], (t_rows[rs],), (t_attd,))
        att_t = f32t("att_t", [NS, 8, 128])
        den_t = f32t("den_t", [NS, 8])
        t_fin = Tok()
        k.dma(att_t[:].rearrange("p a b -> p (a b)"), att_d[:, :], ds_sa, (t_attd,), (t_fin,))
        k.dma(den_t[:, :], den_d[:, :], ds_sa, (t_attd,), (t_fin,))
        k.tt(sprod[:], vs_f[:], pself[:, :].unsqueeze(2).broadcast_to([NS, 8, 128]), ALU.mult, (t_qkvs, t_self), (t_self,))
        k.tt(att_t[:], att_t[:], sprod[:], ALU.add, (t_fin, t_self), (t_fin,))
        k.tt(den_t[:, :], den_t[:, :], pself[:, :], ALU.add, (t_fin, t_self), (t_fin,))
        P.add("dve", lambda e: e.reciprocal(out=den_t[:, :], in_=den_t[:, :]), (t_fin,), (t_fin,))
        k.tt(att_t[:], att_t[:], den_t[:, :].unsqueeze(2).broadcast_to([NS, 8, 128]), ALU.mult, (t_fin,), (t_fin,))
        bk, tbk = bank[0], t_bank[0]
        for h in range(8):
            k.tr(bk[:, h * 16:(h + 1) * 16], att_t[:, h, :], ident_f[0:16, 0:16], (t_fin, t_sc0), (tbk,))
        k.tt(catS[:, 0:8, :], bk[:, 0:128].rearrange("p (h s) -> p h s", h=8), sgaS[:, :, :], ALU.mult, (tbk, t_sgaS), (t_catS,))

    if dbg.get("sattn", 1):
        sample_attn()


    lam_re = din("lam_re", [64, 64])
    lam_im = din("lam_im", [64, 64])
    log_dt = din("log_dt", [1, 64])
    s5_b = din("s5_b", [64, 64, 32])
    s5_c = din("s5_c", [64, 16, 128])
    s5_d = din("s5_d", [1, 1024])
    glu_w = din("glu_w", [1024, 1024])
    glu_b = din("glu_b", [1, 1024])
    st_s5 = din("st_s5", [NS, 8192])
    s5_p = dout("s5_p", [64, 64, 2])
    s5_s = dout("s5_s", [NS, 8192])
    uT_d = nc.dram_tensor("uT_d", [8, 128, TS], BF16).ap()
    sgbT_d = nc.dram_tensor("sgbT_d", [8, 128, TS], BF16).ap()
    zT_d = nc.dram_tensor("zT_d", [8, 128, T], BF16).ap()
    t_uTd, t_sgbTd, t_zTd = Tok(), Tok(), Tok()
    k.barrier()
    k.ptr = sattn_base
    ust = [k.sb("ust%d" % i, [128, TS], BF16) for i in range(2)]
    sgst = [k.sb("sgst%d" % i, [128, TS], BF16) for i in range(2)]
    t_ust, t_sgst = [Tok(), Tok()], [Tok(), Tok()]
    ds_ust, ds_sgst = [P.dsem("ust0"), P.dsem("ust1")], [P.dsem("sgst0"), P.dsem("sgst1")]
    if dbg.get("s5", 1):
        for f in range(8):
            s_ = f % 2
            wu, tu = wload(w_in_even[:, 4096 + 128 * f:4096 + 128 * (f + 1)], KC, 128)
            wgb, tgb = wload(w_in_even[:, 5120 + 128 * f:5120 + 128 * (f + 1)], KC, 128)
            for which, (ww, tw) in enumerate(((wu, tu), (wgb, tgb))):
                for b in range(5):
                    c0 = b * 512
                    n = 512 if b < 4 else NS
                    bi = (which * 5 + b) % 2
                    for kc in range(KC):
                        k.mm(bank[bi][:, 0:n], ww[:, kc, :], hT[:, kc, c0:c0 + n], kc == 0, kc == KC - 1, (t_hT, tw), (t_bank[bi],))
                    if which == 0:
                        k.act(ust[s_][:, c0:c0 + n], bank[bi][:, 0:n], AF.Copy, (t_bank[bi],), (t_ust[s_],))
                    else:
                        k.act(sgst[s_][:, c0:c0 + n], bank[bi][:, 0:n], AF.Silu, (t_bank[bi],), (t_sgst[s_],))
            k.dma(uT_d[f], ust[s_][:, :], ds_ust[s_], (t_ust[s_],), (t_uTd,))
            k.dma(sgbT_d[f], sgst[s_][:, :], ds_sgst[s_], (t_sgst[s_],), (t_sgbTd,))

    def l0c():
        k.barrier()
        k.ptr = hT_base
        f32t = lambda name, shape: k.sb(name, shape, F32)
        TWO_PI = float(2.0 * np.pi)
        ident_f = f32t("ident_f", [128, 128])
        evec = f32t("evec", [128, 17])
        nvec = f32t("nvec", [128, 128])
        t_c2 = Tok()
        ds_s5 = P.dsem("s5ld")
        k.dma(ident_f[:], cst[:, C_ID:C_ID + 128], ds_s5, (), (t_c2,))
        k.dma(evec[:], cst[:, C_EV:C_EV + 17], ds_s5, (), (t_c2,))
        k.dma(nvec[:], cst[:, C_NV:C_NV + 128], ds_s5, (), (t_c2,))
        dvec, gbvec = f32t("dvec", [128, 8]), f32t("gbvec", [128, 8])
        yf = [f32t("yf%d" % i, [128, 512]) for i in range(2)]
        zst = [k.sb("zst%d" % i, [128, 512], BF16) for i in range(2)]
        t_yf, t_zst = [Tok(), Tok()], [Tok(), Tok()]
        ds_zst = [P.dsem("zst0"), P.dsem("zst1")]
        zS = k.sb("zS", [128, 8, NS], BF16)
        t_zS = Tok()
        glu_base = k.ptr
        lr, li, lgdt = f32t("lr", [128, 32]), f32t("li", [128, 32]), f32t("lgdt", [128, 32])
        t_l = Tok()
        with nc.allow_non_contiguous_dma(reason="tiny parameter relayout"):
            for g2 in range(2):
                hs = slice(g2 * 64, (g2 + 1) * 64)
                k.dma(lr[hs, :], lam_re.rearrange("(q g) p -> g p q", g=2)[g2], ds_s5, (), (t_l,), allow_slow_non_contiguous=True)
                k.dma(li[hs, :], lam_im.rearrange("(q g) p -> g p q", g=2)[g2], ds_s5, (), (t_l,), allow_slow_non_contiguous=True)
                k.dma(lgdt[hs, :], log_dt.rearrange("o (q g) -> g o q", g=2)[g2].partition_broadcast(64), ds_s5, (), (t_l,), allow_slow_non_contiguous=True)
            k.dma(dvec[:, :], s5_d.rearrange("o (f p) -> p (o f)", p=128), ds_s5, (), (t_l,), allow_slow_non_contiguous=True)
            k.dma(gbvec[:, :], glu_b.rearrange("o (f p) -> p (o f)", p=128), ds_s5, (), (t_l,), allow_slow_non_contiguous=True)
        NE = 17 * 32
        tA, tB, tC, tD = (f32t("tmp%d" % i, [128, 544]) for i in range(4))
        tI = k.sb("tmpI", [128, 544], I32)
        t_tmp = Tok()

        def sincos(ang, n, out_s, out_c, r, w):
            for shift, out in ((0.0, out_s), (float(np.pi / 2), out_c)):
                if out is None:
                    continue
                k.ts(tA[:, 0:n], ang, shift, 1.0 / TWO_PI, ALU.add, ALU.mult, r, (t_tmp,))
                k.cp(tI[:, 0:n], tA[:, 0:n], (t_tmp,), (t_tmp,))
                k.cp(tA[:, 0:n], tI[:, 0:n], (t_tmp,), (t_tmp,))
                k.stt(tA[:, 0:n], tA[:, 0:n], -TWO_PI, ang, ALU.mult, ALU.add, r + (t_tmp,), (t_tmp,))
                k.act(out, tA[:, 0:n], AF.Sin, (t_tmp,), w, bias=shift) if shift == 0.0 else None
                if shift != 0.0:
                    k.ts(tA[:, 0:n], tA[:, 0:n], shift, None, ALU.add, None, (t_tmp,), (t_tmp,))
                    k.ts(tB[:, 0:n], tA[:, 0:n], float(np.pi), -TWO_PI, ALU.is_gt, ALU.mult, (t_tmp,), (t_tmp,))
                    k.tt(tA[:, 0:n], tA[:, 0:n], tB[:, 0:n], ALU.add, (t_tmp,), (t_tmp,))
                    k.act(out, tA[:, 0:n], AF.Sin, (t_tmp,), w)

        dtv, rho, th = f32t("dtv", [128, 32]), f32t("rho", [128, 32]), f32t("th", [128, 32])
        k.act(dtv[:, :], lgdt[:, :], AF.Exp, (t_l,), (t_l,))
        k.tt(rho[:, :], lr[:, :], dtv[:, :], ALU.mult, (t_l,), (t_l,))
        k.tt(th[:, :], li[:, :], dtv[:, :], ALU.mult, (t_l,), (t_l,))
        RH, TH, MAG = f32t("RH", [128, 17, 32]), f32t("TH", [128, 17, 32]), f32t("MAG", [128, 17, 32])
        APR, API = f32t("APR", [128, 17, 32]), f32t("API", [128, 17, 32])
        SN, CS = f32t("SN", [128, 17, 32]), f32t("CS", [128, 17, 32])
        t_tab = Tok()
        ev_b = evec[:, :].unsqueeze(2).broadcast_to([128, 17, 32])
        k.tt(RH[:], rho[:, :].unsqueeze(1).broadcast_to([128, 17, 32]), ev_b, ALU.mult, (t_l, t_c2), (t_tab,))
        k.tt(TH[:], th[:, :].unsqueeze(1).broadcast_to([128, 17, 32]), ev_b, ALU.mult, (t_l, t_c2), (t_tab,))
        flat = lambda t: t[:].rearrange("p a b -> p (a b)")
        k.act(flat(MAG), flat(RH), AF.Exp, (t_tab,), (t_tab,))
        sincos(flat(TH), NE, flat(SN), flat(CS), (t_tab,), (t_tab,))
        k.tt(APR[:], MAG[:], CS[:], ALU.mult, (t_tab,), (t_tab,))
        k.tt(API[:], MAG[:], SN[:], ALU.mult, (t_tab,), (t_tab,))
        ph16 = f32t("ph16", [128, 32])
        k.ts(tA[:, 0:32], TH[:, 16, :], 1.0 / TWO_PI, None, ALU.mult, None, (t_tab,), (t_tmp,))
        k.cp(tI[:, 0:32], tA[:, 0:32], (t_tmp,), (t_tmp,))
        k.cp(tA[:, 0:32], tI[:, 0:32], (t_tmp,), (t_tmp,))
        k.stt(ph16[:, :], tA[:, 0:32], -TWO_PI, TH[:, 16, :], ALU.mult, ALU.add, (t_tmp, t_tab), (t_tab,))
        cr, ci = f32t("cr", [128, 32]), f32t("ci", [128, 32])
        e1, e2, e3 = tA[:, 0:32], tB[:, 0:32], tC[:, 0:32]
        k.tt(e1, lr[:, :], lr[:, :], ALU.mult, (t_l,), (t_tmp,))
        k.tt(e2, li[:, :], li[:, :], ALU.mult, (t_l,), (t_tmp,))
        k.tt(e1, e1, e2, ALU.add, (t_tmp,), (t_tmp,))
        P.add("dve", lambda e: e.reciprocal(out=tD[:, 0:32], in_=tA[:, 0:32]), (t_tmp,), (t_tmp,))
        k.ts(e3, APR[:, 1, :], -1.0, None, ALU.add, None, (t_tab,), (t_tmp,))
        k.tt(e1, e3, lr[:, :], ALU.mult, (t_tmp, t_l), (t_tmp,))
        k.tt(e2, API[:, 1, :], li[:, :], ALU.mult, (t_tab, t_l), (t_tmp,))
        k.tt(e1, e1, e2, ALU.add, (t_tmp,), (t_tmp,))
        k.tt(cr[:, :], e1, tD[:, 0:32], ALU.mult, (t_tmp,), (t_tab,))
        k.tt(e1, API[:, 1, :], lr[:, :], ALU.mult, (t_tab, t_l), (t_tmp,))
        k.tt(e2, e3, li[:, :], ALU.mult, (t_tmp, t_l), (t_tmp,))
        k.tt(e1, e1, e2, ALU.subtract, (t_tmp,), (t_tmp,))
        k.tt(ci[:, :], e1, tD[:, 0:32], ALU.mult, (t_tmp,), (t_tab,))
        if dbg.get("s5stop") == 1:
            return
        Braw = f32t("Braw", [128, 32, 32])
        Bbr, Bbi = f32t("Bbr", [128, 32, 16]), f32t("Bbi", [128, 32, 16])
        t_B = Tok()
        for g2 in range(2):
            hs = slice(g2 * 64, (g2 + 1) * 64)
            k.dma(Braw[hs, :, :], s5_b.rearrange("(q g) p x -> g p q x", g=2)[g2], ds_s5, (), (t_B,))
        Bre = Braw[:, :, 0:32:2]
        Bim = Braw[:, :, 1:32:2]
        crb = cr[:, :].unsqueeze(2).broadcast_to([128, 32, 16])
        cib = ci[:, :].unsqueeze(2).broadcast_to([128, 32, 16])
        v3 = lambda t: t[:, 0:512].rearrange("p (a b) -> p a b", a=32)
        k.tt(v3(tA), Bre, crb, ALU.mult, (t_B, t_tab), (t_tmp,))
        k.tt(v3(tB), Bim, cib, ALU.mult, (t_B, t_tab), (t_tmp,))
        k.tt(Bbr[:], v3(tA), v3(tB), ALU.subtract, (t_tmp,), (t_B,))
        k.tt(v3(tA), Bim, crb, ALU.mult, (t_B, t_tab), (t_tmp,))
        k.tt(v3(tB), Bre, cib, ALU.mult, (t_B, t_tab), (t_tmp,))
        k.tt(Bbi[:], v3(tA), v3(tB), ALU.add, (t_tmp,), (t_B,))
        if dbg.get("s5stop") == 2:
            return
        big_base = k.ptr
        big = f32t("big", [16, 8192])
        t_big = Tok()
        k.ptr = big_base
        W4 = [128, 17, 4, 16]
        t1, t2 = f32t("t1", W4), f32t("t2", W4)
        Wr, Wi = f32t("Wr", [128, 16, 4, 2, 16]), f32t("Wi", [128, 16, 4, 2, 16])
        Kw = k.sb("Kw", [128, 16, 128], BF16)
        Win = k.sb("Win", [128, 16, 2, 128], BF16)
        assert k.ptr >= big_base + 32768
        CreT, CimT = f32t("CreT", [128, 32, 16]), f32t("CimT", [128, 32, 16])
        X0r, X0i = f32t("X0r", [128, 32, 16]), f32t("X0i", [128, 32, 16])
        CrePad, NCimPad = f32t("CrePad", [128, 32, 2, 16]), f32t("NCimPad", [128, 32, 2, 16])
        t_C = Tok()
        bview = big[:, :].rearrange("c (q x) -> c q x", q=32)

        def to_pair_layout(dst_r, dst_i, wtok):
            for ri, dst in ((0, dst_r), (1, dst_i)):
                pb = bank[ri]
                pbv = pb[:, :].rearrange("p (q c) -> p q c", q=32)
                for q in range(32):
                    k.tr(pbv[:, q, :], bview[:, q, ri:256:2], ident_f[0:16, 0:16], (t_big, t_c2), (t_bank[ri],))
                k.cp(dst[:], pbv, (t_bank[ri],), (wtok,), eng="act")

        k.dma(big[:, :].rearrange("c (q g x) -> c q g x", q=32, g=2), s5_c.rearrange("(q g) c x -> c q g x", g=2), ds_s5, (), (t_big,))
        to_pair_layout(CreT, CimT, t_C)
        k.dma(big[:, :], st_s5[:, :], ds_s5, (), (t_big,))
        to_pair_layout(X0r, X0i, t_C)
        k.memset(CrePad[:].rearrange("p a b c -> p (a b c)"), 0.0, (t_C,))
        k.memset(NCimPad[:].rearrange("p a b c -> p (a b c)"), 0.0, (t_C,))
        for g2 in range(2):
            hs = slice(g2 * 64, (g2 + 1) * 64)
            k.cp(CrePad[hs, :, g2, :], CreT[hs, :, :], (t_C,), (t_C,))
            k.ts(NCimPad[hs, :, g2, :], CimT[hs, :, :], -1.0, None, ALU.mult, None, (t_C,), (t_C,))

        if dbg.get("s5stop") == 3:
            return
        Vr, Vi = k.sb("Vr", [128, 17, 4, 2, 16], BF16), k.sb("Vi", [128, 17, 4, 2, 16], BF16)
        t_w = Tok()
        t_t12 = Tok()
        k.fence([t_big], [t_w, t_t12])
        for tz in (Wr, Wi):
            k.memset(tz[:].rearrange("p a b c d -> p (a b c d)"), 0.0, (t_w,))
        for tz in (Vr, Vi):
            k.memset(tz[:].rearrange("p a b c d -> p (a b c d)"), 0.0, (t_w,))
        k.memset(Kw[:].rearrange("p a b -> p (a b)"), 0.0, (t_w,))
        uTb = [k.sb("uTb%d" % i, [128, TS], BF16) for i in range(2)]
        t_uTb = [Tok(), Tok()]
        ds_uTb = [P.dsem("uTb0"), P.dsem("uTb1")]
        Sre, Sim = f32t("Sre", [128, 4, 128]), f32t("Sim", [128, 4, 128])
        cosn, sinn = f32t("cosn", [128, 4, 128]), f32t("sinn", [128, 4, 128])
        m1, m2 = f32t("m1", [128, 4, 128]), f32t("m2", [128, 4, 128])
        Zr, Zi = f32t("Zr", [128, 4, 128]), f32t("Zi", [128, 4, 128])
        R0 = f32t("R0", [128, 4, 128])
        Xpr, Xpi = k.sb("Xpr", [128, 4, 128], BF16), k.sb("Xpi", [128, 4, 128], BF16)
        t_scan = Tok()
        k.memset(Xpr[:].rearrange("p a b -> p (a b)"), 0.0, (t_scan,))
        k.memset(Xpi[:].rearrange("p a b -> p (a b)"), 0.0, (t_scan,))
        s5st_p = f32t("s5st_p", [128, 32, 2])
        s5st_s = f32t("s5st_s", [128, 32, 2, 16])
        t_st = Tok()
        xsn_r, xsn_i = f32t("xsn_r", [128, 4, 16]), f32t("xsn_i", [128, 4, 16])
        xsb_r, xsb_i = k.sb("xsb_r", [128, 4, 16], BF16), k.sb("xsb_i", [128, 4, 16], BF16)
        t_xs = Tok()
        yfs = f32t("yfs", [128, 16])
        xsps = f32t("xsps", [128, 4, 2, 16])
        t_xsps = Tok()
        f2 = lambda t: t[:].rearrange("p a b -> p (a b)")
        bc_e = lambda tab, ne, qs: tab[:, 0:ne, qs].unsqueeze(3).broadcast_to([128, ne, 4, 16])
        bc_x = lambda src, ne, qs: src[:, qs, :].unsqueeze(1).broadcast_to([128, ne, 4, 16])

        for f in range(8):
            qs = slice(4 * f, 4 * f + 4)
            s_ = f % 2
            k.dma(uTb[s_][:, :], uT_d[f], ds_uTb[s_], (t_uTd,), (t_uTb[s_],))
            for dst, (ta, xa, tb, xb, op) in ((Wr, (APR, Bbr, API, Bbi, ALU.subtract)), (Wi, (APR, Bbi, API, Bbr, ALU.add))):
                k.tt(t1[:, 0:16], bc_e(ta, 16, qs), bc_x(xa, 16, qs), ALU.mult, (t_tab, t_B), (t_t12,))
                k.tt(t2[:, 0:16], bc_e(tb, 16, qs), bc_x(xb, 16, qs), ALU.mult, (t_tab, t_B), (t_t12,))
                for g2 in range(2):
                    hs = slice(g2 * 64, (g2 + 1) * 64)
                    k.tt(dst[hs, :, :, g2, :], t1[hs, 0:16], t2[hs, 0:16], op, (t_t12,), (t_w,))
            kb = bank[2]
            kbv = kb[:, :].rearrange("p (a b) -> p a b", a=4)
            for t0 in range(0, 16, 4):
                for tl in range(4):
                    tau = t0 + tl
                    for qi in range(4):
                        ps_ = slice(32 * qi, 32 * qi + 32)
                        k.mm(kbv[ps_, tl, ps_], Wr[:, tau, qi, :, :].rearrange("p a b -> p (a b)"),
                             CrePad[:, 4 * f + qi, :, :].rearrange("p a b -> p (a b)"), True, False, (t_w, t_C), (t_bank[2],), tp=(0, 32 * qi))
                        k.mm(kbv[ps_, tl, ps_], Wi[:, tau, qi, :, :].rearrange("p a b -> p (a b)"),
                             NCimPad[:, 4 * f + qi, :, :].rearrange("p a b -> p (a b)"), False, True, (t_w, t_C), (t_bank[2],), tp=(0, 32 * qi))
                for qi in range(4):
                    ps_ = slice(32 * qi, 32 * qi + 32)
                    k.cp(Kw[ps_, t0:t0 + 4, ps_], kbv[ps_, :, ps_], (t_bank[2],), (t_w,), eng="act")
            cnt = 0
            for i in range(16):
                for ri, src in ((0, Wr), (1, Wi)):
                    wb = bank[3 + (cnt // 4) % 2]
                    twb = t_bank[3 + (cnt // 4) % 2]
                    k.tr(wb[:, (cnt % 4) * 128:(cnt % 4 + 1) * 128], src[:, 15 - i, :, :, :].rearrange("p a b c -> p (a b c)"),
                         ident_f[:, :], (t_w, t_c2), (twb,))
                    cnt += 1
                    if cnt % 4 == 0:
                        i0 = (cnt - 4) // 2
                        k.cp(Win[:, i0:i0 + 2, :, :].rearrange("p a b c -> p (a b c)"), wb[:, :], (twb,), (t_w,), eng="act")
            k.tt(t1[:], bc_e(APR, 17, qs), bc_x(CreT, 17, qs), ALU.mult, (t_tab, t_C), (t_t12,))
            k.tt(t2[:], bc_e(API, 17, qs), bc_x(CimT, 17, qs), ALU.mult, (t_tab, t_C), (t_t12,))
            for g2 in range(2):
                hs = slice(g2 * 64, (g2 + 1) * 64)
                k.tt(Vr[hs, :, :, g2, :], t1[hs], t2[hs], ALU.subtract, (t_t12,), (t_w,))
            k.tt(t1[:], bc_e(API, 17, qs), bc_x(CreT, 17, qs), ALU.mult, (t_tab, t_C), (t_t12,))
            k.tt(t2[:], bc_e(APR, 17, qs), bc_x(CimT, 17, qs), ALU.mult, (t_tab, t_C), (t_t12,))
            for g2 in range(2):
                hs = slice(g2 * 64, (g2 + 1) * 64)
                k.stt(Vi[hs, :, :, g2, :], t1[hs], -1.0, t2[hs], ALU.mult, ALU.subtract, (t_t12,), (t_w,))
            if dbg.get("s5stop") == 4:
                return
            u = uTb[s_]
            for qi in range(4):
                ps_ = slice(32 * qi, 32 * qi + 32)
                for ri in range(2):
                    for i in range(16):
                        k.mm(bank[qi][:, ri * 128:(ri + 1) * 128], Win[ps_, i, ri, :], u[ps_, i:T:16], i == 0, i == 15,
                             (t_w, t_uTb[s_]), (t_bank[qi],), tp=(32 * qi, 0))
                k.cp(Sre[:, qi, :], bank[qi][:, 0:128], (t_bank[qi],), (t_scan,), eng="act")
                k.cp(Sim[:, qi, :], bank[qi][:, 128:256], (t_bank[qi],), (t_scan,), eng="act")
            if dbg.get("s5stop") == 41:
                return
            k.tt(m1[:], ph16[:, qs].unsqueeze(2).broadcast_to([128, 4, 128]), nvec[:, :].unsqueeze(1).broadcast_to([128, 4, 128]),
                 ALU.mult, (t_tab, t_c2), (t_scan,))
            k.ts(f2(m1), f2(m1), float(128 * 2 * np.pi), None, ALU.add, None, (t_scan,), (t_scan,))
            sincos(f2(m1), 512, f2(sinn), f2(cosn), (t_scan,), (t_scan,))
            if dbg.get("s5stop") == 42:
                return
            k.cp(R0[:], MAG[:, 16, qs].unsqueeze(2).broadcast_to([128, 4, 128]), (t_tab,), (t_scan,))
            k.memset(R0[:, :, 0:1], 0.0, (t_scan,))
            k.tt(m1[:], Sre[:], cosn[:], ALU.mult, (t_scan,), (t_scan,))
            k.tt(m2[:], Sim[:], sinn[:], ALU.mult, (t_scan,), (t_scan,))
            k.tt(m1[:], m1[:], m2[:], ALU.add, (t_scan,), (t_scan,))
            k.tt(m2[:], Sim[:], cosn[:], ALU.mult, (t_scan,), (t_scan,))
            k.tt(Sim[:], Sre[:], sinn[:], ALU.mult, (t_scan,), (t_scan,))
            k.tt(m2[:], m2[:], Sim[:], ALU.subtract, (t_scan,), (t_scan,))
            if dbg.get("s5stop") == 43:
                return
            P.add("dve", lambda e: e.tensor_tensor_scan(out=f2(Zr), data0=f2(R0), data1=f2(m1), initial=0.0, op0=ALU.mult, op1=ALU.add),
                  (t_scan,), (t_scan,))
            P.add("dve", lambda e: e.tensor_tensor_scan(out=f2(Zi), data0=f2(R0), data1=f2(m2), initial=0.0, op0=ALU.mult, op1=ALU.add),
                  (t_scan,), (t_scan,))
            if dbg.get("s5stop") == 44:
                return
            k.tt(m1[:], Zr[:], cosn[:], ALU.mult, (t_scan,), (t_scan,))
            k.tt(m2[:], Zi[:], sinn[:], ALU.mult, (t_scan,), (t_scan,))
            k.tt(Sre[:], m1[:], m2[:], ALU.subtract, (t_scan,), (t_scan,))
            k.tt(m1[:], Zr[:], sinn[:], ALU.mult, (t_scan,), (t_scan,))
            k.tt(m2[:], Zi[:], cosn[:], ALU.mult, (t_scan,), (t_scan,))
            k.tt(Sim[:], m1[:], m2[:], ALU.add, (t_scan,), (t_scan,))
            if dbg.get("s5stop") == 45:
                return
            k.cp(s5st_p[:, qs, 0:1], Sre[:, :, 127:128], (t_scan,), (t_st,))
            k.cp(s5st_p[:, qs, 1:2], Sim[:, :, 127:128], (t_scan,), (t_st,))
            k.cp(Xpr[:, :, 1:128], Sre[:, :, 0:127], (t_scan,), (t_scan,))
            k.cp(Xpi[:, :, 1:128], Sim[:, :, 0:127], (t_scan,), (t_scan,))
            if dbg.get("s5stop") == 5:
                return
            for b in range(4):
                yb = bank[4 + b % 2]
                tyb = t_bank[4 + b % 2]
                ybv = yb[:, :].rearrange("p (n j) -> p n j", j=16)
                ubv = u[:, b * 512:(b + 1) * 512].rearrange("p (n j) -> p n j", j=16)
                for tau in range(16):
                    k.mm(ybv[:, :, tau:16], Kw[:, tau, :], ubv[:, :, 0:16 - tau], tau == 0, False, (t_w, t_uTb[s_]), (tyb,))
                for qi in range(4):
                    ps_ = slice(32 * qi, 32 * qi + 32)
                    for j in range(16):
                        last = (qi == 3 and j == 15)
                        k.mm(yb[ps_, j:512:16], Vr[:, j + 1, qi, :, :].rearrange("p a b -> p (a b)"), Xpr[:, qi, b * 32:(b + 1) * 32],
                             False, False, (t_w, t_scan), (tyb,), tp=(0, 32 * qi))
                        k.mm(yb[ps_, j:512:16], Vi[:, j + 1, qi, :, :].rearrange("p a b -> p (a b)"), Xpi[:, qi, b * 32:(b + 1) * 32],
                             False, last, (t_w, t_scan), (tyb,), tp=(0, 32 * qi))
                ys_ = (4 * f + b) % 2
                k.stt(yf[ys_][:, :], u[:, b * 512:(b + 1) * 512], dvec[:, f:f + 1], yb[:, :], ALU.mult, ALU.add,
                      (t_uTb[s_], t_l, tyb), (t_yf[ys_],))
                k.act(zst[ys_][:, :], yf[ys_][:, :], AF.Gelu_apprx_tanh, (t_yf[ys_],), (t_zst[ys_],))
                k.dma(zT_d[f, :, b * 512:(b + 1) * 512], zst[ys_][:, :], ds_zst[ys_], (t_zst[ys_],), (t_zTd,))
            if dbg.get("s5stop") == 6:
                return
            xbv = xsps[:]
            for qi in range(4):
                ps_ = slice(32 * qi, 32 * qi + 32)
                for ri in range(2):
                    k.mm(bank[qi][:, 256 + ri * 16:256 + (ri + 1) * 16], Win[ps_, 15, ri, :], u[ps_, T:TS], True, True,
                         (t_w, t_uTb[s_]), (t_bank[qi],), tp=(32 * qi, 0))
                k.cp(xsps[:, qi, :, :], bank[qi][:, 256:288].rearrange("p (r s) -> p r s", r=2), (t_bank[qi],), (t_xsps,), eng="act")
            a1r = APR[:, 1, qs].unsqueeze(2).broadcast_to([128, 4, 16])
            a1i = API[:, 1, qs].unsqueeze(2).broadcast_to([128, 4, 16])
            k.tt(xsn_r[:], X0r[:, qs, :], a1r, ALU.mult, (t_C, t_tab), (t_xs,))
            k.tt(xsn_i[:], X0i[:, qs, :], a1i, ALU.mult, (t_C, t_tab), (t_xs,))
            k.tt(xsn_r[:], xsn_r[:], xsn_i[:], ALU.subtract, (t_xs,), (t_xs,))
            k.tt(s5st_s[:, qs, 0, :], xsn_r[:], xbv[:, :, 0, :], ALU.add, (t_xs, t_xsps), (t_st,))
            k.tt(xsn_r[:], X0r[:, qs, :], a1i, ALU.mult, (t_C, t_tab), (t_xs,))
            k.tt(xsn_i[:], X0i[:, qs, :], a1r, ALU.mult, (t_C, t_tab), (t_xs,))
            k.tt(xsn_r[:], xsn_r[:], xsn_i[:], ALU.add, (t_xs,), (t_xs,))
            k.tt(s5st_s[:, qs, 1, :], xsn_r[:], xbv[:, :, 1, :], ALU.add, (t_xs, t_xsps), (t_st,))
            k.cp(xsb_r[:], s5st_s[:, qs, 0, :], (t_st,), (t_xs,))
            k.cp(xsb_i[:], s5st_s[:, qs, 1, :], (t_st,), (t_xs,))
            yb = bank[3]
            for qi in range(4):
                ps_ = slice(32 * qi, 32 * qi + 32)
                k.mm(yb[ps_, 0:16], Vr[:, 0, qi, :, :].rearrange("p a b -> p (a b)"), xsb_r[:, qi, :], True, False, (t_w, t_xs), (t_bank[3],), tp=(0, 32 * qi))
                k.mm(yb[ps_, 0:16], Vi[:, 0, qi, :, :].rearrange("p a b -> p (a b)"), xsb_i[:, qi, :], False, True, (t_w, t_xs), (t_bank[3],), tp=(0, 32 * qi))
            k.stt(yfs[:, :], u[:, T:TS], dvec[:, f:f + 1], yb[:, 0:16], ALU.mult, ALU.add, (t_uTb[s_], t_l, t_bank[3]), (t_xs,))
            k.act(zS[:, f, :], yfs[:, :], AF.Gelu_apprx_tanh, (t_xs,), (t_zS,))

        ds_so = P.dsem("s5out")
        with nc.allow_non_contiguous_dma(reason="small state relayout"):
            for g2 in range(2):
                hs = slice(g2 * 64, (g2 + 1) * 64)
                k.dma(s5_p.rearrange("(q g) p r -> g p q r", g=2)[g2], s5st_p[hs, :, :], ds_so, (t_st,), (), allow_slow_non_contiguous=True)
        k.fence([t_w, t_t12], [t_big])
        for ri in range(2):
            pb = bank[ri]
            pbv = pb[0:16, :].rearrange("s (q x) -> s q x", q=4)
            for q0 in range(0, 32, 4):
                for ql in range(4):
                    k.tr(pbv[:, ql, :], s5st_s[:, q0 + ql, ri, :], ident_f[:, :], (t_st, t_c2), (t_bank[ri],))
                k.cp(bview[:, q0:q0 + 4, ri:256:2], pbv, (t_bank[ri],), (t_big,), eng="act")
        k.dma(s5_s[:, :], big[:, :], ds_so, (t_big,), ())

        k.barrier()
        k.ptr = glu_base
        zT = k.sb("zT", [128, 8, T], BF16)
        t_zT = Tok()
        ds_zl = P.dsem("zload")
        for f in range(8):
            k.dma(zT[:, f, :], zT_d[f], ds_zl, (t_zTd,), (t_zT,))
        sgl = [k.sb("sgl%d" % i, [128, TS], BF16) for i in range(2)]
        t_sgl = [Tok(), Tok()]
        ds_sgl = [P.dsem("sgl0"), P.dsem("sgl1")]
        sg = [f32t("sg%d" % i, [128, 512]) for i in range(2)]
        t_sg = [Tok(), Tok()]
        g_ct = [0]
        for fo in range(8):
            s_ = fo % 2
            wg, twg = wload(glu_w[:, fo * 128:(fo + 1) * 128], 8, 128)
            k.dma(sgl[s_][:, :], sgbT_d[fo], ds_sgl[s_], (t_sgbTd,), (t_sgl[s_],))
            for b in range(5):
                n = 512 if b < 4 else NS
                c0 = b * 512
                bi = 4 + b % 2
                for kc in range(8):
                    rhs = zT[:, kc, c0:c0 + n] if b < 4 else zS[:, kc, :]
                    k.mm(bank[bi][:, 0:n], wg[:, kc, :], rhs, kc == 0, kc == 7, (twg, t_zT, t_zS), (t_bank[bi],))
                gi = g_ct[0] % 2
                g_ct[0] += 1
                k.act(sg[gi][:, 0:n], bank[bi][:, 0:n], AF.Sigmoid, (t_bank[bi], t_l), (t_sg[gi],), bias=gbvec[:, fo:fo + 1])
                zsrc = zT[:, fo, c0:c0 + n] if b < 4 else zS[:, fo, :]
                k.tt(sg[gi][:, 0:n], sg[gi][:, 0:n], zsrc, ALU.mult, (t_sg[gi], t_zT, t_zS), (t_sg[gi],))
                if b < 4:
                    cs = gi
                    k.tt(zst[cs][:, :], sg[gi][:, :], sgl[s_][:, c0:c0 + 512], ALU.mult, (t_sg[gi], t_sgl[s_]), (t_zst[cs],))
                    k.dma(catT[8 + fo, :, c0:c0 + 512], zst[cs][:, :], ds_zst[cs], (t_zst[cs],), (t_catT,))
                else:
                    k.tt(catS[:, 8 + fo, :], sg[gi][:, 0:NS], sgl[s_][:, T:TS], ALU.mult, (t_sg[gi], t_sgl[s_]), (t_catS,))

    if dbg.get("s5", 1):
        l0c()


    w_out_even = din("w_out_even", [D, D])
    x1_d = nc.dram_tensor("x1_d", [TS, D], F32).ap()
    t_x1d = Tok()

    def outproj(w_out, nkc, actT_d, actS, t_actTd, t_actS, resp, ress, r_toks, dstp, dsts, t_dst, nm):
        k.barrier()
        k.ptr = hT_base
        nh = nkc // 16
        actR = [k.sb(nm + "actR%d" % h, [128, 16, TS], BF16) for h in range(nh)]
        t_actR = Tok()
        ds_a = P.dsem(nm + "actR")
        k.ptr = max(k.ptr, region)
        xin = [k.sb(nm + "xin%d" % i, [128, 256], F32) for i in range(3)]
        xo = [k.sb(nm + "xo%d" % i, [128, 256], F32) for i in range(2)]
        t_xin, t_xo = [Tok() for _ in range(3)], [Tok(), Tok()]
        ds_xin, ds_xo = [P.dsem(nm + "xin%d" % i) for i in range(3)], [P.dsem(nm + "xo0"), P.dsem(nm + "xo1")]
        for h in range(nh):
            for ft in range(16):
                k.dma(actR[h][:, ft, 0:T], actT_d[h * 16 + ft], ds_a, (t_actTd,), (t_actR,))
            k.cp(actR[h][:, :, T:TS], actS[:, h * 16:(h + 1) * 16, :], (t_actS,), (t_actR,))
        steps = [(db, i) for db in range(8) for i in range(NT + 1)]

        def load_res(n):
            db, i = steps[n]
            np_ = 128 if i < NT else NS
            c0 = i * 128
            rsrc = resp[c0:c0 + 128, db * 256:(db + 1) * 256] if i < NT else ress[:, db * 256:(db + 1) * 256]
            k.dma(xin[n % 3][:np_, :], rsrc, ds_xin[n % 3], tuple(r_toks), (t_xin[n % 3],))

        load_res(0)
        load_res(1)
        slabs = None
        for n, (db, i) in enumerate(steps):
            if i == 0:
                slabs = [wload(w_out[h * 2048:(h + 1) * 2048, db * 256:(db + 1) * 256], 16, 256) for h in range(nh)]
            if n + 2 < len(steps):
                load_res(n + 2)
            np_ = 128 if i < NT else NS
            c0 = i * 128
            s_ = n % 2
            bk, tbk = bank[s_], t_bank[s_]
            for h in range(nh):
                slab, tslab = slabs[h]
                for kc in range(16):
                    k.mm(bk[:np_, 0:256], actR[h][:, kc, c0:c0 + np_], slab[:, kc, :], h == 0 and kc == 0, h == nh - 1 and kc == 15,
                         (t_actR, tslab), (tbk,))
            k.tt(xo[s_][:np_, :], bk[:np_, 0:256], xin[n % 3][:np_, :], ALU.add, (tbk, t_xin[n % 3]), (t_xo[s_],))
            dd = dstp[c0:c0 + 128, db * 256:(db + 1) * 256] if i < NT else dsts[:, db * 256:(db + 1) * 256]
            k.dma(dd, xo[s_][:np_, :], ds_xo[s_], (t_xo[s_],), (t_dst,))

    def l1a():
        k.barrier()
        k.ptr = region
        nonlocal_alloc = {}
        return nonlocal_alloc

    if dbg.get("l0d", 1) and dbg.get("s5", 1):
        outproj(w_out_even, 16, catT, catS, t_catT, t_catS, xp, xs, (), x1_d[0:T, :], x1_d[T:TS, :], t_x1d, "od0")
        k.barrier()
        k.ptr = region
        nw_bc = k.sb("nw_bc1", [128, D], F32)
        xt = [k.sb("xt1_%d" % i, [128, D], F32) for i in range(2)]
        hb = [k.sb("hb1_%d" % i, [128, D], BF16) for i in range(2)]
        junk = k.sb("junk1", [128, D], BF16)
        ss = k.sb("ss1", [128, NT + 1], F32)
        rstd = k.sb("rstd1", [128, NT + 1], F32)
        l0a(1, x1_d, x1_d[T:TS, :], rtoks=(t_x1d,))
    if "x1" in dbg:
        o = dout("dbg_x1", [TS, D], F32)
        dsd3 = P.dsem("dbg3")
        k.dma(o, x1_d, dsd3, (t_x1d,), ())
        o = dout("dbg_hT", [128, KC * TS], BF16)
        k.dma(o, hT[:].rearrange("p a b -> p (a b)"), dsd3, (t_hT,), ())


    w_in_odd = din("w_in_odd", [D, 10304])
    conv_w = din("conv_w", [4, 6144])
    conv_b = din("conv_b", [1, 6144])
    dt_bias = din("dt_bias", [1, 64])
    a_log = din("a_log", [1, 64])
    ssd_d = din("ssd_d", [1, 64])
    ssd_nw = din("ssd_nw", [1, 4096])
    w_out_odd = din("w_out_odd", [4096, D])
    y_p = dout("y_p", [T, D])
    y_s = dout("y_s", [NS, D])
    conv_p = dout("conv_p", [3, 6144])
    ssd_p = dout("ssd_p", [64 * 64, 128])
    st_conv = din("st_conv", [NS, 3, 6144])
    st_ssd = din("st_ssd", [NS, 4096, 128])
    conv_s = dout("conv_s", [NS, 3, 6144])
    ssd_s = dout("ssd_s", [NS, 4096, 128])
    ynT_d = nc.dram_tensor("ynT_d", [32, 128, T], BF16).ap()
    t_ynTd = Tok()
    ynS = k.sb("ynS", [128, 32, NS], BF16) if False else None

    def l1():
        k.barrier()
        k.ptr = region
        f32t = lambda name, shape: k.sb(name, shape, F32)
        bft = lambda name, shape: k.sb(name, shape, BF16)
        ds_l1 = P.dsem("l1ld")
        Uf, onesf, ident_f = f32t("Uf", [128, 128]), f32t("onesf", [128, 128]), f32t("identf1", [128, 128])
        t_c3 = Tok()
        k.dma(Uf[:], cst[:, C_U:C_U + 128], ds_l1, (), (t_c3,))
        k.dma(onesf[:], cst[:, C_ONE:C_ONE + 128], ds_l1, (), (t_c3,))
        k.dma(ident_f[:], cst[:, C_ID:C_ID + 128], ds_l1, (), (t_c3,))
        cm1 = bft("cm1", [128, 128])
        esel = bft("esel8", [8, 8, 128])
        k.dma(cm1[:], cst[:, C_CM:C_CM + 128], ds_l1, (), (t_c3,), eng="pool")
        for j in range(8):
            k.dma(esel[:, j, :], cst[0:8, C_ES + j * 128:C_ES + (j + 1) * 128], ds_l1, (), (t_c3,), eng="pool")
        dtb_bc, A_bc, D_bc = f32t("dtb_bc", [128, 64]), f32t("A_bc", [128, 64]), f32t("D_bc", [128, 64])
        k.dma(dtb_bc[:], dt_bias[0:1, :].partition_broadcast(128), ds_l1, (), (t_c3,))
        k.dma(A_bc[:], a_log[0:1, :].partition_broadcast(128), ds_l1, (), (t_c3,))
        k.dma(D_bc[:], ssd_d[0:1, :].partition_broadcast(128), ds_l1, (), (t_c3,))
        k.act(A_bc[:], A_bc[:], AF.Exp, (t_c3,), (t_c3,))
        k.ts(A_bc[:], A_bc[:], -1.0, None, ALU.mult, None, (t_c3,), (t_c3,))
        cw, cb = f32t("cw", [128, 4, 48]), f32t("cb", [128, 48])
        for k4 in range(4):
            k.dma(cw[:, k4, :], conv_w[k4:k4 + 1, :].rearrange("o (t p) -> p (o t)", p=128), ds_l1, (), (t_c3,), allow_slow_non_contiguous=True)
        k.dma(cb[:], conv_b.rearrange("o (t p) -> p (o t)", p=128), ds_l1, (), (t_c3,), allow_slow_non_contiguous=True)
        convst = f32t("convst", [128, 48, 3])
        t_convst = Tok()
        NTT = NT + 1
        dtv, nacs, atot, av = f32t("dtv", [128, NTT, 64]), f32t("nacs", [128, NT, 64]), f32t("atot", [128, NT, 64]), f32t("av", [128, NT, 64])
        t_sc = Tok()
        wdt, twdt = wload(w_in_odd[:, 10240:10304], KC, 64)
        for i in range(NTT):
            np_ = 128 if i < NT else NS
            c0 = i * 128
            bk, tbk = bank[i % 2], t_bank[i % 2]
            for kc in range(KC):
                k.mm(bk[:np_, 0:64], hT[:, kc, c0:c0 + np_], wdt[:, kc, :], kc == 0, kc == KC - 1, (t_hT, twdt), (tbk,))
            k.tt(dtv[:np_, i, :], bk[:np_, 0:64], dtb_bc[:np_, :], ALU.add, (tbk, t_c3), (t_sc,))
        dflat = dtv[:].rearrange("p a b -> p (a b)")
        k.act(dflat, dflat, AF.Exp, (t_sc,), (t_sc,))
        k.act(dflat, dflat, AF.Ln, (t_sc,), (t_sc,), bias=1.0)
        k.tt(av[:], dtv[:, 0:NT, :], A_bc[:, :].unsqueeze(1).broadcast_to([128, NT, 64]), ALU.mult, (t_sc, t_c3), (t_sc,))
        for c in range(NT):
            bk, tbk = bank[c % 2], t_bank[c % 2]
            k.mm(bk[:, 0:64], Uf[:, :], av[:, c, :], True, True, (t_c3, t_sc), (tbk,))
            k.mm(bk[:, 64:128], onesf[:, :], av[:, c, :], True, True, (t_c3, t_sc), (tbk,))
            k.ts(nacs[:, c, :], bk[:, 0:64], -1.0, None, ALU.mult, None, (tbk,), (t_sc,))
            k.cp(atot[:, c, :], bk[:, 64:128], (tbk,), (t_sc,), eng="act")
        xbcS, xcS, szS = f32t("xbcS", [128, 48, NS]), f32t("xcS", [128, 48, NS]), f32t("szS", [128, 32, NS])
        t_xS = Tok()
        cbuf0 = f32t("cbuf0", [NS, 3, 128])
        cbuf = [cbuf0, cbuf0]
        t_cbuf0 = Tok()
        t_cbuf = [t_cbuf0, t_cbuf0]
        ds_cbuf0 = P.dsem("cbuf0")
        ds_cbuf = [ds_cbuf0, ds_cbuf0]
        cacc = f32t("cacc", [128, NS])
        t_cacc = Tok()
        l1s_base = k.ptr
        xcT = bft("xcT", [128, 6, T])
        t_xcT = Tok()
        xpre0 = bft("xpre0", [128, 3 + T])
        xpre = [xpre0, xpre0]
        t_xp0 = Tok()
        t_xpre = [t_xp0, t_xp0]
        k.memset(xpre0[:, 0:3], 0.0, (t_xp0,))
        dg0 = bft("dg0", [128, 4, 128])
        dg = [dg0, dg0]
        t_dg0 = Tok()
        t_dg = [t_dg0, t_dg0]
        sz = bft("sz", [128, NT, 512])
        t_sz = Tok()
        acsT = f32t("acsT", [8, 512])
        acsh, acsl = bft("acsh", [8, NT, 128]), bft("acsl", [8, NT, 128])
        t_acsT = Tok()
        nwg = f32t("nwg", [128, 512])
        t_nwg = Tok()
        ds_nwg = P.dsem("nwg")
        tok0 = bft("tok0", [128, 5, 128])
        tok = [tok0, tok0]
        t_tok0 = Tok()
        t_tok = [t_tok0, t_tok0]
        Lt = [f32t("Lt%d" % i, [128, 128]) for i in range(2)]
        t_Lt = [Tok(), Tok()]
        Mt0 = bft("Mt0", [128, 8, 128])
        Mt = [Mt0, Mt0]
        t_Mt0 = Tok()
        t_Mt = [t_Mt0, t_Mt0]
        xdt_, xdd_, xsD_ = [bft("xdt%d" % i, [128, 8, 64]) for i in range(2)], [bft("xdd%d" % i, [128, 8, 64]) for i in range(2)], [bft("xsD%d" % i, [128, 8, 64]) for i in range(2)]
        t_xd_ = [Tok(), Tok()]
        eac, decc, etc_ = f32t("eac", [128, 8]), f32t("decc", [128, 8]), f32t("etc", [128, 8])
        t_ec = Tok()
        yv, ygt = f32t("yv", [128, 512]), f32t("ygt", [128, 512])
        t_yv = Tok()
        ssn = f32t("ssn", [128, 2])
        ynb0 = bft("ynb0", [128, 512])
        ynb = [ynb0, ynb0]
        t_ynb0 = Tok()
        t_ynb = [t_ynb0, t_ynb0]
        ynst0 = bft("ynst0", [128, 4, 128])
        ynst = [ynst0, ynst0]
        t_ynst0 = Tok()
        t_ynst = [t_ynst0, t_ynst0]
        ds_ynst0 = P.dsem("ynst0")
        ds_ynst = [ds_ynst0, ds_ynst0]
        Hs, Hb = f32t("Hs", [128, 512]), bft("Hb", [128, 512])
        t_H = Tok()
        hout = yv[:, :].rearrange("p (a b) -> p a b", a=4)
        t_hout = t_yv
        ds_hout = P.dsem("hout")
        junk2 = bft("junk2", [128, 512])
        t_junk2 = Tok()
        pstb = pst[0]
        for g in (dbg["l1only"] if "l1only" in dbg else range(int(dbg.get("l1groups", 8)))):
            k.dma(nwg[:], ssd_nw[0:1, 512 * g:512 * (g + 1)].partition_broadcast(128), ds_nwg, (), (t_nwg,))
            tiles = [(4096 + 512 * g + 128 * j, 4 * g + j) for j in range(4)] + [(8192 + 128 * g, 32 + g), (9216 + 128 * g, 40 + g)]
            slabs = {}
            for ti, (col, tidx) in enumerate(tiles):
                if ti in (0, 2):
                    sl_, tsl_ = wload(w_in_odd[:, col:col + 256], KC, 256)
                    slabs[ti] = (sl_, tsl_, 0)
                    slabs[ti + 1] = (sl_, tsl_, 128)
                elif ti >= 4:
                    sl_, tsl_ = wload(w_in_odd[:, col:col + 128], KC, 128)
                    slabs[ti] = (sl_, tsl_, 0)
                wv, tw, off = slabs[ti]
                xs_ = ti % 2
                xp_ = xpre[xs_]
                for k4 in range(4):
                    k.ts(dg[xs_][:, k4, :], ident[:, :], cw[:, k4, tidx:tidx + 1], None, ALU.mult, None, (t_const, t_c3), (t_dg[xs_],))
                for b in range(4):
                    bk, tbk = bank[b % 2], t_bank[b % 2]
                    for kc in range(KC):
                        k.mm(bk[:, :], wv[:, kc, off:off + 128], hT[:, kc, b * 512:(b + 1) * 512], kc == 0, kc == KC - 1, (t_hT, tw), (tbk,))
                    k.cp(xp_[:, 3 + b * 512:3 + (b + 1) * 512], bk[:, :], (tbk,), (t_xpre[xs_],), eng="act")
                    if b == 3:
                        k.cp(convst[:, tidx, :], bk[:, 509:512], (tbk,), (t_convst,))
                for b in range(4):
                    bk, tbk = bank[2 + b % 2], t_bank[2 + b % 2]
                    for k4 in range(4):
                        k.mm(bk[:, :], dg[xs_][:, k4, :], xp_[:, b * 512 + k4:b * 512 + k4 + 512], k4 == 0, k4 == 3, (t_dg[xs_], t_xpre[xs_]), (tbk,))
                    k.act(xcT[:, ti, b * 512:(b + 1) * 512], bk[:, :], AF.Silu, (tbk, t_c3), (t_xcT,), bias=cb[:, tidx:tidx + 1])
                bk, tbk = bank[5], t_bank[5]
                for kc in range(KC):
                    k.mm(bk[:, 0:NS], wv[:, kc, off:off + 128], hT[:, kc, T:TS], kc == 0, kc == KC - 1, (t_hT, tw), (tbk,))
                k.cp(xbcS[:, tidx, :], bk[:, 0:NS], (tbk,), (t_xS,))
                c2 = tidx % 2
                k.dma(cbuf[c2][:, :, :], st_conv[:, :, tidx * 128:(tidx + 1) * 128], ds_cbuf[c2], (), (t_cbuf[c2],))
                for k3 in range(3):
                    k.tr(bk[:, 64 + k3 * NS:64 + (k3 + 1) * NS], cbuf[c2][:, k3, :], ident_f[0:NS, 0:NS], (t_cbuf[c2], t_c3), (tbk,))
                k.ts(cacc[:, :], bk[:, 64:64 + NS], cw[:, 0, tidx:tidx + 1], None, ALU.mult, None, (tbk, t_c3), (t_cacc,))
                for k3 in (1, 2):
                    k.stt(cacc[:, :], bk[:, 64 + k3 * NS:64 + (k3 + 1) * NS], cw[:, k3, tidx:tidx + 1], cacc[:, :], ALU.mult, ALU.add,
                          (tbk, t_c3, t_cacc), (t_cacc,))
                k.stt(cacc[:, :], xbcS[:, tidx, :], cw[:, 3, tidx:tidx + 1], cacc[:, :], ALU.mult, ALU.add, (t_xS, t_c3, t_cacc), (t_cacc,))
                k.act(xcS[:, tidx, :], cacc[:, :], AF.Silu, (t_cacc, t_c3), (t_xS,), bias=cb[:, tidx:tidx + 1])
            for zh in range(2):
                wz, twz = wload(w_in_odd[:, 512 * g + 256 * zh:512 * g + 256 * (zh + 1)], KC, 256)
                for c in range(NT):
                    bk, tbk = bank[c % 2], t_bank[c % 2]
                    for kc in range(KC):
                        k.mm(bk[:, 0:256], hT[:, kc, c * 128:(c + 1) * 128], wz[:, kc, :], kc == 0, kc == KC - 1, (t_hT, twz), (tbk,))
                    k.act(sz[:, c, 256 * zh:256 * (zh + 1)], bk[:, 0:256], AF.Silu, (tbk,), (t_sz,))
                for jj in range(2):
                    bk, tbk = bank[5], t_bank[5]
                    for kc in range(KC):
                        k.mm(bk[:, 0:NS], wz[:, kc, jj * 128:(jj + 1) * 128], hT[:, kc, T:TS], kc == 0, kc == KC - 1, (t_hT, twz), (tbk,))
                    k.act(szS[:, 4 * g + 2 * zh + jj, :], bk[:, 0:NS], AF.Silu, (tbk,), (t_xS,))
            for c4 in range(0, NT, 4):
                bk, tbk = bank[4], t_bank[4]
                for cl in range(4):
                    k.mm(bk[0:8, cl * 128:(cl + 1) * 128], av[:, c4 + cl, 8 * g:8 * g + 8], Uf[:, :], True, True, (t_sc, t_c3), (tbk,))
                hi = acsh[:, c4:c4 + 4, :].rearrange("p a b -> p (a b)")
                lo = acsl[:, c4:c4 + 4, :].rearrange("p a b -> p (a b)")
                k.cp(hi, bk[0:8, :], (tbk,), (t_acsT,))
                k.cp(acsT[:, :], hi, (t_acsT,), (t_acsT,))
                k.tt(acsT[:, :], bk[0:8, :], acsT[:, :], ALU.subtract, (tbk, t_acsT), (t_acsT,))
                k.cp(lo, acsT[:, :], (t_acsT,), (t_acsT,))
            for c in range(NT if g < int(dbg.get("l1p2", 8)) else 0):
                cs_ = c % 2
                cc = slice(c * 128, (c + 1) * 128)
                hs8 = slice(8 * g, 8 * g + 8)
                xdt, xdd, xsD, t_xd = xdt_[cs_], xdd_[cs_], xsD_[cs_], t_xd_[cs_]
                for j in range(5):
                    k.tr(pstb[:, j, :], xcT[:, j, cc], ident[:, :], (t_xcT, t_const), (t_pst[0],))
                k.cp(tok[cs_][:, :, :], pstb[:, 0:5, :], (t_pst[0],), (t_tok[cs_],), eng="act")
                xs_tok = tok[cs_][:, 0:4, :].rearrange("p a (h q) -> p (a h) q", q=64)
                k.act(eac[:, :], nacs[:, c, hs8], AF.Exp, (t_sc,), (t_ec,), scale=-1.0)
                k.tt(decc[:, :], atot[:, c, hs8], nacs[:, c, hs8], ALU.add, (t_sc,), (t_ec,))
                k.act(decc[:, :], decc[:, :], AF.Exp, (t_ec,), (t_ec,))
                k.act(etc_[:, :], atot[:, c, hs8], AF.Exp, (t_sc,), (t_ec,))
                bc8 = lambda t: t[:, :].unsqueeze(2).broadcast_to([128, 8, 64])
                k.tt(xdt[:], xs_tok, dtv[:, c, hs8].unsqueeze(2).broadcast_to([128, 8, 64]), ALU.mult, (t_tok[cs_], t_sc), (t_xd,), eng=POOLC)
                k.tt(xdd[:], xdt[:], bc8(decc), ALU.mult, (t_xd, t_ec), (t_xd,), eng=POOLC)
                k.tt(xsD[:], xs_tok, D_bc[:, hs8].unsqueeze(2).broadcast_to([128, 8, 64]), ALU.mult, (t_tok[cs_], t_c3), (t_xd,), eng=POOLC)
                bG, tG = bank[4], t_bank[4]
                k.mm(bG[:, 0:128], xcT[:, 4, cc], xcT[:, 5, cc], True, True, (t_xcT,), (tG,))
                for j in range(8):
                    bR, tR = bank[2 + j % 2], t_bank[2 + j % 2]
                    k.mm(bR[:, 0:128], esel[0:8, j, :], acsh[:, c, :], True, False, (t_c3, t_acsT), (tR,))
                    k.mm(bR[:, 0:128], esel[0:8, j, :], acsl[:, c, :], False, False, (t_c3, t_acsT), (tR,))
                    k.mm(bR[:, 0:128], ident[:, :], cm1[:, :], False, True, (t_const, t_c3), (tR,))
                    k.act(Lt[j % 2][:, :], bR[:, 0:128], AF.Exp, (tR, t_sc), (t_Lt[j % 2],), bias=nacs[:, c, 8 * g + j:8 * g + j + 1])
                    k.tt(Mt[cs_][:, j, :], bG[:, 0:128], Lt[j % 2][:, :], ALU.mult, (tG, t_Lt[j % 2]), (t_Mt[cs_],))
                bY, tY = bank[0], t_bank[0]
                for j in range(8):
                    js = slice(64 * j, 64 * (j + 1))
                    k.mm(bY[:, js], Mt[cs_][:, j, :], xdt[:, j, :], True, False, (t_Mt[cs_], t_xd), (tY,))
                    k.mm(bY[:, js], ident[:, :], xsD[:, j, :], False, True, (t_const, t_xd), (tY,))
                if c > 0:
                    bO, tO = bank[1], t_bank[1]
                    k.mm(bO[:, :], xcT[:, 5, cc], Hb[:, :], True, True, (t_xcT, t_H), (tO,))
                    k.tt(yv[:, :].rearrange("p (h q) -> p h q", q=64), bO[:, :].rearrange("p (h q) -> p h q", q=64), bc8(eac), ALU.mult,
                         (tO, t_ec), (t_yv,))
                    k.tt(yv[:, :], yv[:, :], bY[:, :], ALU.add, (t_yv, tY), (t_yv,))
                    k.tt(ygt[:, :], yv[:, :], sz[:, c, :], ALU.mult, (t_yv, t_sz), (t_yv,))
                else:
                    k.tt(ygt[:, :], bY[:, :], sz[:, c, :], ALU.mult, (tY, t_sz), (t_yv,))
                k.act(junk2[:, :], ygt[:, :], AF.Square, (t_yv,), (t_junk2, t_yv), accum=ssn[:, 0:1])
                k.ts(ssn[:, 1:2], ssn[:, 0:1], 1.0 / 512, EPS, ALU.mult, ALU.add, (t_yv,), (t_yv,))
                k.act(ssn[:, 1:2], ssn[:, 1:2], AF.Ln, (t_yv,), (t_yv,))
                k.act(ssn[:, 1:2], ssn[:, 1:2], AF.Exp, (t_yv,), (t_yv,), scale=-0.5)
                k.stt(ynb[cs_][:, :], ygt[:, :], ssn[:, 1:2], nwg[:, :], ALU.mult, ALU.mult, (t_yv, t_nwg), (t_ynb[cs_],))
                for j in range(4):
                    k.tr(pst[1][:, j, :], ynb[cs_][:, j * 128:(j + 1) * 128], ident[:, :], (t_ynb[cs_], t_const), (t_pst[1],))
                k.cp(ynst[cs_][:, :, :], pst[1][:, 0:4, :], (t_pst[1],), (t_ynst[cs_],), eng="act")
                k.dma(ynT_d[4 * g:4 * g + 4, :, cc].rearrange("a p t -> p a t"), ynst[cs_][:, :, :], ds_ynst[cs_], (t_ynst[cs_],), (t_ynTd,))
                bS, tS = bank[5], t_bank[5]
                k.mm(bS[:, :], tok[cs_][:, 4, :], xdd[:].rearrange("p h q -> p (h q)"), True, True, (t_tok[cs_], t_xd), (tS,))
                if c > 0:
                    k.tt(Hs[:, :].rearrange("p (h q) -> p h q", q=64), Hs[:, :].rearrange("p (h q) -> p h q", q=64), bc8(etc_), ALU.mult,
                         (t_H, t_ec), (t_H,), eng=POOLC)
                    k.tt(Hs[:, :], Hs[:, :], bS[:, :], ALU.add, (t_H, tS), (t_H,))
                else:
                    k.cp(Hs[:, :], bS[:, :], (tS,), (t_H,))
                if c < NT - 1:
                    k.cp(Hb[:, :], Hs[:, :], (t_H,), (t_H,), eng="act")
            if g >= int(dbg.get("l1p2", 8)):
                continue
            bT, tT = bank[4], t_bank[4]
            for j in range(4):
                k.tr(bT[:, j * 128:(j + 1) * 128], Hs[:, j * 128:(j + 1) * 128], ident_f[:, :], (t_H, t_c3), (tT,))
            k.cp(yv[:, :], bT[:, :], (tT,), (t_hout,), eng="act")
            k.dma(ssd_p[512 * g:512 * (g + 1), :].rearrange("(a p) n -> p a n", p=128), hout, ds_hout, (t_hout,), ())
        for k3 in range(3):
            k.dma(conv_p[k3:k3 + 1, :].rearrange("o (t p) -> p (o t)", p=128), convst[:, :, k3], ds_hout, (t_convst,), (), allow_slow_non_contiguous=True)

        k.barrier()
        k.ptr = l1s_base
        ds_ss = P.dsem("ss_ld")
        t_pl = Tok()
        D_pl, nw_pl = f32t("D_pl", [128, 32]), f32t("nw_pl", [128, 32])
        for h2 in range(2):
            hs = slice(h2 * 64, (h2 + 1) * 64)
            k.dma(D_pl[hs, :], ssd_d.rearrange("o (hp hh) -> hh o hp", hh=2)[h2].partition_broadcast(64), ds_ss, (), (t_pl,),
                  allow_slow_non_contiguous=True)
        k.dma(nw_pl[:, :], ssd_nw.rearrange("o (t p) -> p (o t)", p=128), ds_ss, (), (t_pl,), allow_slow_non_contiguous=True)
        k.dma(conv_s[:, 0:2, :], st_conv[:, 1:3, :], ds_ss, (), ())
        cst_ = [f32t("cst_%d" % i, [NS, 512]) for i in range(2)]
        t_cst = [Tok(), Tok()]
        ds_cst = [P.dsem("cst0"), P.dsem("cst1")]
        for r4 in range(12):
            bk, tbk = bank[r4 % 2], t_bank[r4 % 2]
            for j in range(4):
                k.tr(bk[0:NS, j * 128:(j + 1) * 128], xbcS[:, 4 * r4 + j, :], ident_f[:, :], (t_xS, t_c3), (tbk,))
            k.cp(cst_[r4 % 2][:, :], bk[0:NS, :], (tbk,), (t_cst[r4 % 2],), eng="act")
            k.dma(conv_s[:, 2, r4 * 512:(r4 + 1) * 512], cst_[r4 % 2][:, :], ds_cst[r4 % 2], (t_cst[r4 % 2],), ())
        a_s, dec_s = f32t("a_s", [NS, 64]), f32t("dec_s", [NS, 64])
        dexp = f32t("dexp", [NS, 32, NS])
        dt_pl, dec_pl, xdt_pl = f32t("dt_pl", [128, 32, NS]), f32t("dec_pl", [128, 32, NS]), f32t("xdt_pl", [128, 32, NS])
        k.tt(a_s[:, :], dtv[:NS, NT, :], A_bc[:NS, :], ALU.mult, (t_sc, t_c3), (t_pl,))
        k.act(dec_s[:, :], a_s[:, :], AF.Exp, (t_pl,), (t_pl,))
        id16 = ident_f[0:NS, 0:NS].unsqueeze(1).broadcast_to([NS, 32, NS])
        for src, dst, rt in ((dtv[:NS, NT, :], dt_pl, (t_sc,)), (dec_s[:, :], dec_pl, (t_pl,))):
            bk, tbk = bank[2], t_bank[2]
            for h2 in range(2):
                k.tt(dexp[:], src.rearrange("p (hp hh) -> p hp hh", hh=2)[:, :, h2:h2 + 1].broadcast_to([NS, 32, NS]), id16, ALU.mult,
                     rt + (t_c3,), (t_pl,))
                k.mm(bk[h2 * 64:(h2 + 1) * 64, :], onesf[0:NS, 0:64], dexp[:].rearrange("p a b -> p (a b)"), True, True, (t_c3, t_pl), (tbk,),
                     tp=(0, 64 * h2))
            k.cp(dst[:].rearrange("p a b -> p (a b)"), bk[:, :], (tbk,), (t_pl,))
        k.tt(xdt_pl[:], xcS[:, 0:32, :], dt_pl[:], ALU.mult, (t_xS, t_pl), (t_pl,))
        hst0 = f32t("hst0", [128, 32, 128])
        hst = [hst0, hst0]
        t_hst0 = Tok()
        t_hst = [t_hst0, t_hst0]
        ds_hst0 = P.dsem("hst0")
        ds_hst = [ds_hst0, ds_hst0]
        tmpS = f32t("tmpS", [128, 32, 128])
        t_tmpS = Tok()
        dB = f32t("dB", [128, 8, 128])
        t_dB = Tok()
        Bbc, Cbc = f32t("Bbc", [128, 8, 128]), f32t("Cbc", [128, 8, 128])
        t_bc = Tok()
        y_pl = f32t("y_pl", [128, 32, NS])
        t_ypl = Tok()
        idb = ident_f[:, :].unsqueeze(1).broadcast_to([128, 8, 128])
        for s_ in range(NS):
            hs_ = s_ % 2
            H = hst[hs_]
            k.dma(H[:, :, :], st_ssd[s_].rearrange("(hp q) n -> q hp n", q=128), ds_hst[hs_], (), (t_hst[hs_],))
            for which, dstbc in ((32, Bbc), (40, Cbc)):
                k.tt(dB[:], xcS[:, which:which + 8, s_:s_ + 1].broadcast_to([128, 8, 128]), idb, ALU.mult, (t_xS, t_c3), (t_dB,))
                for hf in range(2):
                    bk, tbk = bank[3 + hf], t_bank[3 + hf]
                    k.mm(bk[:, :], onesf[:, :], dB[:, 4 * hf:4 * hf + 4, :].rearrange("p a b -> p (a b)"), True, True, (t_c3, t_dB), (tbk,))
                    k.cp(dstbc[:, 4 * hf:4 * hf + 4, :].rearrange("p a b -> p (a b)"), bk[:, :], (tbk,), (t_bc,), eng="act")
            v4 = lambda t: t[:].rearrange("p (g a) n -> p g a n", a=4)
            k.tt(v4(tmpS), Bbc[:].unsqueeze(2).broadcast_to([128, 8, 4, 128]),
                 xdt_pl[:, :, s_:s_ + 1].rearrange("p (g a) o -> p g a o", a=4).broadcast_to([128, 8, 4, 128]), ALU.mult, (t_bc, t_pl), (t_tmpS,))
            k.tt(H[:], H[:], dec_pl[:, :, s_:s_ + 1].broadcast_to([128, 32, 128]), ALU.mult, (t_hst[hs_], t_pl), (t_hst[hs_],))
            k.tt(H[:], H[:], tmpS[:], ALU.add, (t_hst[hs_], t_tmpS), (t_hst[hs_],))
            k.dma(ssd_s[s_].rearrange("(hp q) n -> q hp n", q=128), H[:, :, :], ds_hst[hs_], (t_hst[hs_],), ())
            k.tt(v4(tmpS), v4(H), Cbc[:].unsqueeze(2).broadcast_to([128, 8, 4, 128]), ALU.mult, (t_hst[hs_], t_bc), (t_tmpS,))
            k.red(y_pl[:, :, s_], tmpS[:], ALU.add, (t_tmpS,), (t_ypl,))
        ygS, sqS = f32t("ygS", [128, 32, NS]), f32t("sqS", [128, 32, NS])
        ssS, rsS = f32t("ssS", [128, 8, NS]), f32t("rsS", [128, 8, NS])
        k.tt(ygS[:], xcS[:, 0:32, :], D_pl[:, :].unsqueeze(2).broadcast_to([128, 32, NS]), ALU.mult, (t_xS, t_pl), (t_ypl,))
        k.tt(ygS[:], ygS[:], y_pl[:], ALU.add, (t_ypl,), (t_ypl,))
        k.tt(ygS[:], ygS[:], szS[:], ALU.mult, (t_ypl, t_xS), (t_ypl,))
        k.tt(sqS[:], ygS[:], ygS[:], ALU.mult, (t_ypl,), (t_ypl,))
        bk, tbk = bank[0], t_bank[0]
        k.mm(bk[:, :], onesf[:, :], sqS[:].rearrange("p a b -> p (a b)"), True, True, (t_c3, t_ypl), (tbk,))
        k.red(ssS[:], bk[:, :].rearrange("p (g a s) -> p g s a", g=8, a=4), ALU.add, (tbk,), (t_ypl,))
        k.ts(ssS[:].rearrange("p a b -> p (a b)"), ssS[:].rearrange("p a b -> p (a b)"), 1.0 / 512, EPS, ALU.mult, ALU.add, (t_ypl,), (t_ypl,))
        k.act(rsS[:].rearrange("p a b -> p (a b)"), ssS[:].rearrange("p a b -> p (a b)"), AF.Ln, (t_ypl,), (t_ypl,))
        k.act(rsS[:].rearrange("p a b -> p (a b)"), rsS[:].rearrange("p a b -> p (a b)"), AF.Exp, (t_ypl,), (t_ypl,), scale=-0.5)
        k.tt(ygS[:].rearrange("p (g a) s -> p g a s", a=4), ygS[:].rearrange("p (g a) s -> p g a s", a=4),
             rsS[:].unsqueeze(2).broadcast_to([128, 8, 4, NS]), ALU.mult, (t_ypl,), (t_ypl,))
        k.tt(ynSd[:], ygS[:], nw_pl[:, :].unsqueeze(2).broadcast_to([128, 32, NS]), ALU.mult, (t_ypl, t_pl), (t_ynSd,))

    if dbg.get("l1", 1) and dbg.get("l0d", 1) and dbg.get("s5", 1):
        l1()
        outproj(w_out_odd, 32, ynT_d, ynSd, t_ynTd, t_ynSd, x1_d[0:T, :], x1_d[T:TS, :], (t_x1d,), y_p, y_s, Tok(), "od1")

    if "catT" in dbg:
        o = dout("dbg_catT", [16, 128, T], BF16)
        dsd2 = P.dsem("dbg2")
        k.dma(o, catT, dsd2, (t_catT,), ())
    if "hT" in dbg:
        o = dout("dbg_hT", [128, KC * TS], BF16)
        ds = P.dsem("dbg")
        k.dma(o, hT[:].rearrange("p a b -> p (a b)"), ds, (t_hT,), ())

    with ExitStack() as es:
        P.emit(es)
    return nc


def _core_inputs(c, a, cst, rope):
    s0, s1 = NS * c, NS * (c + 1)
    f = np.ascontiguousarray
    m = {
        "xp": f(a["x_prompt"][c]), "xs": f(a["x_sample"][s0:s1, 0]), "norm_w": f(a["norm_w"]),
        "w_in_even": f(a["w_in_even"][0]), "q_norm_w": f(a["q_norm_w"]), "k_norm_w": f(a["k_norm_w"]),
        "cst": cst, "rope": rope,
        "lam_re": f(a["s5_lambda_re"][0]), "lam_im": f(a["s5_lambda_im"][0]), "log_dt": f(a["s5_log_dt"]),
        "s5_b": f(a["s5_b"][0]).reshape(64, 64, 32), "s5_c": f(a["s5_c"][0]).reshape(64, 16, 128), "s5_d": f(a["s5_d"]),
        "glu_w": f(a["s5_glu_w"][0]), "glu_b": f(a["s5_glu_b"]), "st_s5": f(a["state_s5"][0, s0:s1]).reshape(NS, 8192),
        "w_out_even": f(a["w_out_even"][0]), "w_in_odd": f(a["w_in_odd"][0]), "conv_w": f(a["conv_w"][0]),
        "conv_b": f(a["conv_b"]), "dt_bias": f(a["ssd_dt_bias"]), "a_log": f(a["ssd_a_log"]), "ssd_d": f(a["ssd_d"]),
        "ssd_nw": f(a["ssd_norm_w"]), "w_out_odd": f(a["w_out_odd"][0]),
        "cache_k": a["cache_k"][0].reshape(2560 * 128, 1024), "cache_v": a["cache_v"][0].reshape(2560 * 128, 1024),
        "page_table": f(a["page_table"][s0:s1]).reshape(1, NS * 16).astype(np.int32),
        "st_conv": f(a["state_conv"][0, s0:s1]), "st_ssd": f(a["state_ssd"][0, s0:s1]).reshape(NS, 4096, 128),
    }
    return m


def kernel(**inputs):
    a = {k_: np.asarray(v) for k_, v in inputs.items()}
    nc = build()
    cst, rope = host_consts(), host_rope()
    in_maps = [_core_inputs(c, a, cst, rope) for c in range(NCORES)]
    names = set()
    for alloc in nc.allocations:
        try:
            if alloc.kind == "ExternalInput":
                names.add(alloc.memorylocations[0].name)
        except Exception:
            pass
    if names:
        in_maps = [{k_: v for k_, v in m.items() if k_ in names} for m in in_maps]
    res = run_bass_kernel_spmd(nc, in_maps, core_ids=list(range(NCORES))).results
    B = NCORES
    g = lambda name: [np.asarray(r[name]) for r in res]
    z = lambda *shape: np.zeros(shape, np.float32)
    y_prompt = np.stack(g("y_p"), 0)
    y_sample = np.concatenate(g("y_s"), 0).reshape(B * NS, 1, D)
    k_prompt = np.stack(g("k_p"), 0).reshape(1, B, T, 8, 128)
    v_prompt = np.stack(g("v_p"), 0).reshape(1, B, T, 8, 128)
    k_sample = np.concatenate(g("k_s"), 0).reshape(1, B * NS, 1, 8, 128)
    v_sample = np.concatenate(g("v_s"), 0).reshape(1, B * NS, 1, 8, 128)
    s5_prompt = np.stack(g("s5_p"), 0).reshape(1, B, 64, 64, 2)
    s5_sample = np.concatenate(g("s5_s"), 0).reshape(1, B * NS, 64, 64, 2)
    conv_prompt = np.stack(g("conv_p"), 0).reshape(1, B, 3, 6144)
    ssd_prompt = np.stack(g("ssd_p"), 0).reshape(1, B, 64, 64, 128)
    if "conv_s" in res[0]:
        conv_sample = np.concatenate(g("conv_s"), 0).reshape(1, B * NS, 3, 6144)
        ssd_sample = np.concatenate(g("ssd_s"), 0).reshape(1, B * NS, 64, 64, 128)
    else:
        conv_sample, ssd_sample = z(1, B * NS, 3, 6144), z(1, B * NS, 64, 64, 128)
    return (y_prompt, y_sample, k_prompt, v_prompt, k_sample, v_sample, s5_prompt, s5_sample,
            conv_prompt, conv_sample, ssd_prompt, ssd_sample)
```
